# Optimizing a Trainium2 kernel written in Bass

```python
import math
import jax, jax.numpy as jnp
from jax import lax
import numpy as np

D_MODEL = 1024
BATCH = 8
SEQ = 2048
DEPTH = 2
DEC_BATCH = 32
DEC_SEQ = 8
PAST_LEN = 16384
PAGE_SIZE = 128

F32 = jnp.float32
MIX = D_MODEL
GW = MIX // 4
HD = 64
H_A = GW // HD
DK_A = HD
DV_A = HD
H_B = GW // HD
H_C = GW // HD
DK_C = HD
DV_C = HD
H_D = GW // HD
CONV_W = 4
PATTERNS = ((128, 1), (512, 4), (2048, 16))
W_MAX = 2048
N_BUCKETS = 32
MAX_DIST = 2048
BAND_BLK = 128
CHUNK_A = 16
CHUNK_C = 64
LRU_C = 8.0
D_FF = -(-8 * D_MODEL // (3 * 256)) * 256
D_PLE = 256
EPS = 1e-6
IN_SIZES = (GW, GW, GW, GW, GW, GW, GW, GW, GW, GW, GW, H_C, H_C, GW, GW)
D_IN = sum(IN_SIZES)

kernel_name = 'hybrid_hgrn2_dilated_gdn_rglru_step'


def rmsnorm(x, g):
    xf = x.astype(F32)
    y = xf * lax.rsqrt(jnp.mean(xf * xf, axis=-1, keepdims=True) + EPS) * g.astype(F32)
    return y.astype(x.dtype)


def head_rmsnorm(o, g):
    return o * lax.rsqrt(jnp.mean(o * o, axis=-1, keepdims=True) + EPS) * g.astype(F32)


def l2norm(x):
    return x * lax.rsqrt(jnp.sum(x * x, axis=-1, keepdims=True) + EPS)


def split_cols(t):
    out, o = [], 0
    for s in IN_SIZES:
        out.append(t[..., o:o + s])
        o += s
    return out


def causal_conv(u, buf, w, b=None):
    T = u.shape[1]
    full = jnp.concatenate([buf, u], axis=1)
    y = full[:, 0:T] * w[0]
    for j in range(1, CONV_W):
        y = y + full[:, j:j + T] * w[j]
    if b is not None:
        y = y + b
    return y, full[:, T:]


def t5_bucket(dist):
    exact = N_BUCKETS // 2
    d = jnp.maximum(dist, 0)
    logb = exact + (jnp.log(jnp.maximum(d, 1).astype(F32) / exact) / math.log(MAX_DIST / exact)
                    * (N_BUCKETS - exact)).astype(jnp.int32)
    return jnp.where(d < exact, d, jnp.minimum(logb, N_BUCKETS - 1))


def hgrn2_chunked(q, log_f, k, v, s0):
    Bn, T = q.shape[:2]
    C = CHUNK_A
    N = -(-T // C)
    pad = N * C - T
    padt = lambda t: jnp.pad(t, ((0, 0), (0, pad), (0, 0), (0, 0)))
    ch = lambda t: padt(t).reshape(Bn, N, C, t.shape[2], t.shape[3])
    q, log_f, k, v = ch(q), ch(log_f), ch(k), ch(v)
    G = jnp.cumsum(log_f, axis=2)
    tri = jnp.tril(jnp.ones((C, C), bool))[:, :, None, None]
    diff = G[:, :, :, None] - G[:, :, None, :]
    decay = jnp.where(tri, jnp.exp(jnp.where(tri, diff, 0.0)), 0.0)
    attn = jnp.einsum('bnthk,bnshk,bntshk->bnhts', q, k, decay)
    o = jnp.einsum('bnhts,bnshv->bnthv', attn, v)
    G_last = G[:, :, -1]
    q_dec = q * jnp.exp(G)
    k_dec = k * jnp.exp(G_last[:, :, None] - G)

    def step(S, inp):
        qd, kd, vv, gl = inp
        o_inter = jnp.einsum('bthk,bhkv->bthv', qd, S)
        S = S * jnp.exp(gl)[..., None] + jnp.einsum('bthk,bthv->bhkv', kd, vv)
        return S, o_inter

    sw = lambda t: jnp.swapaxes(t, 0, 1)
    S, o_inter = lax.scan(step, s0, (sw(q_dec), sw(k_dec), sw(v), sw(G_last)))
    o = (o + sw(o_inter)).reshape(Bn, N * C, o.shape[3], o.shape[4])[:, :T]
    return o, S


def gated_delta_chunked(q, k, v, beta, log_a, s0):
    Bn, T, H, DK = q.shape
    DV = v.shape[-1]
    C = CHUNK_C
    N = -(-T // C)
    pad = N * C - T
    hm4 = lambda t: jnp.pad(t, ((0, 0), (0, pad), (0, 0), (0, 0))).reshape(
        Bn, N, C, H, t.shape[-1]).transpose(0, 1, 3, 2, 4)
    hm3 = lambda t: jnp.pad(t, ((0, 0), (0, pad), (0, 0))).reshape(Bn, N, C, H).transpose(0, 1, 3, 2)
    q, k, v = hm4(q), hm4(k), hm4(v)
    beta, log_a = hm3(beta), hm3(log_a)
    g = jnp.cumsum(log_a, axis=-1)
    tri = jnp.tril(jnp.ones((C, C), bool))
    stri = jnp.tril(jnp.ones((C, C), bool), -1)
    gd = g[..., :, None] - g[..., None, :]
    decay = jnp.where(tri, jnp.exp(jnp.where(tri, gd, 0.0)), 0.0)
    kb = k * beta[..., None]
    A = jnp.where(stri, jnp.einsum('bnhtk,bnhsk->bnhts', kb, k) * decay, 0.0)
    M = A + jnp.eye(C, dtype=F32)
    rhs = jnp.concatenate([v * beta[..., None], kb * jnp.exp(g)[..., None]], axis=-1)
    sol = lax.linalg.triangular_solve(M, rhs, left_side=True, lower=True, unit_diagonal=True)
    u, w = sol[..., :DV], sol[..., DV:]
    qk = jnp.einsum('bnhtk,bnhsk->bnhts', q, k) * decay
    q_dec = q * jnp.exp(g)[..., None]
    g_last = g[..., -1]
    k_dec = k * jnp.exp(g_last[..., None] - g)[..., None]

    def step(S, inp):
        u_n, w_n, qd, qk_n, kd, gl = inp
        v_new = u_n - jnp.einsum('bhtk,bhkv->bhtv', w_n, S)
        o = jnp.einsum('bhtk,bhkv->bhtv', qd, S) + jnp.einsum('bhts,bhsv->bhtv', qk_n, v_new)
        S = S * jnp.exp(gl)[..., None, None] + jnp.einsum('bhtk,bhtv->bhkv', kd, v_new)
        return S, o

    sw = lambda t: jnp.swapaxes(t, 0, 1)
    S, o = lax.scan(step, s0, (sw(u), sw(w), sw(q_dec), sw(qk), sw(k_dec), sw(g_last)))
    o = o.transpose(1, 0, 3, 2, 4).reshape(Bn, N * C, H, DV)[:, :T]
    return o, S


def rglru(xc, wa, ba, wx, bx, lam, h0):
    Bn, T, Wd = xc.shape
    xb = xc.reshape(Bn, T, H_D, Wd // H_D)
    r = jax.nn.sigmoid(jnp.einsum('bthi,hij->bthj', xb, wa).reshape(Bn, T, Wd) + ba)
    ig = jax.nn.sigmoid(jnp.einsum('bthi,hij->bthj', xb, wx).reshape(Bn, T, Wd) + bx)
    log_a = -LRU_C * r * jax.nn.softplus(-lam)
    a = jnp.exp(log_a)
    b = jnp.sqrt(-jnp.expm1(2.0 * log_a)) * ig * xc

    def comb(e1, e2):
        return e1[0] * e2[0], e2[0] * e1[1] + e2[1]

    a_cum, h = lax.associative_scan(comb, (a, b), axis=1)
    h = h + a_cum * h0[:, None]
    return h, h[:, -1]


def dilated_band(q, k, v, window, dil, rel_bias):
    Bn, S, H, E = q.shape
    L = S // dil
    nb = -(-L // BAND_BLK)
    Lp = nb * BAND_BLK

    def strided(t):
        t = t.reshape(Bn, L, dil, H, E).transpose(0, 2, 1, 3, 4)
        return jnp.pad(t, ((0, 0), (0, 0), (0, Lp - L), (0, 0), (0, 0)))

    def windows(t):
        t = jnp.pad(strided(t), ((0, 0), (0, 0), (BAND_BLK, 0), (0, 0), (0, 0)))
        t = t.reshape(Bn, dil, nb + 1, BAND_BLK, H, E)
        return jnp.concatenate([t[:, :, :-1], t[:, :, 1:]], axis=3)

    qs = strided(q).reshape(Bn, dil, nb, BAND_BLK, H, E)
    kw, vw = windows(k), windows(v)
    qi = jnp.arange(BAND_BLK)[:, None]
    kj = jnp.arange(2 * BAND_BLK)[None, :]
    delta = qi + BAND_BLK - kj
    kidx = jnp.arange(nb)[:, None, None] * BAND_BLK + kj[None] - BAND_BLK
    valid = (delta >= 0) & (delta <= window // dil) & (kidx >= 0)
    bias = rel_bias[t5_bucket(delta * dil)].astype(F32).transpose(2, 0, 1)
    logits = jnp.einsum('brnqhe,brnkhe->brnhqk', qs, kw) + bias
    logits = jnp.where(valid[None, None, :, None], logits, -jnp.inf)
    m = jnp.max(logits, axis=-1)
    p = jnp.exp(logits - m[..., None])
    l = jnp.sum(p, axis=-1)
    acc = jnp.einsum('brnhqk,brnkhe->brnqhe', p, vw)

    def unstride(t):
        t = t.reshape((Bn, dil, Lp) + t.shape[4:])[:, :, :L]
        return jnp.swapaxes(t, 1, 2).reshape((Bn, S) + t.shape[3:])

    return unstride(jnp.swapaxes(m, 3, 4)), unstride(jnp.swapaxes(l, 3, 4)), unstride(acc)


def dilated_gather(q, k_all, v_all, buf_len, window, dil, rel_bias):
    T = q.shape[1]
    steps = jnp.arange(window // dil + 1)
    idx = buf_len + jnp.arange(T)[:, None] - steps[None, :] * dil
    valid = idx >= 0
    idx = jnp.maximum(idx, 0)
    kg, vg = k_all[:, idx], v_all[:, idx]
    bias = rel_bias[t5_bucket(steps * dil)].astype(F32).T
    logits = jnp.einsum('bthe,btnhe->bthn', q, kg) + bias
    logits = jnp.where(valid[None, :, None, :], logits, -jnp.inf)
    m = jnp.max(logits, axis=-1)
    p = jnp.exp(logits - m[..., None])
    l = jnp.sum(p, axis=-1)
    acc = jnp.einsum('bthn,btnhe->bthe', p, vg)
    return m, l, acc


def merge_by_denominator(parts):
    ms = jnp.stack([pt[0] for pt in parts])
    ls = jnp.stack([pt[1] for pt in parts])
    accs = jnp.stack([pt[2] for pt in parts])
    wts = jnp.exp(ms - jnp.max(ms, axis=0))
    return jnp.sum(wts[..., None] * accs, axis=0) / jnp.sum(wts * ls, axis=0)[..., None]


def run_layer(h, pl, li, lb, st, prm):
    Bn, T, _ = h.shape
    dt = h.dtype
    heads = lambda t: t.reshape(Bn, T, -1, HD)
    hn = rmsnorm(h, prm['norm1_g'][li])
    (aq, af, ai, ag, bq, bk, bv, cq, ck, cv, cz, cb, ca, dx, dg) = split_cols(
        (hn @ prm['w_in'][li]).astype(F32))

    zf = heads(af)
    lbh = lb.reshape(H_A, DK_A)
    log_f = jnp.log(lbh + (1.0 - lbh) * jax.nn.sigmoid(zf))
    k_a = (1.0 - lbh) * jax.nn.sigmoid(-zf)
    oa, sa = hgrn2_chunked(heads(aq), log_f, k_a, heads(ai), st['a'])
    oa = (head_rmsnorm(oa, prm['a_norm_g'][li].reshape(H_A, DV_A))
          * jax.nn.silu(heads(ag))).reshape(Bn, T, GW)

    qb = heads(bq) * HD ** -0.5
    kb, vb = heads(bk), heads(bv)
    rel = prm['rel_bias']
    if st['b_k'] is None:
        ob = merge_by_denominator([dilated_band(qb, kb, vb, w, d, rel) for (w, d) in PATTERNS])
        keep = min(W_MAX, T)
        nbk, nbv = kb[:, T - keep:], vb[:, T - keep:]
    else:
        buf_len = st['b_k'].shape[1]
        k_all = jnp.concatenate([st['b_k'], kb], axis=1)
        v_all = jnp.concatenate([st['b_v'], vb], axis=1)
        ob = merge_by_denominator([dilated_gather(qb, k_all, v_all, buf_len, w, d, rel)
                                   for (w, d) in PATTERNS])
        nbk, nbv = kb, vb
    ob = rmsnorm(ob.reshape(Bn, T, GW), prm['b_norm_g'][li])

    c_in = jnp.concatenate([cq, ck, cv], axis=-1)
    c_mix, ncc = causal_conv(c_in, st['c_conv'], prm['c_conv_w'][li].astype(F32))
    c_mix = jax.nn.silu(c_mix)
    qc = l2norm(heads(c_mix[..., :GW])) * DK_C ** -0.5
    kc = l2norm(heads(c_mix[..., GW:2 * GW]))
    vc = heads(c_mix[..., 2 * GW:])
    beta = jax.nn.sigmoid(cb)
    log_alpha = -jnp.exp(prm['c_a_log'][li].astype(F32)) * jax.nn.softplus(
        ca + prm['c_dt_bias'][li].astype(F32))
    oc, sc = gated_delta_chunked(qc, kc, vc, beta, log_alpha, st['c'])
    oc = (head_rmsnorm(oc, prm['c_norm_g'][li]) * jax.nn.silu(heads(cz))).reshape(Bn, T, GW)

    d_mix, ndc = causal_conv(dx, st['d_conv'], prm['d_conv_w'][li].astype(F32),
                             prm['d_conv_b'][li].astype(F32))
    hd, ndh = rglru(d_mix, prm['d_wa'][li].astype(F32), prm['d_ba'][li].astype(F32),
                    prm['d_wx'][li].astype(F32), prm['d_bx'][li].astype(F32),
                    prm['d_lambda'][li].astype(F32), st['d_h'])
    od = rmsnorm(hd * jax.nn.gelu(dg), prm['d_norm_g'][li])

    mix = jnp.concatenate([oa, ob, oc, od], axis=-1).astype(dt) @ prm['w_out'][li]
    h = h + mix

    gu = rmsnorm(h, prm['norm2_g'][li]) @ prm['w_ffn_in'][li]
    h = h + (jax.nn.silu(gu[..., :D_FF]) * gu[..., D_FF:]) @ prm['w_ffn_out'][li]

    gate = jax.nn.sigmoid(rmsnorm(h, prm['ple_norm_g'][li]) @ prm['w_ple_gate'][li])
    h = h + (pl @ prm['w_ple'][li]) * gate

    new = dict(b_k=nbk.astype(dt), b_v=nbv.astype(dt), a=sa.astype(dt), c=sc.astype(dt),
               c_conv=ncc.astype(dt), d_h=ndh.astype(dt), d_conv=ndc.astype(dt))
    return h, new


STATE_NAMES = ('b_k', 'b_v', 'a', 'c', 'c_conv', 'd_h', 'd_conv')


def run_trunk(x, p, cache, prm):
    Bn = x.shape[0]
    lbs = jax.nn.softmax(prm['a_lb'].astype(F32), axis=0)
    lbs = jnp.cumsum(lbs, axis=0) - lbs[0]
    h = x
    outs = {n: [] for n in STATE_NAMES}
    for li in range(DEPTH):
        if cache is None:
            st = dict(b_k=None, b_v=None,
                      a=jnp.zeros((Bn, H_A, DK_A, DV_A), F32),
                      c=jnp.zeros((Bn, H_C, DK_C, DV_C), F32),
                      c_conv=jnp.zeros((Bn, CONV_W - 1, 3 * GW), F32),
                      d_h=jnp.zeros((Bn, GW), F32),
                      d_conv=jnp.zeros((Bn, CONV_W - 1, GW), F32))
        else:
            st = {n: cache[n][li].astype(F32) for n in STATE_NAMES}
        h, new = run_layer(h, p[li], li, lbs[li], st, prm)
        for n in STATE_NAMES:
            outs[n].append(new[n])
    y = rmsnorm(h, prm['final_norm_g'])
    return y, {n: jnp.stack(outs[n]) for n in STATE_NAMES}


def setup_inputs(seed: int = 0) -> dict:
    key = jax.random.key(seed)
    ks = iter(jax.random.split(key, 48))
    nrm = lambda shape, scale=1.0: jax.random.normal(next(ks), shape, F32) * scale
    gain = lambda shape: 1.0 + 0.01 * jax.random.normal(next(ks), shape, F32)
    buf = min(W_MAX, PAST_LEN)
    inp = {}
    inp['x_prompt'] = nrm((BATCH, SEQ, D_MODEL))
    inp['x_sample'] = nrm((DEC_BATCH, DEC_SEQ, D_MODEL))
    inp['cache_b_k'] = nrm((DEPTH, DEC_BATCH, buf, H_B, HD))
    inp['cache_b_v'] = nrm((DEPTH, DEC_BATCH, buf, H_B, HD))
    inp['state_a'] = nrm((DEPTH, DEC_BATCH, H_A, DK_A, DV_A), 0.5)
    inp['state_c'] = nrm((DEPTH, DEC_BATCH, H_C, DK_C, DV_C), 0.3)
    inp['state_c_conv'] = nrm((DEPTH, DEC_BATCH, CONV_W - 1, 3 * GW))
    inp['state_d_h'] = nrm((DEPTH, DEC_BATCH, GW), 0.5)
    inp['state_d_conv'] = nrm((DEPTH, DEC_BATCH, CONV_W - 1, GW))
    inp['p_prompt'] = nrm((DEPTH, BATCH, SEQ, D_PLE))
    inp['p_sample'] = nrm((DEPTH, DEC_BATCH, DEC_SEQ, D_PLE))
    inp['rel_bias'] = nrm((N_BUCKETS, H_B), 0.5)
    inp['norm1_g'] = gain((DEPTH, D_MODEL))
    inp['w_in'] = nrm((DEPTH, D_MODEL, D_IN), D_MODEL ** -0.5)
    inp['a_lb'] = nrm((DEPTH, H_A * DK_A))
    inp['a_norm_g'] = gain((DEPTH, H_A * DV_A))
    inp['b_norm_g'] = gain((DEPTH, GW))
    inp['c_conv_w'] = nrm((DEPTH, CONV_W, 3 * GW), CONV_W ** -0.5)
    inp['c_a_log'] = jnp.log(jax.random.uniform(next(ks), (DEPTH, H_C), F32, 1.0, 16.0))
    dtv = jnp.exp(jax.random.uniform(next(ks), (DEPTH, H_C), F32, math.log(1e-3), math.log(1e-1)))
    inp['c_dt_bias'] = dtv + jnp.log(-jnp.expm1(-dtv))
    inp['c_norm_g'] = gain((DEPTH, DV_C))
    inp['d_conv_w'] = nrm((DEPTH, CONV_W, GW), CONV_W ** -0.5)
    inp['d_conv_b'] = nrm((DEPTH, GW), 0.02)
    inp['d_wa'] = nrm((DEPTH, H_D, HD, HD), HD ** -0.5)
    inp['d_ba'] = nrm((DEPTH, GW), 0.02)
    inp['d_wx'] = nrm((DEPTH, H_D, HD, HD), HD ** -0.5)
    inp['d_bx'] = nrm((DEPTH, GW), 0.02)
    a0 = jax.random.uniform(next(ks), (DEPTH, GW), F32, 0.9, 0.999)
    s = a0 ** (1.0 / LRU_C)
    inp['d_lambda'] = jnp.log(s) - jnp.log1p(-s)
    inp['d_norm_g'] = gain((DEPTH, GW))
    inp['w_out'] = nrm((DEPTH, MIX, D_MODEL), MIX ** -0.5)
    inp['norm2_g'] = gain((DEPTH, D_MODEL))
    inp['w_ffn_in'] = nrm((DEPTH, D_MODEL, 2 * D_FF), D_MODEL ** -0.5)
    inp['w_ffn_out'] = nrm((DEPTH, D_FF, D_MODEL), D_FF ** -0.5)
    inp['w_ple'] = nrm((DEPTH, D_PLE, D_MODEL), D_PLE ** -0.5)
    inp['ple_norm_g'] = gain((DEPTH, D_MODEL))
    inp['w_ple_gate'] = nrm((DEPTH, D_MODEL, D_MODEL), D_MODEL ** -0.5)
    inp['final_norm_g'] = gain((D_MODEL,))
    return inp


def reference(x_prompt, x_sample, cache_b_k, cache_b_v, state_a, state_c, state_c_conv,
              state_d_h, state_d_conv, p_prompt, p_sample, rel_bias, norm1_g, w_in, a_lb,
              a_norm_g, b_norm_g, c_conv_w, c_a_log, c_dt_bias, c_norm_g, d_conv_w, d_conv_b,
              d_wa, d_ba, d_wx, d_bx, d_lambda, d_norm_g, w_out, norm2_g, w_ffn_in, w_ffn_out,
              w_ple, ple_norm_g, w_ple_gate, final_norm_g):
    prm = dict(rel_bias=rel_bias, norm1_g=norm1_g, w_in=w_in, a_lb=a_lb, a_norm_g=a_norm_g,
               b_norm_g=b_norm_g, c_conv_w=c_conv_w, c_a_log=c_a_log, c_dt_bias=c_dt_bias,
               c_norm_g=c_norm_g, d_conv_w=d_conv_w, d_conv_b=d_conv_b, d_wa=d_wa, d_ba=d_ba,
               d_wx=d_wx, d_bx=d_bx, d_lambda=d_lambda, d_norm_g=d_norm_g, w_out=w_out,
               norm2_g=norm2_g, w_ffn_in=w_ffn_in, w_ffn_out=w_ffn_out, w_ple=w_ple,
               ple_norm_g=ple_norm_g, w_ple_gate=w_ple_gate, final_norm_g=final_norm_g)
    cache = dict(b_k=cache_b_k, b_v=cache_b_v, a=state_a, c=state_c, c_conv=state_c_conv,
                 d_h=state_d_h, d_conv=state_d_conv)
    y_prompt, sp = run_trunk(x_prompt, p_prompt, None, prm)
    y_sample, ss = run_trunk(x_sample, p_sample, cache, prm)
    return (y_prompt, y_sample,
            sp['b_k'], sp['b_v'], sp['a'], sp['c'], sp['c_conv'], sp['d_h'], sp['d_conv'],
            ss['b_k'], ss['b_v'], ss['a'], ss['c'], ss['c_conv'], ss['d_h'], ss['d_conv'])
```

```python
import numpy as np
from contextlib import ExitStack
import concourse.bass as bass
import concourse.mybir as mybir
from concourse.bass_utils import run_bass_kernel_spmd

F32 = mybir.dt.float32
BF16 = mybir.dt.bfloat16
I32 = mybir.dt.int32
AF = mybir.ActivationFunctionType
ALU = mybir.AluOpType
AX = mybir.AxisListType


def _hkey(h):
    if isinstance(h, (tuple, str, int)):
        return h
    n = getattr(h, "name", None)
    if n is not None:
        return ("t", n)
    return ("id", id(h))


class Prog:
    ENG = ("pe", "act", "dve", "pool", "sp")

    def __init__(self, nc):
        self.nc = nc
        self.es = ExitStack()
        self.eng = {"pe": nc.tensor, "act": nc.scalar, "dve": nc.vector,
                    "pool": nc.gpsimd, "sp": nc.sync}
        self.sem = {e: self.es.enter_context(nc.semaphore("s_" + e)) for e in ("pe", "act", "dve", "pool")}
        self.cnt = {e: 0 for e in self.sem}
        self.clock = {e: {} for e in self.ENG}
        self.last_w = {}
        self.readers = {}
        self.tok_clock = {}
        self.dsem = {}
        self.dcnt = {}
        self.out_keys = set()
        self.nwait = 0
        self.pe_pending = False
        self.psum_keys = set()
        self.nops = 0

    def sbuf(self, name, shape, dtype):
        return self.es.enter_context(self.nc.sbuf_tensor("sb_" + name, list(shape), dtype))

    def psum(self, name, shape, dtype):
        t = self.es.enter_context(self.nc.psum_tensor("ps_" + name, list(shape), dtype))
        self.psum_keys.add(_hkey(t))
        return t

    def _covered(self, clk, tok):
        return clk.get(tok[0], 0) >= tok[1]

    def _merge(self, clk, other):
        for k, v in other.items():
            if clk.get(k, 0) < v:
                clk[k] = v

    def _deps(self, reads, writes, ename=None):
        deps = []
        for h in list(reads) + list(writes):
            t = self.last_w.get(_hkey(h))
            if t is not None:
                deps.append(t)
        for h in reads:
            k = _hkey(h)
            if k in self.psum_keys:
                deps.extend(t for t in self.readers.get(k, ()) if t[0] != ename)
        for h in writes:
            deps.extend(self.readers.get(_hkey(h), ()))
        return deps

    def _wait(self, ename, deps):
        e = self.eng[ename]
        clk = self.clock[ename]
        pend = []
        best = {}
        for t in deps:
            if ename == "pe" and t[0] == "pe":
                continue
            if self._covered(clk, t):
                continue
            if best.get(t[0], 0) < t[1]:
                best[t[0]] = t[1]
        for k, v in best.items():
            if clk.get(k, 0) >= v:
                continue
            s = self.sem[k] if k in self.sem else self.dsem[k]
            if k == "pe" and v > self.cnt["pe"]:
                raise RuntimeError("wait on un-signalled PE op (mark the producer sig=True)")
            pend.append((s, v))
            self.nwait += 1
            self._merge(clk, self.tok_clock[(k, v)])
        for s, v in pend[:-1]:
            e.wait_ge(s, v)
            if len(pend) > 2:
                e.nop()
        return pend[-1] if pend else None

    def _commit(self, tok, ename, reads, writes):
        c = dict(self.clock[ename])
        c[tok[0]] = max(c.get(tok[0], 0), tok[1])
        self.tok_clock[tok] = c
        for h in writes:
            k = _hkey(h)
            self.last_w[k] = tok
            self.readers[k] = []
        for h in reads:
            self.readers.setdefault(_hkey(h), []).append(tok)

    def op(self, ename, fn, reads=(), writes=(), sig=True):
        lastw = self._wait(ename, self._deps(reads, writes, ename))
        ins = fn(self.eng[ename])
        if lastw is not None:
            ins._wait_ge(lastw[0], lastw[1])
        self.nops += 1
        if ename != "pe":
            sig = True
        if sig:
            self.cnt[ename] += 1
            ins.then_inc(self.sem[ename], 1)
            tok = (ename, self.cnt[ename])
            if ename == "pe":
                self.pe_pending = False
        else:
            tok = (ename, self.cnt[ename] + 1)
            self.pe_pending = True
        self._commit(tok, ename, reads, writes)
        return ins

    def dma(self, qname, out_ap, in_ap, reads=(), writes=(), key=None, out=False, **kw):
        if key is None:
            hs = list(writes) if writes else list(reads)
            key = ("dma",) + tuple(_hkey(h) for h in hs[:1])
        key = ("d", key)
        if key not in self.dsem:
            self.dsem[key] = self.es.enter_context(self.nc.semaphore("d%d" % len(self.dsem)))
            self.dcnt[key] = 0
        lastw = self._wait(qname, self._deps(reads, writes, qname))
        ins = self.eng[qname].dma_start(out=out_ap, in_=in_ap, **kw)
        if lastw is not None:
            ins._wait_ge(lastw[0], lastw[1])
        self.dcnt[key] += 16
        ins.then_inc(self.dsem[key], 16)
        tok = (key, self.dcnt[key])
        self._commit(tok, qname, reads, writes)
        if out:
            self.out_keys.add(key)
        return ins

    def finish(self):
        sp = self.eng["sp"]
        for key in self.out_keys:
            sp.wait_ge(self.dsem[key], self.dcnt[key])
        for e in ("pe", "act", "dve", "pool"):
            if self.cnt[e]:
                sp.wait_ge(self.sem[e], self.cnt[e])
        self.es.close()


D = 1024
DIN = 3336
DFF = 2816
NFF = 22
NTOK = 2048
NBLK = 512
NS = 4
LS = 8
EPS = 1e-6
COLS = dict(aq=0, af=256, ai=512, ag=768, bq=1024, bk=1280, bv=1536, cq=1792, ck=2048,
            cv=2304, cz=2560, cb=2816, ca=2820, dx=2824, dg=3080)
TABL = 2304
ZW = 2560

VEC = {}
_o = 0
def _v(name, n):
    global _o
    VEC[name] = (_o, n)
    _o += n
for _l in range(2):
    for _n, _c in (("n1g", 8), ("n2g", 8), ("png", 8), ("ang", 2), ("bng", 2), ("cng", 1),
                   ("ccw", 24), ("dcw", 8), ("dcb", 2), ("dba", 2), ("dbx", 2), ("dlam", 2),
                   ("dng", 2), ("calog", 4), ("cdtb", 4)):
        _v("%s%d" % (_n, _l), _c)
_v("fng", 8)
_v("alb0", 2)
_v("alb1", 2)
NV = _o

CST = {}
_o = 0
def _c(name, n):
    global _o
    CST[name] = (_o, n)
    _o += n
_c("ident", 128)
_c("onesbd", 128)
_c("mask16T", 128)
_c("blk16", 8)
_c("reset16", 512)
_c("resetS", 32)
_c("uloc", 64)
_c("lsloc", 64)
_c("trilS", 64)
_c("triuI", 64)
_c("id64", 64)
_c("u64", 128)
_c("l64s", 128)
_c("sel64", 128)
_c("selc", 2)
_c("sel8", 8)
NC_ = _o


def make_consts():
    c = np.zeros((128, NC_), np.float32)
    p = np.arange(128)[:, None]
    def put(name, arr):
        o, n = CST[name]
        c[:, o:o + n] = arr
    t = np.arange(128)[None, :]
    put("ident", (p == t))
    put("onesbd", (p // 64 == t // 64))
    put("mask16T", (p // 16 == t // 16) & (p <= t))
    put("blk16", (p // 16 == np.arange(8)[None, :]))
    put("reset16", np.broadcast_to((np.arange(512)[None, :] % 16 != 0), (128, 512)))
    put("resetS", np.broadcast_to((np.arange(32)[None, :] % 8 != 0), (128, 32)))
    pl = p % 64
    s = np.arange(64)[None, :]
    put("uloc", pl <= s)
    put("lsloc", pl > s)
    put("trilS", pl > s)
    put("triuI", pl <= s)
    put("id64", pl == s)
    put("u64", (p // 64 == t // 64) & (p <= t))
    put("l64s", (p // 64 == t // 64) & (p > t))
    put("sel64", p == (t // 64) * 64 + 63)
    put("selc", p == np.arange(2)[None, :] * 64 + 63)
    put("sel8", np.broadcast_to(p == 7, (128, 8)))
    return c


def make_disttab():
    import math
    M = np.zeros((32, TABL), np.float32)
    for u in range(TABL):
        d = u - 127
        if d < 0 or d > 2048:
            continue
        mult = 0
        if d <= 128:
            mult += 1
        if d <= 512 and d % 4 == 0:
            mult += 1
        if d <= 2048 and d % 16 == 0:
            mult += 1
        if mult == 0:
            continue
        if d < 16:
            b = d
        else:
            v = np.float32(np.log(np.float32(max(d, 1)) / np.float32(16.0))) / np.float32(math.log(2048 / 16)) * np.float32(16)
            b = min(16 + int(np.float32(v)), 31)
        M[b, u] = mult
    return M


def fm(v):
    v = np.asarray(v, np.float32)
    return np.ascontiguousarray(v.reshape(-1, 128).T)


def host_shared(inp):
    sh = {}
    f32 = lambda a: np.ascontiguousarray(np.asarray(a, np.float32))
    w_in = f32(inp["w_in"])
    sh["w_in_t"] = f32(w_in.reshape(2, 8, 128, DIN).transpose(0, 2, 1, 3))
    sh["w_out_t"] = f32(f32(inp["w_out"]).reshape(2, 8, 128, D).transpose(0, 2, 1, 3))
    wfi = f32(inp["w_ffn_in"]).reshape(2, 8, 128, 2, NFF, 128)
    sh["w_ffi_t"] = f32(wfi.transpose(0, 4, 2, 1, 3, 5).reshape(2, NFF, 128, 8, 256))
    sh["w_ffo_t"] = f32(f32(inp["w_ffn_out"]).reshape(2, NFF, 128, D).transpose(0, 2, 1, 3))
    sh["w_ple_t"] = f32(f32(inp["w_ple"]).reshape(2, 2, 128, D).transpose(0, 2, 1, 3))
    sh["w_gate_t"] = f32(f32(inp["w_ple_gate"]).reshape(2, 8, 128, D).transpose(0, 2, 1, 3))
    vecs = np.zeros((128, NV), np.float32)
    def put(name, arr):
        o, n = VEC[name]
        vecs[:, o:o + n] = arr
    for l in range(2):
        put("n1g%d" % l, fm(inp["norm1_g"][l]))
        put("n2g%d" % l, fm(inp["norm2_g"][l]))
        put("png%d" % l, fm(inp["ple_norm_g"][l]))
        put("ang%d" % l, fm(inp["a_norm_g"][l]))
        put("bng%d" % l, fm(inp["b_norm_g"][l]))
        put("cng%d" % l, np.tile(np.asarray(inp["c_norm_g"][l], np.float32), 2)[:, None])
        ccw = np.asarray(inp["c_conv_w"][l], np.float32)
        put("ccw%d" % l, ccw.reshape(4, 6, 128).transpose(2, 1, 0).reshape(128, 24))
        dcw = np.asarray(inp["d_conv_w"][l], np.float32)
        put("dcw%d" % l, dcw.reshape(4, 2, 128).transpose(2, 1, 0).reshape(128, 8))
        put("dcb%d" % l, fm(inp["d_conv_b"][l]))
        put("dba%d" % l, fm(inp["d_ba"][l]))
        put("dbx%d" % l, fm(inp["d_bx"][l]))
        put("dlam%d" % l, fm(inp["d_lambda"][l]))
        put("dng%d" % l, fm(inp["d_norm_g"][l]))
        put("calog%d" % l, np.broadcast_to(np.asarray(inp["c_a_log"][l], np.float32)[None, :], (128, 4)))
        put("cdtb%d" % l, np.broadcast_to(np.asarray(inp["c_dt_bias"][l], np.float32)[None, :], (128, 4)))
    put("fng", fm(inp["final_norm_g"]))
    put("alb0", fm(inp["a_lb"][0]))
    put("alb1", fm(inp["a_lb"][1]))
    sh["vecs"] = vecs
    dgw = np.zeros((128, 2, 2, 2, 128), np.float32)
    for l in range(2):
        for wi, nm in enumerate(("d_wa", "d_wx")):
            w = np.asarray(inp[nm][l], np.float32)
            for h in range(4):
                r = (h % 2) * 64
                dgw[r:r + 64, l, wi, h // 2, r:r + 64] = w[h]
    sh["dgw"] = dgw.reshape(128, 2 * 2 * 2 * 128)
    sh["relb"] = f32(inp["rel_bias"])
    sh["cst"] = make_consts()
    sh["cM"] = make_disttab()
    return sh


def host_core(inp, c):
    f32 = lambda a: np.ascontiguousarray(np.asarray(a, np.float32))
    m = {}
    x = f32(inp["x_prompt"][c])
    m["xT"] = f32(x.reshape(NTOK, 8, 128).transpose(2, 1, 0))
    p = f32(inp["p_prompt"][:, c])
    m["pT"] = f32(p.reshape(2, NTOK, 2, 128).transpose(0, 3, 2, 1))
    sl = slice(NS * c, NS * c + NS)
    xs = f32(inp["x_sample"][sl]).reshape(NS * LS, 8, 128)
    m["xsT"] = f32(xs.transpose(2, 1, 0))
    ps = f32(inp["p_sample"][:, sl]).reshape(2, NS * LS, 2, 128)
    m["psT"] = f32(ps.transpose(0, 3, 2, 1))
    ck = f32(inp["cache_b_k"][:, sl])
    ck = ck.reshape(2, NS, 2048, 2, 2, 64)
    m["kcT"] = f32(ck.transpose(0, 1, 4, 5, 3, 2).reshape(2, NS, 128, 2, 2048))
    cv = f32(inp["cache_b_v"][:, sl]).reshape(2, NS, 16, 128, 256)
    m["vc"] = f32(cv.transpose(0, 1, 3, 2, 4))
    def st(a):
        a = f32(a[:, sl]).reshape(2, NS, 2, 2, 64, 64)
        return f32(a.transpose(0, 1, 3, 4, 2, 5).reshape(2, NS, 128, 2, 64))
    m["sa"] = st(inp["state_a"])
    m["sc"] = st(inp["state_c"])
    scc = f32(inp["state_c_conv"][:, sl]).reshape(2, NS, 3, 6, 128)
    m["scc"] = f32(scc.transpose(0, 4, 3, 1, 2))
    sdc = f32(inp["state_d_conv"][:, sl]).reshape(2, NS, 3, 2, 128)
    m["sdc"] = f32(sdc.transpose(0, 4, 3, 1, 2))
    sdh = f32(inp["state_d_h"][:, sl]).reshape(2, NS, 2, 128)
    m["sdh"] = f32(sdh.transpose(0, 3, 2, 1))
    return m


OUT_SPECS = {
    "yT": [128, 8, NTOK], "ysT": [128, 8, NS * LS],
    "pbk": [2, NTOK, 256], "pbv": [2, NTOK, 256],
    "pa": [2, 128, 2, 64], "pc": [2, 128, 2, 64],
    "pcc": [2, 128, 6, 3], "pdh": [2, 128, 2], "pdc": [2, 128, 2, 3],
    "sbk": [2, NS * LS, 256], "sbv": [2, NS * LS, 256],
    "sao": [2, NS, 128, 2, 64], "sco": [2, NS, 128, 2, 64],
    "scco": [2, 128, 6, NS, 3], "sdho": [2, 128, 2, NS], "sdco": [2, 128, 2, NS, 3],
}


def host_gather(res):
    nco = len(res)
    def unst(a):
        a = a.reshape(2, 2, 64, 2, 64)
        return a.transpose(0, 3, 1, 2, 4).reshape(2, 4, 64, 64)
    y = np.stack([r["yT"].transpose(2, 1, 0).reshape(NTOK, D) for r in res])
    ys = np.concatenate([r["ysT"].transpose(2, 1, 0).reshape(NS, LS, D) for r in res])
    pbk = np.stack([r["pbk"].reshape(2, NTOK, 4, 64) for r in res], 1)
    pbv = np.stack([r["pbv"].reshape(2, NTOK, 4, 64) for r in res], 1)
    pa = np.stack([unst(r["pa"]) for r in res], 1)
    pc = np.stack([unst(r["pc"]) for r in res], 1)
    pcc = np.stack([r["pcc"].transpose(0, 3, 2, 1).reshape(2, 3, 768) for r in res], 1)
    pdh = np.stack([r["pdh"].transpose(0, 2, 1).reshape(2, 256) for r in res], 1)
    pdc = np.stack([r["pdc"].transpose(0, 3, 2, 1).reshape(2, 3, 256) for r in res], 1)
    sbk = np.concatenate([r["sbk"].reshape(2, NS, LS, 4, 64) for r in res], 1)
    sbv = np.concatenate([r["sbv"].reshape(2, NS, LS, 4, 64) for r in res], 1)
    sao = np.concatenate([np.stack([unst(r["sao"][:, s]) for s in range(NS)], 1) for r in res], 1)
    sco = np.concatenate([np.stack([unst(r["sco"][:, s]) for s in range(NS)], 1) for r in res], 1)
    scco = np.concatenate([r["scco"].transpose(0, 3, 4, 2, 1).reshape(2, NS, 3, 768) for r in res], 1)
    sdho = np.concatenate([r["sdho"].transpose(0, 3, 2, 1).reshape(2, NS, 256) for r in res], 1)
    sdco = np.concatenate([r["sdco"].transpose(0, 3, 4, 2, 1).reshape(2, NS, 3, 256) for r in res], 1)
    outs = (y, ys, pbk, pbv, pa, pc, pcc, pdh, pdc, sbk, sbv, sao, sco, scco, sdho, sdco)
    return tuple(np.ascontiguousarray(o, dtype=np.float32) for o in outs)


class Blk:
    def __init__(self, kind, idx):
        self.kind = kind
        self.idx = idx
        if kind == "P":
            self.N, self.nseq, self.L = NBLK, 1, NBLK
            self.tok0 = idx * NBLK
            self.tiles = [(i * 128, 128, 0, idx * 4 + i) for i in range(4)]
        else:
            self.N, self.nseq, self.L = NS * LS, NS, LS
            self.tok0 = 0
            self.tiles = [(s * LS, LS, s, 0) for s in range(NS)]
        self.first = (kind == "S") or idx == 0
        self.last = (kind == "S") or idx == NTOK // NBLK - 1


class Builder:
    def __init__(self, nc, opts=None):
        self.nc = nc
        self.o = dict(mixers="ABCD", dense=True, blocks=None, layers=2, dump=())
        if opts:
            self.o.update(opts)
        self.P = Prog(nc)
        self.dumps = {}
        self.decl()
        self.alloc()

    def decl(self):
        nc = self.nc
        I = lambda n, s: nc.dram_tensor(n, list(s), F32, kind="ExternalInput")
        self.d = {}
        for n, s in (("xT", [128, 8, NTOK]), ("pT", [2, 128, 2, NTOK]), ("xsT", [128, 8, NS * LS]),
                     ("psT", [2, 128, 2, NS * LS]), ("kcT", [2, NS, 128, 2, 2048]),
                     ("vc", [2, NS, 128, 16, 256]), ("sa", [2, NS, 128, 2, 64]), ("sc", [2, NS, 128, 2, 64]),
                     ("scc", [2, 128, 6, NS, 3]), ("sdc", [2, 128, 2, NS, 3]), ("sdh", [2, 128, 2, NS]),
                     ("w_in_t", [2, 128, 8, DIN]), ("w_out_t", [2, 128, 8, D]),
                     ("w_ffi_t", [2, NFF, 128, 8, 256]), ("w_ffo_t", [2, 128, NFF, D]),
                     ("w_ple_t", [2, 128, 2, D]), ("w_gate_t", [2, 128, 8, D]),
                     ("vecs", [128, NV]), ("dgw", [128, 1024]), ("relb", [32, 4]),
                     ("cst", [128, NC_]), ("cM", [32, TABL])):
            self.d[n] = I(n, s)
        self.od = {n: nc.dram_tensor(n, list(s), F32, kind="ExternalOutput") for n, s in OUT_SPECS.items()}
        self.zt = nc.dram_tensor("ztab", [4, 128, ZW], BF16, kind="Internal")
        self.wcache = nc.dram_tensor("wcache", [128, 225408], BF16, kind="Internal")
        self.wc_map = {}
        self.wc_off = 0

    def dump(self, name, ap, shape, reads):
        if name not in self.o["dump"] or name in self.dumps:
            return
        t = self.nc.dram_tensor("dbg_" + name, list(shape), ap.dtype, kind="ExternalOutput")
        self.dumps[name] = t
        self.P.dma("act", t.ap(), ap, reads=reads, key="dbg_" + name, out=True)

    def alloc(self):
        P = self.P
        self.cst = P.sbuf("cst", [128, NC_], F32)
        self.vecs = P.sbuf("vecs", [128, NV], F32)
        self.dgw = P.sbuf("dgw", [128, 512], F32)
        self.ones_bf = P.sbuf("ones_bf", [128, 128], BF16)
        self.drv = P.sbuf("drv", [128, 32], F32)
        self.E = P.sbuf("E", [128, 4, TABL], BF16)
        self.qT = P.sbuf("qT", [128, 4, NBLK], BF16)
        self.PT = [P.sbuf("PT%d" % i, [128, 512], BF16) for i in range(2)]
        self.Vnew = P.sbuf("Vnew", [128, NS, 256], BF16)
        self.hT = P.sbuf("hT", [128, 8, NBLK], F32)
        self.hnT = P.sbuf("hnT", [128, 8, NBLK], BF16)
        self.mixT = P.sbuf("mixT", [128, 8, NBLK], BF16)
        self.actT = P.sbuf("actT", [128, NFF // 2, NBLK], BF16)
        self.KT = [P.sbuf("KT%d" % l, [128, 2, NTOK + 128], BF16) for l in range(2)]
        self.Vh = [P.sbuf("Vh%d" % l, [128, 17, 256], BF16) for l in range(2)]
        self.wstg = [P.sbuf("wstg%d" % i, [128, 2048], F32) for i in range(2)]
        self.wbf = [P.sbuf("wbf%d" % i, [128, 2048], BF16) for i in range(3)]
        self.wple = P.sbuf("wple", [128, 2048], BF16)
        self.rt = P.sbuf("rt", [128, NBLK], F32)
        self.sqb = [P.sbuf("sqb%d" % i, [128, NBLK], BF16) for i in range(2)]
        self.S = [P.sbuf("S%d" % i, [128, 1024], F32) for i in range(10)]
        self.pb = [P.psum("pb%d" % i, [128, 512], F32) for i in range(8)]
        self.wi = 0
        self.wj = 0
        self.dtail = [P.sbuf("dtail%d" % l, [128, 2, 3], F32) for l in range(2)]
        self.dhst = [P.sbuf("dhst%d" % l, [128, 2, NS], F32) for l in range(2)]
        self.dext = P.sbuf("dext", [128, 2, NBLK + 3], F32)
        self.SAw = P.sbuf("SAw", [128, 2, 9, 64], F32)
        self.SAp = [P.sbuf("SAp%d" % l, [128, 2, 64], F32) for l in range(2)]
        self.SCp = [P.sbuf("SCp%d" % l, [128, 2, 64], F32) for l in range(2)]
        self.ctail = [P.sbuf("ctail%d" % l, [128, 6, 3], F32) for l in range(2)]
        self.negA = P.sbuf("negA", [128, 8], F32)
        self.SCw = P.sbuf("SCw", [128, 2, 64], F32)
        self.SCm = P.sbuf("SCm", [128, 4, 64], F32)
        self.kmt = P.sbuf("kmt", [128, 4, 128], F32)
        self.kbgm = P.sbuf("kbgm", [128, 2, 256], F32)
        self.kdm = P.sbuf("kdm", [128, 2, 256], F32)

    def V(self, name):
        o, n = VEC[name]
        return self.vecs[:, o:o + n]

    def C(self, name):
        o, n = CST[name]
        return self.cst[:, o:o + n]

    def wload(self, src_ap, shape, key):
        P = self.P
        n = int(np.prod(shape[1:]))
        bufs = self.wbf if getattr(self, "in_ple", False) else self.wbf + [self.wple]
        wb = bufs[self.wj % len(bufs)]
        self.wj += 1
        bv = wb[:, :n]
        if len(shape) == 3:
            bv = bv.rearrange("p (a b) -> p a b", a=shape[1])
        if key not in self.wc_map:
            off = self.wc_off
            self.wc_map[key] = off
            self.wc_off += n
            stg = self.wstg[self.wi % len(self.wstg)]
            self.wi += 1
            sv = stg[:, :n]
            if len(shape) == 3:
                sv = sv.rearrange("p (a b) -> p a b", a=shape[1])
            P.dma("sp", sv, src_ap, writes=[stg])
            P.op("act", lambda e: e.activation(wb[:, :n], stg[:, :n], AF.Copy), reads=[stg], writes=[wb])
            P.dma("act", self.wcache.ap()[:, off:off + n], wb[:, :n], reads=[wb], writes=[("wc", key)],
                  key=("wcw",) + tuple(_hkey(wb)))
        else:
            off = self.wc_map[key]
            P.dma("sp", wb[:, :n], self.wcache.ap()[:, off:off + n], reads=[("wc", key)], writes=[wb])
        return wb, bv

    def w_in_unit(self, l, c0, n):
        return self.wload(self.d["w_in_t"].ap()[l][:, :, c0:c0 + n], [128, 8, n], ("w_in", l, c0))

    def proj_fm(self, wb, wv, j0, m, out_ps, N, handle):
        P = self.P
        for k in range(8):
            P.op("pe", lambda e, k=k: e.matmul(out_ps[:m, :N], wv[:, k, j0:j0 + m], self.hnT[:, k, :N],
                                              start=(k == 0), stop=(k == 7)),
                 reads=[wb, self.hnT], writes=[handle], sig=(k == 7))

    def rmsnorm_fm(self, src, gv, dst, N, srch, dsth):
        P = self.P
        ps = self.pb[4]
        for k in range(8):
            sq = self.sqb[k % 2]
            P.op("act", lambda e, k=k, sq=sq: e.activation(sq[:, :N], src[:, k, :N], AF.Square),
                 reads=[srch], writes=[sq])
            P.op("pe", lambda e, k=k, sq=sq: e.matmul(ps[:, :N], self.ones_bf[:], sq[:, :N],
                                                      start=(k == 0), stop=(k == 7)),
                 reads=[sq, self.ones_bf], writes=[ps])
        rt = self.rt
        P.op("act", lambda e: e.activation(rt[:, :N], ps[:, :N], AF.Ln, bias=EPS, scale=1.0 / D),
             reads=[ps], writes=[rt])
        P.op("act", lambda e: e.activation(rt[:, :N], rt[:, :N], AF.Exp, scale=-0.5),
             reads=[rt], writes=[rt])
        for k in range(8):
            P.op("dve", lambda e, k=k: e.scalar_tensor_tensor(dst[:, k, :N], src[:, k, :N], gv[:, k:k + 1],
                                                             rt[:, :N], ALU.mult, ALU.mult),
                 reads=[srch, rt, self.vecs], writes=[dsth])

    def setup(self):
        P = self.P
        d = self.d
        P.dma("sp", self.cst[:], d["cst"].ap(), writes=[self.cst])
        P.dma("sp", self.vecs[:], d["vecs"].ap(), writes=[self.vecs])
        P.op("dve", lambda e: e.memset(self.ones_bf[:], 1.0), writes=[self.ones_bf])
        P.op("dve", lambda e: e.memset(self.qT[:, :, :], 0.0), writes=[self.qT])
        for i in range(5, 10):
            P.op("dve", lambda e, i=i: e.memset(self.S[i][:, :], 0.0), writes=[self.S[i]])
        drv = self.drv
        for l in range(2):
            t = self.S[0]
            lam = self.V("dlam%d" % l)
            P.op("act", lambda e: e.activation(t[:, 0:2], lam, AF.Exp, scale=-1.0), reads=[self.vecs], writes=[t])
            P.op("act", lambda e: e.activation(t[:, 0:2], t[:, 0:2], AF.Ln, bias=1.0), reads=[t], writes=[t])
            P.op("dve", lambda e, l=l: e.tensor_scalar(drv[:, 4 * l:4 * l + 2], t[:, 0:2], -8.0, None, ALU.mult),
                 reads=[t], writes=[drv])
            P.op("dve", lambda e, l=l: e.tensor_scalar(drv[:, 4 * l + 2:4 * l + 4], t[:, 0:2], -16.0, None, ALU.mult),
                 reads=[t], writes=[drv])
        P.op("dve", lambda e: e.memset(drv[:, 8:10], 0.0), writes=[drv])
        P.op("dve", lambda e: e.memset(drv[:, 10:12], 1.0), writes=[drv])
        P.op("dve", lambda e: e.memset(drv[:, 12:14], -1.0), writes=[drv])
        t = self.S[0]
        P.op("dve", lambda e: e.tensor_tensor(t[:, 8:10], self.V("alb1"), self.V("alb0"), ALU.subtract),
             reads=[self.vecs], writes=[t])
        P.op("act", lambda e: e.activation(drv[:, 14:16], t[:, 8:10], AF.Sigmoid), reads=[t], writes=[drv])
        P.op("dve", lambda e: e.tensor_scalar(drv[:, 16:18], drv[:, 14:16], -1.0, 1.0, ALU.mult, ALU.add),
             reads=[drv], writes=[drv])
        P.op("dve", lambda e: e.tensor_scalar(drv[:, 18:20], drv[:, 14:16], 1.0, -1.0, ALU.mult, ALU.add),
             reads=[drv], writes=[drv])
        for l in range(2):
            P.op("act", lambda e, l=l: e.activation(self.negA[:, 4 * l:4 * l + 4], self.V("calog%d" % l), AF.Exp), reads=[self.vecs], writes=[self.negA])
        P.op("dve", lambda e: e.tensor_scalar(self.negA[:, :], self.negA[:, :], -1.0, None, ALU.mult), reads=[self.negA], writes=[self.negA])
        if "B" in self.o["mixers"]:
            self.setup_E()

    def lbv(self, l):
        o = 8 + 6 * l
        return self.drv[:, o:o + 2], self.drv[:, o + 2:o + 4], self.drv[:, o + 4:o + 6]

    def setup_E(self):
        P = self.P
        relb = self.S[1]
        lh = self.S[2]
        cm = self.S[0]
        P.dma("sp", relb[:32, 0:4], self.d["relb"].ap(), writes=[relb])
        P.op("act", lambda e: e.activation(relb[:32, 4:8], relb[:32, 0:4], AF.Exp), reads=[relb], writes=[relb])
        for h in range(4):
            P.op("dve", lambda e, h=h: e.tensor_copy(lh[:32, h * 128:(h + 1) * 128],
                                                     relb[:32, 4 + h:5 + h].to_broadcast([32, 128])),
                 reads=[relb], writes=[lh])
        for cb in range(5):
            u0 = cb * 512
            n = min(512, TABL - u0)
            P.dma("sp", cm[:32, :n], self.d["cM"].ap()[:, u0:u0 + n], writes=[cm])
            for h in range(4):
                ps = self.pb[h]
                P.op("pe", lambda e, h=h, ps=ps: e.matmul(ps[:, :n], lh[:32, h * 128:(h + 1) * 128], cm[:32, :n],
                                                          start=True, stop=True),
                     reads=[lh, cm], writes=[ps])
                P.op("act", lambda e, h=h, ps=ps: e.activation(self.E[:, h, u0:u0 + n], ps[:, :n], AF.Copy),
                     reads=[ps], writes=[("E", h)])
        for h in range(4):
            dst = bass.AP(self.zt, h * 128 * ZW, [[ZW + 1, 128], [1, TABL]])
            P.dma("sp", dst, self.E[:, h, :TABL], reads=[("E", h)], writes=[("zt", h)], key="ztw%d" % h)
            src = bass.AP(self.zt, h * 128 * ZW + 127, [[ZW, 128], [1, 17 * 128]])
            P.dma("sp", self.E[:, h, :17 * 128], src, reads=[("zt", h)], writes=[("E", h)], key="ztr%d" % h)

    def run(self):
        P = self.P
        self.setup()
        blocks = [Blk("P", i) for i in range(NTOK // NBLK)] + [Blk("S", 0)]
        if self.o["blocks"] is not None:
            blocks = [blocks[i] for i in self.o["blocks"]]
        for blk in blocks:
            N = blk.N
            if blk.kind == "P":
                src = self.d["xT"].ap()[:, :, blk.tok0:blk.tok0 + N]
            else:
                src = self.d["xsT"].ap()
            P.dma("sp", self.hT[:, :, :N], src, writes=[("hT", f) for f in range(8)])
            for l in range(self.o["layers"]):
                self.layer(blk, l)
            self.final_norm(blk)
        P.finish()

    def hh(self):
        return [("hT", f) for f in range(8)]

    def layer(self, blk, l):
        N = blk.N
        self.rmsnorm_fm(self.hT, self.V("n1g%d" % l), self.hnT, N, self.hh(), [self.hnT])
        mx = self.o["mixers"]
        for k in range(8):
            if "ABCD"[k // 2] not in mx:
                self.P.op("dve", lambda e, k=k: e.memset(self.mixT[:, k, :N], 0.0), writes=[("mixT", k)])
        if "D" in mx:
            self.mixer_D(blk, l)
        if "B" in mx:
            self.mixer_B(blk, l)
        if "A" in mx:
            self.mixer_A(blk, l)
        if "C" in mx:
            self.mixer_C(blk, l)
        if self.o.get("mixin") and blk.idx == 0 and l == 0:
            md = self.nc.dram_tensor("mixin_dbg", [128, 8, N], F32, kind="ExternalInput")
            for k2 in range(4):
                t = self.S[k2]
                self.P.dma("sp", t[:, :2 * N].rearrange("p (c n) -> p c n", c=2), md.ap()[:, 2 * k2:2 * k2 + 2, :], writes=[t])
                self.P.op("dve", lambda e, k2=k2, t=t: e.tensor_copy(self.mixT[:, 2 * k2:2 * k2 + 2, :N], t[:, :2 * N].rearrange("p (c n) -> p c n", c=2)),
                          reads=[t], writes=[("mixT", 2 * k2), ("mixT", 2 * k2 + 1)])
        if blk.idx == 0 and l == 0:
            self.dump("mixT_" + blk.kind, self.mixT[:, :, :N], [128, 8, N], [("mixT", k) for k in range(8)])
        if self.o["dense"]:
            self.dense(blk, l)

    def rmsnorm_fm(self, src, gv, dst, N, srch, dsth, dst_fn=None):
        P = self.P
        ps = self.pb[4]
        for k in range(8):
            sq = self.sqb[k % 2]
            if k % 2 == 0:
                P.op("act", lambda e, k=k, sq=sq: e.activation(sq[:, :N], src[:, k, :N], AF.Square),
                     reads=[srch[k]], writes=[sq])
            else:
                P.op("dve", lambda e, k=k, sq=sq: e.tensor_tensor(sq[:, :N], src[:, k, :N], src[:, k, :N], ALU.mult),
                     reads=[srch[k]], writes=[sq])
            P.op("pe", lambda e, k=k, sq=sq: e.matmul(ps[:, :N], self.ones_bf[:], sq[:, :N],
                                                      start=(k == 0), stop=(k == 7)),
                 reads=[sq, self.ones_bf], writes=[ps])
        rt = self.rt
        P.op("act", lambda e: e.activation(rt[:, :N], ps[:, :N], AF.Ln, bias=EPS, scale=1.0 / D),
             reads=[ps], writes=[rt])
        P.op("act", lambda e: e.activation(rt[:, :N], rt[:, :N], AF.Exp, scale=-0.5),
             reads=[rt], writes=[rt])
        for k in range(8):
            if dst_fn is None:
                o, oh = dst[:, k, :N], dsth
            else:
                o, oh = dst_fn(k)
            P.op("dve", lambda e, k=k, o=o: e.scalar_tensor_tensor(o, src[:, k, :N], gv[:, k:k + 1],
                                                                  rt[:, :N], ALU.mult, ALU.mult),
                 reads=[srch[k], rt, self.vecs], writes=oh)

    def final_norm(self, blk):
        P = self.P
        N = blk.N
        def dst_fn(k):
            t = self.S[k // 2]
            return t[:, (k % 2) * N:(k % 2 + 1) * N], [t]
        self.rmsnorm_fm(self.hT, self.V("fng"), None, N, self.hh(), None, dst_fn=dst_fn)
        for k2 in range(4):
            t = self.S[k2]
            if blk.kind == "P":
                dst = self.od["yT"].ap()[:, 2 * k2:2 * k2 + 2, blk.tok0:blk.tok0 + N]
            else:
                dst = self.od["ysT"].ap()[:, 2 * k2:2 * k2 + 2, :]
            P.dma("act", dst, t[:, :2 * N].rearrange("p (c n) -> p c n", c=2), reads=[t], key="y%d" % k2, out=True)

    def dense(self, blk, l):
        P = self.P
        N = blk.N
        d = self.d
        hT, hnT = self.hT, self.hnT
        mixh = [("mixT", k) for k in range(8)]
        for u in range(4):
            wb, wv = self.wload(d["w_out_t"].ap()[l][:, :, 256 * u:256 * u + 256], [128, 8, 256], ("w_out", l, u))
            for fc in range(2):
                f = 2 * u + fc
                ps = self.pb[f % 2]
                for k in range(8):
                    P.op("pe", lambda e, k=k, ps=ps, wv=wv, fc=fc: e.matmul(
                        ps[:, :N], wv[:, k, fc * 128:(fc + 1) * 128], self.mixT[:, k, :N],
                        start=(k == 0), stop=(k == 7)), reads=[wb, mixh[k]], writes=[ps], sig=(k == 7))
                P.op("dve", lambda e, f=f, ps=ps: e.tensor_tensor(hT[:, f, :N], hT[:, f, :N], ps[:, :N], ALU.add),
                     reads=[ps, ("hT", f)], writes=[("hT", f)])
        if l == 0:
            self.dump("h1_" + blk.kind + str(blk.idx), hT[:, :, :N], [128, 8, N], self.hh())
        self.rmsnorm_fm(hT, self.V("n2g%d" % l), hnT, N, self.hh(), [hnT])
        for hf in range(2):
            for cc in range(NFF // 2):
                c = hf * (NFF // 2) + cc
                wb, wv = self.wload(d["w_ffi_t"].ap()[l, c], [128, 8, 256], ("w_ffi", l, c))
                pg, pu = self.pb[2 * (c % 2)], self.pb[2 * (c % 2) + 1]
                for half, ps in ((0, pg), (1, pu)):
                    for k in range(8):
                        P.op("pe", lambda e, k=k, ps=ps, wv=wv, half=half: e.matmul(
                            ps[:, :N], wv[:, k, half * 128:(half + 1) * 128], hnT[:, k, :N],
                            start=(k == 0), stop=(k == 7)), reads=[wb, hnT], writes=[ps], sig=(k == 7))
                tmp = self.S[8 + c % 2]
                P.op("act", lambda e, pg=pg, tmp=tmp: e.activation(tmp[:, :N], pg[:, :N], AF.Silu),
                     reads=[pg], writes=[tmp])
                P.op("dve", lambda e, cc=cc, pu=pu, tmp=tmp: e.tensor_tensor(self.actT[:, cc, :N], tmp[:, :N], pu[:, :N], ALU.mult),
                     reads=[tmp, pu], writes=[("actT", cc)])
            for f in range(8):
                ps = self.pb[5 + f % 2]
                wb, wv = self.wload(d["w_ffo_t"].ap()[l][:, 11 * hf:11 * hf + 11, 128 * f:128 * f + 128], [128, 11, 128], ("w_ffo", l, hf, f))
                for cc in range(11):
                    P.op("pe", lambda e, cc=cc, ps=ps, wv=wv: e.matmul(
                        ps[:, :N], wv[:, cc, :], self.actT[:, cc, :N], start=(cc == 0), stop=(cc == 10)),
                        reads=[wb, ("actT", cc)], writes=[ps], sig=(cc == 10))
                P.op("dve", lambda e, f=f, ps=ps: e.tensor_tensor(hT[:, f, :N], hT[:, f, :N], ps[:, :N], ALU.add),
                     reads=[ps, ("hT", f)], writes=[("hT", f)])
        if l == 0:
            self.dump("h2_" + blk.kind + str(blk.idx), hT[:, :, :N], [128, 8, N], self.hh())
        self.rmsnorm_fm(hT, self.V("png%d" % l), hnT, N, self.hh(), [hnT])
        pst = self.S[7]
        if blk.kind == "P":
            src = d["pT"].ap()[l][:, :, blk.tok0:blk.tok0 + N]
        else:
            src = d["psT"].ap()[l]
        P.dma("sp", pst[:, :2 * N].rearrange("p (c n) -> p c n", c=2), src, writes=[pst])
        for k in range(2):
            P.op("dve", lambda e, k=k: e.tensor_copy(self.PT[k][:, :N], pst[:, k * N:(k + 1) * N]), reads=[pst], writes=[self.PT[k]])
        self.in_ple = True
        pkey = ("w_ple", l)
        if pkey not in self.wc_map:
            off = self.wc_off
            self.wc_map[pkey] = off
            self.wc_off += 2048
            stg = self.wstg[self.wi % len(self.wstg)]
            self.wi += 1
            P.dma("sp", stg[:, :2048].rearrange("p (a b) -> p a b", a=2), d["w_ple_t"].ap()[l], writes=[stg])
            P.op("act", lambda e: e.activation(self.wple[:, :], stg[:, :], AF.Copy), reads=[stg], writes=[self.wple])
            P.dma("act", self.wcache.ap()[:, off:off + 2048], self.wple[:, :], reads=[self.wple], writes=[("wc", pkey)], key="wcwp")
        else:
            off = self.wc_map[pkey]
            P.dma("sp", self.wple[:, :], self.wcache.ap()[:, off:off + 2048], reads=[("wc", pkey)], writes=[self.wple])
        wpv = self.wple[:, :].rearrange("p (a b) -> p a b", a=2)
        for u in range(4):
            wb, wv = self.wload(d["w_gate_t"].ap()[l][:, :, 256 * u:256 * u + 256], [128, 8, 256], ("w_gate", l, u))
            for fc in range(2):
                f = 2 * u + fc
                pg, pp = self.pb[2 * (f % 2)], self.pb[2 * (f % 2) + 1]
                for k in range(8):
                    P.op("pe", lambda e, k=k, pg=pg, wv=wv, fc=fc: e.matmul(
                        pg[:, :N], wv[:, k, fc * 128:(fc + 1) * 128], hnT[:, k, :N],
                        start=(k == 0), stop=(k == 7)), reads=[wb, hnT], writes=[pg], sig=(k == 7))
                for k in range(2):
                    P.op("pe", lambda e, k=k, pp=pp, f=f: e.matmul(
                        pp[:, :N], wpv[:, k, f * 128:(f + 1) * 128], self.PT[k][:, :N],
                        start=(k == 0), stop=(k == 1)), reads=[self.wple, self.PT[k]], writes=[pp], sig=(k == 1))
                tmp = self.S[8 + f % 2]
                P.op("act", lambda e, pg=pg, tmp=tmp: e.activation(tmp[:, :N], pg[:, :N], AF.Sigmoid),
                     reads=[pg], writes=[tmp])
                P.op("dve", lambda e, pp=pp, tmp=tmp: e.tensor_tensor(tmp[:, :N], tmp[:, :N], pp[:, :N], ALU.mult),
                     reads=[tmp, pp], writes=[tmp])
                P.op("dve", lambda e, f=f, tmp=tmp: e.tensor_tensor(hT[:, f, :N], hT[:, f, :N], tmp[:, :N], ALU.add),
                     reads=[tmp, ("hT", f)], writes=[("hT", f)])
        self.in_ple = False
        if l == 0:
            self.dump("h3_" + blk.kind + str(blk.idx), hT[:, :, :N], [128, 8, N], self.hh())

    def Sv(self, i, N):
        return self.S[i][:, :2 * N].rearrange("p (c n) -> p c n", c=2)

    def mixer_D(self, blk, l):
        P = self.P
        N, nseq, L = blk.N, blk.nseq, blk.L
        d = self.d
        W = 3 + L
        ext = self.dext
        extv = ext[:, :, :nseq * W].rearrange("p c (s j) -> p c s j", s=nseq) if nseq > 1 else None
        def ev(ch, j0, j1):
            if nseq == 1:
                return ext[:, ch, j0:j1]
            return extv[:, ch, :, j0:j1]
        def v3(ap2):
            if nseq == 1:
                return ap2
            return ap2.rearrange("p (s j) -> p s j", s=nseq)
        gg, dm, r, ig, a, w5, hd, y = (self.Sv(i, N) for i in range(8))
        Sh = self.S
        dh = self.dhst[l]
        if blk.kind == "P":
            if blk.first:
                P.op("dve", lambda e: e.memset(ext[:, :, 0:3], 0.0), writes=[ext])
                P.op("dve", lambda e: e.memset(dh[:, :, :], 0.0), writes=[dh])
            else:
                P.op("dve", lambda e: e.tensor_copy(ext[:, :, 0:3], self.dtail[l][:, :, :]),
                     reads=[self.dtail[l]], writes=[ext])
        else:
            for ch in range(2):
                P.dma("sp", extv[:, ch, :, 0:3], d["sdc"].ap()[l][:, ch], writes=[ext], key="sdc_in")
            P.dma("sp", dh[:, :, :], d["sdh"].ap()[l], writes=[dh])
        wb, wv = self.w_in_unit(l, COLS["dx"], 256)
        for ch in range(2):
            ps = self.pb[ch]
            self.proj_fm(wb, wv, ch * 128, 128, ps, N, ps)
            P.op("act", lambda e, ch=ch, ps=ps: e.activation(ev(ch, 3, 3 + L), v3(ps[:, :N]), AF.Copy),
                 reads=[ps], writes=[ext])
        wb, wv = self.w_in_unit(l, COLS["dg"], 256)
        for ch in range(2):
            ps = self.pb[2 + ch]
            self.proj_fm(wb, wv, ch * 128, 128, ps, N, ps)
            P.op("act", lambda e, ch=ch, ps=ps: e.activation(gg[:, ch, :], ps[:, :N], AF.Gelu_apprx_tanh),
                 reads=[ps], writes=[Sh[0]])
        cw = self.V("dcw%d" % l)
        cbias = self.V("dcb%d" % l)
        for ch in range(2):
            P.op("dve", lambda e, ch=ch: e.tensor_scalar(v3(dm[:, ch, :]), ev(ch, 0, L), cw[:, ch * 4:ch * 4 + 1],
                                                         cbias[:, ch:ch + 1], ALU.mult, ALU.add),
                 reads=[ext, self.vecs], writes=[Sh[1]])
            for j in range(1, 4):
                P.op("dve", lambda e, ch=ch, j=j: e.scalar_tensor_tensor(
                    v3(dm[:, ch, :]), ev(ch, j, j + L), cw[:, ch * 4 + j:ch * 4 + j + 1], v3(dm[:, ch, :]),
                    ALU.mult, ALU.add), reads=[ext, self.vecs, Sh[1]], writes=[Sh[1]])
        if blk.kind == "P":
            if blk.last:
                P.dma("act", self.od["pdc"].ap()[l], ext[:, :, L:L + 3], reads=[ext], key="so1_%d_%s" % (l, str(locals().get("gc", "")) + str(locals().get("ch", ""))), out=True)
            else:
                P.op("dve", lambda e: e.tensor_copy(self.dtail[l][:, :, :], ext[:, :, L:L + 3]),
                     reads=[ext], writes=[self.dtail[l]])
        else:
            for ch in range(2):
                P.dma("act", self.od["sdco"].ap()[l][:, ch], extv[:, ch, :, L:L + 3], reads=[ext], key="so2_%d_%s" % (l, str(locals().get("gc", "")) + str(locals().get("ch", ""))), out=True)
        P.dma("sp", self.dgw[:, :], d["dgw"].ap()[:, 512 * l:512 * l + 512], writes=[self.dgw])
        gw = self.dgw[:, :].rearrange("p (w c j) -> p w c j", w=2, c=2)
        sp8 = self.drv[:, 4 * l:4 * l + 2]
        sp16 = self.drv[:, 4 * l + 2:4 * l + 4]
        for ch in range(2):
            for wi, (dst, dsth, bname) in enumerate(((r, Sh[2], "dba"), (ig, Sh[3], "dbx"))):
                ps = self.pb[5 + wi]
                P.op("pe", lambda e, ch=ch, wi=wi, ps=ps: e.matmul(ps[:, :N], gw[:, wi, ch, :], dm[:, ch, :],
                                                                 start=True, stop=True),
                     reads=[self.dgw, Sh[1]], writes=[ps])
                bv = self.V("%s%d" % (bname, l))
                P.op("act", lambda e, ch=ch, ps=ps, dst=dst, bv=bv: e.activation(dst[:, ch, :], ps[:, :N], AF.Sigmoid,
                                                                               bias=bv[:, ch:ch + 1]),
                     reads=[ps, self.vecs], writes=[dsth])
            P.op("act", lambda e, ch=ch: e.activation(a[:, ch, :], r[:, ch, :], AF.Exp, scale=sp8[:, ch:ch + 1]),
                 reads=[Sh[2], self.drv], writes=[Sh[4]])
            P.op("act", lambda e, ch=ch: e.activation(w5[:, ch, :], r[:, ch, :], AF.Exp, scale=sp16[:, ch:ch + 1]),
                 reads=[Sh[2], self.drv], writes=[Sh[5]])
            P.op("dve", lambda e, ch=ch: e.tensor_scalar(w5[:, ch, :], w5[:, ch, :], -1.0, 1.0, ALU.mult, ALU.add),
                 reads=[Sh[5]], writes=[Sh[5]])
            P.op("act", lambda e, ch=ch: e.activation(w5[:, ch, :], w5[:, ch, :], AF.Sqrt), reads=[Sh[5]], writes=[Sh[5]])
            P.op("dve", lambda e, ch=ch: e.tensor_tensor(w5[:, ch, :], w5[:, ch, :], ig[:, ch, :], ALU.mult),
                 reads=[Sh[5], Sh[3]], writes=[Sh[5]])
            P.op("dve", lambda e, ch=ch: e.tensor_tensor(w5[:, ch, :], w5[:, ch, :], dm[:, ch, :], ALU.mult),
                 reads=[Sh[5], Sh[1]], writes=[Sh[5]])
            for s in range(nseq):
                P.op("dve", lambda e, ch=ch, s=s: e.tensor_tensor_scan(
                    hd[:, ch, s * L:(s + 1) * L], a[:, ch, s * L:(s + 1) * L], w5[:, ch, s * L:(s + 1) * L],
                    dh[:, ch, s:s + 1], ALU.mult, ALU.add), reads=[Sh[4], Sh[5], dh], writes=[Sh[6]])
            if nseq == 1:
                src = hd[:, ch, L - 1:L]
            else:
                src = hd[:, ch, :].rearrange("p (s j) -> p s j", s=nseq)[:, :, L - 1]
            P.op("dve", lambda e, ch=ch, src=src: e.tensor_copy(dh[:, ch, 0:nseq], src), reads=[Sh[6]], writes=[dh])
            P.op("dve", lambda e, ch=ch: e.tensor_tensor(y[:, ch, :], hd[:, ch, :], gg[:, ch, :], ALU.mult),
                 reads=[Sh[6], Sh[0]], writes=[Sh[7]])
        if blk.last:
            if blk.kind == "P":
                P.dma("act", self.od["pdh"].ap()[l], dh[:, :, 0], reads=[dh], key="so3_%d_%s" % (l, str(locals().get("gc", "")) + str(locals().get("ch", ""))), out=True, allow_slow_non_contiguous=True)
            else:
                P.dma("act", self.od["sdho"].ap()[l], dh[:, :, :], reads=[dh], key="so4_%d_%s" % (l, str(locals().get("gc", "")) + str(locals().get("ch", ""))), out=True)
        self.group_rmsnorm(y, Sh[7], self.V("dng%d" % l), 6, N)

    def group_rmsnorm(self, y, yh, gv, k0, N):
        P = self.P
        ps = self.pb[4]
        for ch in range(2):
            sq = self.sqb[ch]
            P.op("act", lambda e, ch=ch, sq=sq: e.activation(sq[:, :N], y[:, ch, :], AF.Square), reads=[yh], writes=[sq])
            P.op("pe", lambda e, ch=ch, sq=sq: e.matmul(ps[:, :N], self.ones_bf[:], sq[:, :N], start=(ch == 0), stop=(ch == 1)),
                 reads=[sq, self.ones_bf], writes=[ps])
        rt = self.rt
        P.op("act", lambda e: e.activation(rt[:, :N], ps[:, :N], AF.Ln, bias=EPS, scale=1.0 / 256), reads=[ps], writes=[rt])
        P.op("act", lambda e: e.activation(rt[:, :N], rt[:, :N], AF.Exp, scale=-0.5), reads=[rt], writes=[rt])
        for ch in range(2):
            P.op("dve", lambda e, ch=ch: e.scalar_tensor_tensor(self.mixT[:, k0 + ch, :N], y[:, ch, :], gv[:, ch:ch + 1],
                                                               rt[:, :N], ALU.mult, ALU.mult),
                 reads=[yh, rt, self.vecs], writes=[("mixT", k0 + ch)])

    def mixer_B(self, blk, l):
        P = self.P
        N = blk.N
        d = self.d
        KT, Vh, qT = self.KT[l], self.Vh[l], self.qT
        Eh = [("E", h) for h in range(4)]
        kcol0 = blk.tok0 if blk.kind == "P" else NTOK
        wb, wv = self.w_in_unit(l, COLS["bq"], 256)
        for pr in range(2):
            ps = self.pb[pr]
            self.proj_fm(wb, wv, pr * 128, 128, ps, N, ps)
            for hh in range(2):
                P.op("act", lambda e, pr=pr, ps=ps, hh=hh: e.activation(
                    qT[hh * 64:(hh + 1) * 64, 2 * pr + hh, :N], ps[hh * 64:(hh + 1) * 64, :N], AF.Copy, scale=0.125),
                    reads=[ps], writes=[qT])
        wbk, wvk = self.w_in_unit(l, COLS["bk"], 256)
        for pr in range(2):
            ps = self.pb[2 + pr]
            self.proj_fm(wbk, wvk, pr * 128, 128, ps, N, ps)
            P.op("act", lambda e, pr=pr, ps=ps: e.activation(KT[:, pr, kcol0:kcol0 + N], ps[:, :N], AF.Copy),
                 reads=[ps], writes=[("KT", l, "new")])
        if self.o.get("Bstop") == 1:
            return
        wbv, wvv = self.w_in_unit(l, COLS["bv"], 256)
        for ti, (c0, nt, seq, pos) in enumerate(blk.tiles):
            if self.o.get("Bvar") == 23:
                break
            ps = self.pb[ti % 2]
            for half, (wbx, wvx) in enumerate(((wbk, wvk), (wbv, wvv))):
                for k in range(8):
                    P.op("pe", lambda e, k=k, ps=ps, wvx=wvx, half=half, c0=c0, nt=nt: e.matmul(
                        ps[:nt, half * 256:(half + 1) * 256], self.hnT[:, k, c0:c0 + nt], wvx[:, k, :],
                        start=(k == 0), stop=(k == 7)), reads=[wbx, self.hnT], writes=[ps], sig=(k == 7 and half == 1))
            if self.o.get("Bvar") == 24:
                continue
            kvs = self.S[8 + ti % 2]
            if self.o.get("Bvar") != 26:
                P.op("act", lambda e, ps=ps, kvs=kvs, nt=nt: e.activation(kvs[:nt, :512], ps[:nt, :512], AF.Copy),
                     reads=[ps], writes=[kvs])
            if blk.kind == "P":
                r0 = blk.tok0 + c0
                ok, ov = self.od["pbk"].ap()[l][r0:r0 + nt, :], self.od["pbv"].ap()[l][r0:r0 + nt, :]
                vdst, vh = Vh[:nt, pos, :], ("Vh", l, pos)
            else:
                ok, ov = self.od["sbk"].ap()[l][c0:c0 + nt, :], self.od["sbv"].ap()[l][c0:c0 + nt, :]
                vdst, vh = self.Vnew[:nt, seq, :], ("Vnew", seq)
            if self.o.get("Bvar") not in (21, 25, 26):
                P.dma("act", ok, kvs[:nt, 0:256], reads=[kvs], key="kv_out%d" % (ti % 2), out=True)
                P.dma("act", ov, kvs[:nt, 256:512], reads=[kvs], key="kv_out%d" % (ti % 2), out=True)
            if self.o.get("Bvar") not in (22, 25):
                P.op("dve", lambda e, ps=ps, vdst=vdst, nt=nt: e.tensor_copy(vdst, ps[:nt, 256:512]), reads=[ps], writes=[vh])
        if self.o.get("Bstop") == 2:
            return
        ob = self.Sv(0, N)
        obh = self.S[0]
        if blk.kind == "P":
            for ti, (c0, nt, seq, pos) in enumerate(blk.tiles):
                self.attn_prompt_tile(l, c0, pos, ob, obh)
        else:
            for ti, (c0, nt, seq, pos) in enumerate(blk.tiles):
                self.attn_sample_seq(l, c0, seq, ob, obh)
        self.group_rmsnorm(ob, obh, self.V("bng%d" % l), 2, N)

    def attn_finish(self, accps, lps, ob, obh, c0, nt):
        P = self.P
        rl = self.S[2]
        n4 = 4 * nt
        P.op("act", lambda e: e.activation(rl[:, :n4], lps[:, :n4], AF.Ln), reads=[lps], writes=[rl])
        P.op("act", lambda e: e.activation(rl[:, :n4], rl[:, :n4], AF.Exp, scale=-1.0), reads=[rl], writes=[rl])
        for h in range(4):
            r0, pr = (h % 2) * 64, h // 2
            P.op("dve", lambda e, h=h, r0=r0, pr=pr: e.tensor_tensor(
                ob[r0:r0 + 64, pr, c0:c0 + nt], accps[r0:r0 + 64, pr * nt:(pr + 1) * nt],
                rl[r0:r0 + 64, h * nt:(h + 1) * nt], ALU.mult), reads=[accps, rl], writes=[obh])

    def attn_prompt_tile(self, l, c0, pos, ob, obh):
        P = self.P
        KT, Vh, qT = self.KT[l], self.Vh[l], self.qT
        accps, lps = self.pb[7], self.pb[3]
        ex = self.S[1]
        P.op("dve", lambda e: e.memset(accps[:, :256], 0.0), writes=[accps])
        def qk(j):
            dl = pos - j
            ST = self.pb[5 + j % 2]
            PT = self.PT[j % 2]
            exj = self.Vnew[:, 2 * (j % 2):2 * (j % 2) + 2, :].rearrange("p a b -> p (a b)")
            exh = ("Vnew", 2 * (j % 2))
            for h in range(4):
                pr = h // 2
                P.op("pe", lambda e, h=h, pr=pr, ST=ST, j=j: e.matmul(
                    ST[:, h * 128:(h + 1) * 128], KT[:, pr, j * 128:(j + 1) * 128],
                    qT[:, h, c0:c0 + 128], start=True, stop=True),
                    reads=[("KT", l, "new"), qT], writes=[ST], sig=(h == 3))
            P.op("act", lambda e, ST=ST: e.activation(exj, ST[:, :512], AF.Exp), reads=[ST],
                 writes=[exh, ("Vnew", 2 * (j % 2) + 1)])
            P.op("dve", lambda e, PT=PT, dl=dl: e.tensor_tensor(
                PT[:, :].rearrange("p (h t) -> p h t", h=4), exj.rearrange("p (h t) -> p h t", h=4),
                self.E[:, :, dl * 128:(dl + 1) * 128], ALU.mult), reads=[exh] + [("E", h) for h in range(4)], writes=[PT])

        def pv(j):
            PT = self.PT[j % 2]
            for h in range(4):
                r0, pr = (h % 2) * 64, h // 2
                P.op("pe", lambda e, h=h, r0=r0, pr=pr, PT=PT, j=j: e.matmul(
                    accps[r0:r0 + 64, pr * 128:(pr + 1) * 128], Vh[:, j, h * 64:(h + 1) * 64],
                    PT[:, h * 128:(h + 1) * 128], start=False, stop=(j == pos), skip_group_check=True),
                    reads=[("Vh", l, j), PT], writes=[accps], sig=False)
            P.op("pe", lambda e, PT=PT, j=j: e.matmul(lps[:, :512], self.ones_bf[:], PT[:, :512],
                                                      start=(j == 0), stop=(j == pos)),
                 reads=[self.ones_bf, PT], writes=[lps])

        qk(0)
        for j in range(pos + 1):
            if j + 1 <= pos:
                qk(j + 1)
            pv(j)
        if self.o.get("Bstop") == 3:
            return
        self.attn_finish(accps, lps, ob, obh, c0, 128)

    def attn_sample_seq(self, l, c0, seq, ob, obh):
        P = self.P
        d = self.d
        KT, Vh, qT = self.KT[l], self.Vh[l], self.qT
        nt = LS
        kth = ("KT", l, "new")
        vhh = [("Vh", l, j) for j in range(16)]
        for q4 in range(4):
            t = self.S[4 + q4]
            P.dma("sp", t[:, :1024].rearrange("p (c n) -> p c n", c=2),
                  d["kcT"].ap()[l, seq][:, :, q4 * 512:(q4 + 1) * 512], writes=[t])
            P.op("dve", lambda e, t=t, q4=q4: e.tensor_copy(KT[:, :, q4 * 512:(q4 + 1) * 512],
                                                           t[:, :1024].rearrange("p (c n) -> p c n", c=2)),
                 reads=[t], writes=[kth])
        for q4 in range(4):
            t = self.S[4 + q4]
            P.dma("sp", t[:, :1024].rearrange("p (j c) -> p j c", j=4),
                  d["vc"].ap()[l, seq][:, 4 * q4:4 * q4 + 4, :], writes=[t])
            P.op("act", lambda e, t=t, q4=q4: e.activation(Vh[:, 4 * q4:4 * q4 + 4, :],
                                                          t[:, :1024].rearrange("p (j c) -> p j c", j=4), AF.Copy),
                 reads=[t], writes=vhh[4 * q4:4 * q4 + 4])
        accps, lps = self.pb[7], self.pb[3]
        ST, ST2 = self.pb[5], self.pb[6]
        PT, PT2 = self.PT[0], self.PT[1]
        ex = self.S[1]
        Ehs = [("E", h) for h in range(4)]
        P.op("dve", lambda e: e.memset(accps[:, :256], 0.0), writes=[accps])
        for dl in range(1, 17):
            j = 16 - dl
            for h in range(4):
                pr = h // 2
                o = (dl - 1) * 32 + h * 8
                P.op("pe", lambda e, h=h, pr=pr, o=o, j=j: e.matmul(
                    ST[:, o:o + 8], KT[:, pr, j * 128:(j + 1) * 128], qT[:, h, c0:c0 + 8], start=True, stop=True),
                    reads=[kth, qT], writes=[ST], sig=(dl == 16 and h == 3))
        P.op("act", lambda e: e.activation(ex[:, :512], ST[:, :512], AF.Exp), reads=[ST], writes=[ex])
        v4 = lambda ap: ap.rearrange("p (d h t) -> p d h t", d=16, h=4)
        Ev = self.E[:, :, 128:128 * 17].rearrange("p h (d t) -> p d h t", t=128)[:, :, :, 0:8]
        P.op("dve", lambda e: e.tensor_tensor(v4(PT[:, :512]), v4(ex[:, :512]), Ev, ALU.mult),
             reads=[ex] + Ehs, writes=[PT])
        for dl in range(1, 17):
            j = 16 - dl
            for h in range(4):
                r0, pr = (h % 2) * 64, h // 2
                o = (dl - 1) * 32 + h * 8
                P.op("pe", lambda e, h=h, r0=r0, pr=pr, o=o, j=j: e.matmul(
                    accps[r0:r0 + 64, pr * 8:(pr + 1) * 8], Vh[:, j, h * 64:(h + 1) * 64], PT[:, o:o + 8],
                    start=False, stop=False, skip_group_check=True), reads=[vhh[j], PT], writes=[accps], sig=False)
            P.op("pe", lambda e, dl=dl: e.matmul(lps[:, 0:32], self.ones_bf[:], PT[:, (dl - 1) * 32:dl * 32],
                                                 start=(dl == 1), stop=False),
                 reads=[self.ones_bf, PT], writes=[lps], sig=False)
        kc = NTOK + c0
        for h in range(4):
            pr = h // 2
            P.op("pe", lambda e, h=h, pr=pr: e.matmul(ST2[:nt, h * 8:(h + 1) * 8], KT[:, pr, kc:kc + nt],
                                                      qT[:, h, c0:c0 + nt], start=True, stop=True),
                 reads=[kth, qT], writes=[ST2], sig=(h == 3))
        P.op("act", lambda e: e.activation(ex[:nt, 512:544], ST2[:nt, 0:32], AF.Exp), reads=[ST2], writes=[ex])
        P.op("dve", lambda e: e.tensor_tensor(PT2[:nt, 0:32].rearrange("p (h t) -> p h t", h=4),
                                              ex[:nt, 512:544].rearrange("p (h t) -> p h t", h=4),
                                              self.E[:nt, :, 0:nt], ALU.mult), reads=[ex] + Ehs, writes=[PT2])
        for h in range(4):
            r0, pr = (h % 2) * 64, h // 2
            P.op("pe", lambda e, h=h, r0=r0, pr=pr: e.matmul(
                accps[r0:r0 + 64, pr * 8:(pr + 1) * 8], self.Vnew[:nt, seq, h * 64:(h + 1) * 64], PT2[:nt, h * 8:(h + 1) * 8],
                start=False, stop=True, skip_group_check=True), reads=[("Vnew", seq), PT2], writes=[accps], sig=(h == 3))
        P.op("pe", lambda e: e.matmul(lps[:, 0:32], self.ones_bf[:nt, :], PT2[:nt, 0:32], start=False, stop=True),
             reads=[self.ones_bf, PT2], writes=[lps])
        self.attn_finish(accps, lps, ob, obh, c0, nt)


def _mixer_A(self, blk, l):
    P = self.P
    N, nseq, L = blk.N, blk.nseq, blk.L
    d = self.d
    csz = 16 if blk.kind == "P" else LS
    ncht = N // csz
    Sh = self.S
    Q, KA, LF, G, EG, W5, OA, SG = (self.Sv(i, N) for i in range(8))
    lb, omlb, nomlb = self.lbv(l)
    aT = self.actT
    ah = lambda c: ("actT", c)
    kt = aT[:, 0:2, :N]
    SA = self.SAw
    SAp = self.SAp[l]
    qT = self.qT
    wb, wv = self.w_in_unit(l, COLS["aq"], 256)
    for pr in range(2):
        ps = self.pb[pr]
        self.proj_fm(wb, wv, pr * 128, 128, ps, N, ps)
        P.op("act", lambda e, pr=pr, ps=ps: e.activation(Q[:, pr, :], ps[:, :N], AF.Copy), reads=[ps], writes=[Sh[0]])
    wb, wv = self.w_in_unit(l, COLS["af"], 256)
    for pr in range(2):
        ps = self.pb[2 + pr]
        self.proj_fm(wb, wv, pr * 128, 128, ps, N, ps)
        P.op("act", lambda e, pr=pr, ps=ps: e.activation(KA[:, pr, :], ps[:, :N], AF.Sigmoid), reads=[ps], writes=[Sh[1]])
        P.op("dve", lambda e, pr=pr: e.tensor_scalar(LF[:, pr, :], KA[:, pr, :], omlb[:, pr:pr + 1], lb[:, pr:pr + 1],
                                                     ALU.mult, ALU.add), reads=[Sh[1], self.drv], writes=[Sh[2]])
        P.op("act", lambda e, pr=pr: e.activation(LF[:, pr, :], LF[:, pr, :], AF.Ln), reads=[Sh[2]], writes=[Sh[2]])
        P.op("dve", lambda e, pr=pr: e.tensor_scalar(KA[:, pr, :], KA[:, pr, :], nomlb[:, pr:pr + 1], omlb[:, pr:pr + 1],
                                                     ALU.mult, ALU.add), reads=[Sh[1], self.drv], writes=[Sh[1]])
        rs = self.C("reset16")[:, :N] if blk.kind == "P" else self.C("resetS")[:, :N]
        P.op("dve", lambda e, pr=pr, rs=rs: e.tensor_tensor_scan(G[:, pr, :], rs, LF[:, pr, :], 0.0, ALU.mult, ALU.add),
             reads=[Sh[2], self.cst], writes=[Sh[3]])
        P.op("act", lambda e, pr=pr: e.activation(EG[:, pr, :], G[:, pr, :], AF.Exp), reads=[Sh[3]], writes=[Sh[4]])
        P.op("act", lambda e, pr=pr: e.activation(W5[:, pr, :], G[:, pr, :], AF.Exp, scale=-1.0), reads=[Sh[3]], writes=[Sh[5]])
        P.op("dve", lambda e, pr=pr: e.tensor_tensor(Q[:, pr, :], Q[:, pr, :], EG[:, pr, :], ALU.mult),
             reads=[Sh[0], Sh[4]], writes=[Sh[0]])
        for hh in range(2):
            P.op("act", lambda e, pr=pr, hh=hh: e.activation(qT[hh * 64:(hh + 1) * 64, 2 * pr + hh, :N],
                                                            Q[hh * 64:(hh + 1) * 64, pr, :], AF.Copy),
                 reads=[Sh[0]], writes=[qT])
        P.op("dve", lambda e, pr=pr: e.tensor_tensor(kt[:, pr, :], KA[:, pr, :], W5[:, pr, :], ALU.mult),
             reads=[Sh[1], Sh[5]], writes=[ah(pr)])
        Gv = G[:, pr, :].rearrange("p (n j) -> p n j", j=csz)
        Wv = W5[:, pr, :].rearrange("p (n j) -> p n j", j=csz)
        P.op("dve", lambda e, Gv=Gv, Wv=Wv: e.tensor_tensor(Wv, Gv[:, :, csz - 1:csz].to_broadcast([128, ncht, csz]), Gv,
                                                            ALU.subtract), reads=[Sh[3], ah(pr)], writes=[Sh[5]])
        P.op("act", lambda e, pr=pr: e.activation(W5[:, pr, :], W5[:, pr, :], AF.Exp), reads=[Sh[5]], writes=[Sh[5]])
        P.op("dve", lambda e, pr=pr: e.tensor_tensor(W5[:, pr, :], W5[:, pr, :], KA[:, pr, :], ALU.mult),
             reads=[Sh[5], Sh[1]], writes=[Sh[5]])
    wb, wv = self.w_in_unit(l, COLS["ag"], 256)
    for pr in range(2):
        ps = self.pb[pr]
        self.proj_fm(wb, wv, pr * 128, 128, ps, N, ps)
        P.op("act", lambda e, pr=pr, ps=ps: e.activation(SG[:, pr, :], ps[:, :N], AF.Silu), reads=[ps], writes=[Sh[7]])
    wbi, wvi = self.w_in_unit(l, COLS["ai"], 256)
    ident = self.C("ident")
    for ti, (c0, nt, seq, pos) in enumerate(blk.tiles):
        nch = max(nt // csz, 1)
        slot = 6 + ti % 2
        kdT = aT[:, slot, 0:256]
        Vb = aT[:, slot, 256:512]
        Vblk = aT[:, 2:6, :]
        if blk.kind == "P":
            if pos == 0:
                P.op("dve", lambda e: e.memset(SA[:, :, 0, :], 0.0), writes=[SA])
            elif ti == 0:
                P.op("dve", lambda e: e.tensor_copy(SA[:, :, 0, :], SAp[:, :, :]), reads=[SAp], writes=[SA])
        else:
            P.dma("sp", SA[:, :, 0, :], d["sa"].ap()[l, seq], writes=[SA])
        pv = self.pb[2]
        for k in range(8):
            P.op("pe", lambda e, k=k, c0=c0, nt=nt: e.matmul(pv[:nt, 0:256], self.hnT[:, k, c0:c0 + nt], wvi[:, k, :],
                                                            start=(k == 0), stop=(k == 7)),
                 reads=[wbi, self.hnT], writes=[pv], sig=(k == 7))
        P.op("act", lambda e, nt=nt, Vb=Vb: e.activation(Vb[:nt, :], pv[:nt, 0:256], AF.Copy), reads=[pv], writes=[ah(slot)])
        if nch > 1:
            for h in range(4):
                P.op("dve", lambda e, h=h, nt=nt: e.tensor_tensor(
                    Vblk[:nt, h, :].rearrange("p (n v) -> p n v", n=8),
                    pv[:nt, h * 64:(h + 1) * 64].rearrange("p (o v) -> p o v", o=1).to_broadcast([nt, 8, 64]),
                    self.C("blk16")[:nt, :].rearrange("p (n o) -> p n o", o=1).to_broadcast([nt, 8, 64]), ALU.mult),
                    reads=[pv, self.cst], writes=[ah(2 + h)])
        pk = self.pb[3]
        for pr in range(2):
            P.op("pe", lambda e, pr=pr, c0=c0, nt=nt: e.transpose(pk[:nt, pr * 128:(pr + 1) * 128], W5[:, pr, c0:c0 + nt], ident),
                 reads=[Sh[5], self.cst], writes=[pk])
        P.op("act", lambda e, nt=nt, kdT=kdT: e.activation(kdT[:nt, :], pk[:nt, 0:256], AF.Copy), reads=[pk], writes=[ah(slot)])
        ST = self.pb[5]
        for h in range(4):
            pr = h // 2
            P.op("pe", lambda e, h=h, pr=pr, c0=c0, nt=nt: e.matmul(ST[:nt, h * 128:h * 128 + nt], kt[:, pr, c0:c0 + nt],
                                                                   qT[:, h, c0:c0 + nt], start=True, stop=True),
                 reads=[ah(pr), qT], writes=[ST], sig=(h == 3))
        PTa = self.PT[ti % 2]
        P.op("dve", lambda e, nt=nt, PTa=PTa: e.tensor_tensor(
            PTa[:nt, :].rearrange("p (h t) -> p h t", h=4)[:, :, :nt],
            ST[:nt, :].rearrange("p (h t) -> p h t", h=4)[:, :, :nt],
            self.C("mask16T")[:nt, :nt].rearrange("p (o t) -> p o t", o=1).to_broadcast([nt, 4, nt]), ALU.mult),
            reads=[ST, self.cst], writes=[PTa])
        Ups = (self.pb[0], self.pb[1])
        for h in range(4):
            r0, pr = (h % 2) * 64, h // 2
            rhs = Vblk[:nt, h, :] if nch > 1 else Vb[:nt, h * 64:(h + 1) * 64]
            rh = ah(2 + h) if nch > 1 else ah(slot)
            P.op("pe", lambda e, h=h, r0=r0, pr=pr, nt=nt, rhs=rhs, nch=nch: e.matmul(
                Ups[pr][r0:r0 + 64, 0:nch * 64], kdT[:nt, h * 64:(h + 1) * 64], rhs, start=True, stop=True),
                reads=[ah(slot), rh], writes=[Ups[pr]], sig=(h % 2 == 1))
        ob = self.pb[6]
        P.op("dve", lambda e: e.memset(ob[:, 0:256], 0.0), writes=[ob])
        for h in range(4):
            r0, pr = (h % 2) * 64, h // 2
            P.op("pe", lambda e, h=h, r0=r0, pr=pr, nt=nt, PTa=PTa, Vb=Vb: e.matmul(
                ob[r0:r0 + 64, pr * 128:pr * 128 + nt], Vb[:nt, h * 64:(h + 1) * 64], PTa[:nt, h * 128:h * 128 + nt],
                start=False, stop=False, skip_group_check=True), reads=[ah(slot), PTa], writes=[ob], sig=False)
        for n in range(nch):
            cend = c0 + n * csz + csz - 1
            for pr in range(2):
                P.op("dve", lambda e, n=n, pr=pr, cend=cend: e.scalar_tensor_tensor(
                    SA[:, pr, n + 1, :], SA[:, pr, n, :], EG[:, pr, cend:cend + 1], Ups[pr][:, n * 64:(n + 1) * 64],
                    ALU.mult, ALU.add), reads=[SA, Sh[4], Ups[pr]], writes=[SA])
        for h in range(4):
            r0, pr = (h % 2) * 64, h // 2
            for n in range(nch):
                cs = c0 + n * csz
                P.op("pe", lambda e, h=h, r0=r0, pr=pr, n=n, cs=cs: e.matmul(
                    ob[r0:r0 + 64, pr * 128 + n * csz:pr * 128 + (n + 1) * csz], SA[r0:r0 + 64, pr, n, :],
                    Q[r0:r0 + 64, pr, cs:cs + csz], start=False, stop=True, skip_group_check=True),
                    reads=[SA, Sh[0]], writes=[ob], sig=(h == 3 and n == nch - 1))
        for pr in range(2):
            P.op("act", lambda e, pr=pr, c0=c0, nt=nt: e.activation(OA[:, pr, c0:c0 + nt], ob[:, pr * 128:pr * 128 + nt], AF.Copy),
                 reads=[ob], writes=[Sh[6]])
        last_of_seq = (blk.kind == "S") or (blk.last and ti == len(blk.tiles) - 1)
        if last_of_seq:
            dst = self.od["pa"].ap()[l] if blk.kind == "P" else self.od["sao"].ap()[l, seq]
            P.dma("act", dst, SA[:, :, nch, :], reads=[SA], key="sa_out", out=True)
        elif ti == len(blk.tiles) - 1:
            P.op("dve", lambda e, nch=nch: e.tensor_copy(SAp[:, :, :], SA[:, :, nch, :]), reads=[SA], writes=[SAp])
        else:
            P.op("dve", lambda e, nch=nch: e.tensor_copy(SA[:, :, 0, :], SA[:, :, nch, :]), reads=[SA], writes=[SA])
    self.head_norm_gate(OA, Sh[6], SG, Sh[7], self.V("ang%d" % l), 0, N)


def _head_norm_gate(self, O, Oh, SG, SGh, gv, k0, N, gcol=None):
    P = self.P
    sq, tmp = self.S[8], self.S[9]
    for pr in range(2):
        ps = self.pb[4]
        P.op("act", lambda e, pr=pr: e.activation(sq[:, :N], O[:, pr, :], AF.Square), reads=[Oh], writes=[sq])
        P.op("pe", lambda e: e.matmul(ps[:, :N], self.C("onesbd"), sq[:, :N], start=True, stop=True),
             reads=[sq, self.cst], writes=[ps])
        rt = self.rt
        P.op("act", lambda e: e.activation(rt[:, :N], ps[:, :N], AF.Ln, bias=EPS, scale=1.0 / 64), reads=[ps], writes=[rt])
        P.op("act", lambda e: e.activation(rt[:, :N], rt[:, :N], AF.Exp, scale=-0.5), reads=[rt], writes=[rt])
        g = gv[:, pr:pr + 1] if gcol is None else gv[:, 0:1]
        P.op("dve", lambda e, pr=pr, g=g: e.scalar_tensor_tensor(tmp[:, :N], O[:, pr, :], g, rt[:, :N], ALU.mult, ALU.mult),
             reads=[Oh, rt, self.vecs], writes=[tmp])
        P.op("dve", lambda e, pr=pr: e.tensor_tensor(self.mixT[:, k0 + pr, :N], tmp[:, :N], SG[:, pr, :], ALU.mult),
             reads=[tmp, SGh], writes=[("mixT", k0 + pr)])


Builder.mixer_A = _mixer_A
Builder.head_norm_gate = _head_norm_gate


def _mixer_C(self, blk, l):
    P = self.P
    N, nseq, L = blk.N, blk.nseq, blk.L
    d = self.d
    cs = 64 if blk.kind == "P" else LS
    nlev = 5 if blk.kind == "P" else 2
    Sh = self.S
    XC = [self.Sv(g, N) for g in range(3)]
    SGz, OC = self.Sv(3, N), self.Sv(4, N)
    W = 3 + L
    ext = self.dext
    extv = ext[:, :, :nseq * W].rearrange("p c (s j) -> p c s j", s=nseq) if nseq > 1 else None
    def ev(ch, j0, j1):
        return ext[:, ch, j0:j1] if nseq == 1 else extv[:, ch, :, j0:j1]
    def v3(ap2):
        return ap2 if nseq == 1 else ap2.rearrange("p (s j) -> p s j", s=nseq)
    cw = self.V("ccw%d" % l)
    ident = self.C("ident")
    for g, cname in enumerate(("cq", "ck", "cv")):
        wb, wv = self.w_in_unit(l, COLS[cname], 256)
        X = XC[g]
        for ch in range(2):
            gc = 2 * g + ch
            if blk.kind == "P":
                if blk.first:
                    P.op("dve", lambda e, ch=ch: e.memset(ext[:, ch, 0:3], 0.0), writes=[ext])
                else:
                    P.op("dve", lambda e, ch=ch, gc=gc: e.tensor_copy(ext[:, ch, 0:3], self.ctail[l][:, gc, :]),
                         reads=[self.ctail[l]], writes=[ext])
            else:
                P.dma("sp", extv[:, ch, :, 0:3], d["scc"].ap()[l][:, gc], writes=[ext], key="scc_in")
            ps = self.pb[ch]
            self.proj_fm(wb, wv, ch * 128, 128, ps, N, ps)
            P.op("act", lambda e, ch=ch, ps=ps: e.activation(ev(ch, 3, 3 + L), v3(ps[:, :N]), AF.Copy),
                 reads=[ps], writes=[ext])
            P.op("dve", lambda e, ch=ch, gc=gc: e.tensor_scalar(v3(X[:, ch, :]), ev(ch, 0, L), cw[:, gc * 4:gc * 4 + 1], None, ALU.mult),
                 reads=[ext, self.vecs], writes=[Sh[g]])
            for j in range(1, 4):
                P.op("dve", lambda e, ch=ch, gc=gc, j=j: e.scalar_tensor_tensor(
                    v3(X[:, ch, :]), ev(ch, j, j + L), cw[:, gc * 4 + j:gc * 4 + j + 1], v3(X[:, ch, :]), ALU.mult, ALU.add),
                    reads=[ext, self.vecs, Sh[g]], writes=[Sh[g]])
            if blk.kind == "P":
                if blk.last:
                    P.dma("act", self.od["pcc"].ap()[l][:, gc, :], ext[:, ch, L:L + 3], reads=[ext], key="so5_%d_%s" % (l, str(locals().get("gc", "")) + str(locals().get("ch", ""))), out=True)
                else:
                    P.op("dve", lambda e, ch=ch, gc=gc: e.tensor_copy(self.ctail[l][:, gc, :], ext[:, ch, L:L + 3]),
                         reads=[ext], writes=[self.ctail[l]])
            else:
                P.dma("act", self.od["scco"].ap()[l][:, gc], extv[:, ch, :, L:L + 3], reads=[ext], key="so6_%d_%s" % (l, str(locals().get("gc", "")) + str(locals().get("ch", ""))), out=True)
            P.op("act", lambda e, ch=ch: e.activation(X[:, ch, :], X[:, ch, :], AF.Silu), reads=[Sh[g]], writes=[Sh[g]])
    for g, sc_ in ((0, 0.125), (1, 1.0)):
        X = XC[g]
        for pr in range(2):
            sq, ps, rt = Sh[8], self.pb[4], self.rt
            P.op("act", lambda e, pr=pr, X=X: e.activation(sq[:, :N], X[:, pr, :], AF.Square), reads=[Sh[g]], writes=[sq])
            P.op("pe", lambda e: e.matmul(ps[:, :N], self.C("onesbd"), sq[:, :N], start=True, stop=True),
                 reads=[sq, self.cst], writes=[ps])
            P.op("act", lambda e: e.activation(rt[:, :N], ps[:, :N], AF.Ln, bias=EPS), reads=[ps], writes=[rt])
            P.op("act", lambda e: e.activation(rt[:, :N], rt[:, :N], AF.Exp, scale=-0.5), reads=[rt], writes=[rt])
            P.op("dve", lambda e, pr=pr, X=X, sc_=sc_: e.scalar_tensor_tensor(X[:, pr, :], X[:, pr, :], sc_, rt[:, :N], ALU.mult, ALU.mult),
                 reads=[Sh[g], rt], writes=[Sh[g]])
    if blk.idx == 0 and l == 0 and blk.kind == "P":
        for g, nm in enumerate(("C_q", "C_k", "C_v")):
            self.dump(nm, XC[g], [128, 2, N], [Sh[g]])
    wb, wv = self.w_in_unit(l, COLS["cz"], 256)
    for pr in range(2):
        ps = self.pb[pr]
        self.proj_fm(wb, wv, pr * 128, 128, ps, N, ps)
        P.op("act", lambda e, pr=pr, ps=ps: e.activation(SGz[:, pr, :], ps[:, :N], AF.Silu), reads=[ps], writes=[Sh[3]])
    wb8, wv8 = self.w_in_unit(l, COLS["cb"], 8)
    SCw, SCp, SCm = self.SCw, self.SCp[l], self.SCm
    kmt, kbgm, kdm = self.kmt, self.kbgm, self.kdm
    for t_ in (SCm, kmt, kbgm, kdm):
        P.op("dve", lambda e, t_=t_: e.memset(t_[:], 0.0), writes=[t_])
    def mask_state():
        for hh in range(2):
            P.op("pool", lambda e, hh=hh: e.tensor_copy(SCm[hh * 64:(hh + 1) * 64, hh::2, :], SCw[hh * 64:(hh + 1) * 64, :, :]),
                 reads=[SCw], writes=[SCm])
    scb = Sh[9]
    def Sq(i, q):
        return Sh[i][:, q * 256:(q + 1) * 256], ("Sq", i, q)
    (kTM, hkTM), (vTM, hvTM), (bv, hbv), (kbg, hkbg) = (Sq(5, q) for q in range(4))
    (eD, heD), (PA, hPA), (eDT, heDT), (RA, hRA) = (Sq(6, q) for q in range(4))
    (Nm, hNm), (PB, hPB), (RB, hRB), (kd, hkd) = (Sq(7, q) for q in range(4))
    (un, hun), (o1s, ho1s), (oTM, hoTM), (wT, hwT) = (Sq(8, q) for q in range(4))
    def sc_t(name, ti):
        o = dict(bet=0, la=4, g=8, eg=12, nbet=16, begp=20, glb=24, kds=28, egl=32, t1=36)[name] + 40 * ti
        return scb[:, o:o + 4], ("sc", name, ti)
    pb = self.pb
    def tile_scalars(ti):
        return tuple(sc_t(n, ti) for n in ("bet", "la", "g", "eg", "nbet", "begp", "glb", "kds", "egl", "t1"))
    for ti, (c0, nt, seq, pos) in enumerate(blk.tiles):
        nc2 = max(nt // 64, 1)
        g0 = 32 * ti
        sc = lambda n: sc_t(n, ti)
        for k in range(8):
            P.op("pe", lambda e, k=k: e.matmul(pb[1][:nt, g0:g0 + 8], self.hnT[:, k, c0:c0 + nt], wv8[:, k, :], start=(k == 0), stop=(k == 7)),
                 reads=[wb8, self.hnT], writes=[pb[1]], sig=(k == 7))
        (bet, hbet), (la, hla), (gg, hgg), (eg, heg), (nbet, hnbet), (begp, hbegp), (glb, hglb), (kds, hkds), (egl, hegl), (t1, ht1) = (
            sc(n) for n in ("bet", "la", "g", "eg", "nbet", "begp", "glb", "kds", "egl", "t1"))
        P.op("act", lambda e: e.activation(bet[:nt, :], pb[1][:nt, g0:g0 + 4], AF.Sigmoid), reads=[pb[1]], writes=[hbet])
        P.op("dve", lambda e: e.tensor_tensor(t1[:nt, :], pb[1][:nt, g0 + 4:g0 + 8], self.V("cdtb%d" % l)[:nt, :], ALU.add),
             reads=[pb[1], self.vecs], writes=[ht1])
        P.op("act", lambda e: e.activation(t1[:nt, :], t1[:nt, :], AF.Exp), reads=[ht1], writes=[ht1])
        P.op("act", lambda e: e.activation(t1[:nt, :], t1[:nt, :], AF.Ln, bias=1.0), reads=[ht1], writes=[ht1])
        P.op("dve", lambda e: e.tensor_tensor(la[:nt, :], t1[:nt, :], self.negA[:nt, 4 * l:4 * l + 4], ALU.mult),
             reads=[ht1, self.negA], writes=[hla])
        P.op("pe", lambda e: e.matmul(pb[1][:nt, g0 + 8:g0 + 12], self.C("u64")[:nt, :nt], la[:nt, :], start=True, stop=True),
             reads=[hla, self.cst], writes=[pb[1]])
        P.op("act", lambda e: e.activation(gg[:nt, :], pb[1][:nt, g0 + 8:g0 + 12], AF.Copy), reads=[pb[1]], writes=[hgg])
        P.op("act", lambda e: e.activation(eg[:nt, :], gg[:nt, :], AF.Exp), reads=[hgg], writes=[heg])
        sel = self.C("sel64")[:nt, :nt] if blk.kind == "P" else self.C("sel8")[:nt, :nt]
        P.op("pe", lambda e: e.matmul(pb[1][:nt, g0 + 12:g0 + 16], sel, gg[:nt, :], start=True, stop=True),
             reads=[hgg, self.cst], writes=[pb[1]])
        P.op("dve", lambda e: e.tensor_tensor(kds[:nt, :], pb[1][:nt, g0 + 12:g0 + 16], gg[:nt, :], ALU.subtract),
             reads=[pb[1], hgg], writes=[hkds])
        P.op("act", lambda e: e.activation(kds[:nt, :], kds[:nt, :], AF.Exp), reads=[hkds], writes=[hkds])
        P.op("dve", lambda e: e.tensor_scalar(nbet[:nt, :], bet[:nt, :], -1.0, None, ALU.mult), reads=[hbet], writes=[hnbet])
        P.op("dve", lambda e: e.tensor_tensor(begp[:nt, :], bet[:nt, :], eg[:nt, :], ALU.mult), reads=[hbet, heg], writes=[hbegp])
        selc = self.C("selc")[:nt, 0:nc2] if blk.kind == "P" else self.C("sel8")[:nt, 0:1]
        for pr in range(2):
            rep = scb[:, 768 + pr * 128:896 + pr * 128]
            P.op("dve", lambda e, pr=pr, rep=rep: e.tensor_copy(
                rep[:nt, :].rearrange("p (a b) -> p a b", a=2),
                eg[:nt, 2 * pr:2 * pr + 2].rearrange("p (a o) -> p a o", o=1).to_broadcast([nt, 2, 64])),
                reads=[heg], writes=[("sc", "rep", pr)])
            P.op("pe", lambda e, pr=pr, rep=rep: e.matmul(pb[1][:, g0 + 16 + 2 * pr:g0 + 16 + 2 * pr + nc2], rep[:nt, :], selc, start=True, stop=True),
                 reads=[("sc", "rep", pr), self.cst], writes=[pb[1]])
        P.op("act", lambda e: e.activation(egl[:, :], pb[1][:, g0 + 16:g0 + 20], AF.Copy), reads=[pb[1]], writes=[hegl])
        if self.o.get("Cstop") == 2:
            break
    for ti, (c0, nt, seq, pos) in enumerate(blk.tiles):
        nc2 = max(nt // 64, 1)
        (bet, hbet), (la, hla), (gg, hgg), (eg, heg), (nbet, hnbet), (begp, hbegp), (glb, hglb), (kds, hkds), (egl, hegl), (t1, ht1) = tile_scalars(ti)
        def hv(ap):
            return ap[:nt, 0:256].rearrange("p (h s) -> p h s", h=4)[:, :, :cs]
        def bc(ap2):
            return ap2.rearrange("p (o s) -> p o s", o=1).to_broadcast([nt, 4, cs])
        CH = [(c, h) for c in range(nc2) for h in range(4)]
        if self.o.get("Cdiag"):
            CH = [(c, h) for (c, h) in CH if h % 2 == c]
        if blk.kind == "P":
            if pos == 0:
                P.op("dve", lambda e: e.memset(SCw[:, :, :], 0.0), writes=[SCw])
            elif ti == 0:
                P.op("dve", lambda e: e.tensor_copy(SCw[:, :, :], SCp[:, :, :]), reads=[SCp], writes=[SCw])
        else:
            P.dma("sp", SCw[:, :, :], d["sc"].ap()[l, seq], writes=[SCw])
        if self.o.get("Cstop") == 1:
            break
        mask_state()
        for hh in range(2):
            P.op("pool", lambda e, hh=hh: e.tensor_copy(kmt[hh * 64:(hh + 1) * 64, hh::2, :nt], XC[1][hh * 64:(hh + 1) * 64, :, c0:c0 + nt]),
                 reads=[Sh[1]], writes=[kmt])
        for g in (1, 2):
            for pr in range(2):
                o = (g - 1) * 256 + pr * 128
                P.op("pe", lambda e, g=g, pr=pr, o=o: e.transpose(pb[0][:nt, o:o + 128], XC[g][:, pr, c0:c0 + nt], ident),
                     reads=[Sh[g], self.cst], writes=[pb[0]])
        P.op("act", lambda e: e.activation(Sh[5][:nt, 0:512], pb[0][:nt, 0:512], AF.Copy), reads=[pb[0]], writes=[hkTM, hvTM])
        if ti == 0 and blk.idx == 0 and l == 0 and blk.kind == "P":
            self.dump("C_sc", scb[:, 0:64], [128, 64], [hbet, hla, hgg, heg, hnbet, hbegp, hkds, hegl])
        for h in range(4):
            bufU = scb[:, 256 + (h % 2) * 128:384 + (h % 2) * 128]
            bufL = scb[:, 512 + (h % 2) * 128:640 + (h % 2) * 128]
            P.op("dve", lambda e, h=h, bufU=bufU: e.tensor_scalar(bufU[:nt, :nt], self.C("u64")[:nt, :nt], la[:nt, h:h + 1], None, ALU.mult),
                 reads=[hla, self.cst], writes=[("sc", "U", h % 2)])
            P.op("pe", lambda e, h=h, bufU=bufU: e.matmul(pb[2][:nt, h * 64:h * 64 + cs], bufU[:nt, :nt], self.C("lsloc")[:nt, :cs],
                                                          start=True, stop=True), reads=[("sc", "U", h % 2), self.cst], writes=[pb[2]])
            P.op("dve", lambda e, h=h, bufL=bufL: e.tensor_scalar(bufL[:nt, :nt], self.C("l64s")[:nt, :nt], la[:nt, h:h + 1], None, ALU.mult),
                 reads=[hla, self.cst], writes=[("sc", "L", h % 2)])
            P.op("pe", lambda e, h=h, bufL=bufL: e.matmul(pb[3][:nt, h * 64:h * 64 + cs], bufL[:nt, :nt], self.C("uloc")[:nt, :cs],
                                                          start=True, stop=True), reads=[("sc", "L", h % 2), self.cst], writes=[pb[3]])
        P.op("act", lambda e: e.activation(hv(eD), hv(pb[2]), AF.Exp), reads=[pb[2]], writes=[heD])
        P.op("act", lambda e: e.activation(hv(eDT), hv(pb[3]), AF.Exp), reads=[pb[3]], writes=[heDT])
        for c, h in CH:
            pc, r0, pr = 64 * c, 64 * (h % 2), h // 2
            cc = c0 + 64 * c
            P.op("pe", lambda e, pc=pc, r0=r0, pr=pr, cc=cc, h=h: e.matmul(
                pb[5][pc:pc + cs, h * 64:h * 64 + cs], kmt[:, h, pc:pc + cs], XC[1][:, pr, cc:cc + cs],
                start=True, stop=True), reads=[Sh[1], kmt], writes=[pb[5]], sig=(c == nc2 - 1 and h == 3))
        for c, h in CH:
            pc, r0, pr = 64 * c, 64 * (h % 2), h // 2
            cc = c0 + 64 * c
            P.op("pe", lambda e, pc=pc, r0=r0, pr=pr, cc=cc, h=h: e.matmul(
                pb[6][pc:pc + cs, h * 64:h * 64 + cs], kmt[:, h, pc:pc + cs], XC[0][:, pr, cc:cc + cs],
                start=True, stop=True), reads=[kmt, Sh[0]], writes=[pb[6]], sig=(c == nc2 - 1 and h == 3))
        P.op("dve", lambda e: e.tensor_tensor(hv(eD), hv(pb[5]), hv(eD), ALU.mult), reads=[pb[5], heD], writes=[heD])
        for h in range(4):
            P.op("dve", lambda e, h=h: e.scalar_tensor_tensor(PA[:nt, h * 64:h * 64 + cs], eD[:nt, h * 64:h * 64 + cs], nbet[:nt, h:h + 1],
                                                             self.C("trilS")[:nt, :cs], ALU.mult, ALU.mult),
                 reads=[heD, hnbet, self.cst], writes=[hPA])
        P.op("dve", lambda e: e.tensor_tensor(hv(eDT), hv(pb[6]), hv(eDT), ALU.mult), reads=[pb[6], heDT], writes=[heDT])
        P.op("dve", lambda e: e.tensor_tensor(hv(eDT), hv(eDT), bc(self.C("triuI")[:nt, :cs]), ALU.mult),
             reads=[heDT, self.cst], writes=[heDT])
        if self.o.get("Cstop") == 3:
            break
        for c, h in CH:
            pc = 64 * c
            P.op("pe", lambda e, pc=pc, h=h: e.matmul(pb[7][pc:pc + cs, h * 64:h * 64 + cs], PA[pc:pc + cs, h * 64:h * 64 + cs],
                                                     ident[pc:pc + cs, pc:pc + cs], start=True, stop=True),
                 reads=[hPA, self.cst], writes=[pb[7]], sig=(c == nc2 - 1 and h == 3))
        P.op("act", lambda e: e.activation(hv(RA), hv(pb[7]), AF.Copy), reads=[pb[7]], writes=[hRA])
        P.op("dve", lambda e: e.tensor_tensor(hv(Nm), hv(RA), bc(self.C("id64")[:nt, :cs]), ALU.add), reads=[hRA, self.cst], writes=[hNm])
        if self.o.get("Cstop") == 4:
            break
        (Pc, hPc), (Rc, hRc), (Pn, hPn), (Rn, hRn) = (PA, hPA), (RA, hRA), (PB, hPB), (RB, hRB)
        for lev in range(1, nlev + 1):
            lastl = lev == nlev
            for c, h in CH:
                pc = 64 * c
                sl = (slice(pc, pc + cs), slice(h * 64, h * 64 + cs))
                P.op("pe", lambda e, sl=sl, Rc=Rc, Pc=Pc: e.matmul(pb[2][sl], Rc[sl], Pc[sl], start=True, stop=True),
                     reads=[hRc, hPc], writes=[pb[2]], sig=(c == nc2 - 1 and h == 3))
            if not lastl:
                for c, h in CH:
                    pc = 64 * c
                    sl = (slice(pc, pc + cs), slice(h * 64, h * 64 + cs))
                    P.op("pe", lambda e, sl=sl, Rc=Rc, Pc=Pc: e.matmul(pb[3][sl], Pc[sl], Rc[sl], start=True, stop=True),
                         reads=[hRc, hPc], writes=[pb[3]], sig=(c == nc2 - 1 and h == 3))
            P.op("act", lambda e, Pn=Pn: e.activation(hv(Pn), hv(pb[2]), AF.Copy), reads=[pb[2]], writes=[hPn])
            if not lastl:
                P.op("act", lambda e, Rn=Rn: e.activation(hv(Rn), hv(pb[3]), AF.Copy), reads=[pb[3]], writes=[hRn])
            for c, h in CH:
                pc = 64 * c
                sl = (slice(pc, pc + cs), slice(h * 64, h * 64 + cs))
                P.op("pe", lambda e, sl=sl, Pn=Pn: e.matmul(pb[5][sl], Pn[sl], Nm[sl], start=True, stop=True),
                     reads=[hPn, hNm], writes=[pb[5]], sig=(c == nc2 - 1 and h == 3))
            P.op("dve", lambda e: e.tensor_tensor(hv(Nm), hv(Nm), hv(pb[5]), ALU.add), reads=[hNm, pb[5]], writes=[hNm])
            (Pc, hPc), (Rc, hRc), (Pn, hPn), (Rn, hRn) = (Pn, hPn), (Rn, hRn), (Pc, hPc), (Rc, hRc)
        if self.o.get("Cstop") == 5:
            break
        if ti == 0 and blk.idx == 0 and l == 0 and blk.kind == "P":
            self.dump("C_N", Nm, [128, 256], [hNm])
            self.dump("C_qk", eDT, [128, 256], [heDT])
        for h in range(4):
            hs = slice(h * 64, (h + 1) * 64)
            P.op("dve", lambda e, h=h, hs=hs: e.tensor_scalar(bv[:nt, hs], vTM[:nt, hs], bet[:nt, h:h + 1], None, ALU.mult),
                 reads=[hvTM, hbet], writes=[hbv])
            P.op("dve", lambda e, h=h, hs=hs: e.tensor_scalar(kbg[:nt, hs], kTM[:nt, hs], begp[:nt, h:h + 1], None, ALU.mult),
                 reads=[hkTM, hbegp], writes=[hkbg])
            P.op("dve", lambda e, h=h, hs=hs: e.tensor_scalar(kd[:nt, hs], kTM[:nt, hs], kds[:nt, h:h + 1], None, ALU.mult),
                 reads=[hkTM, hkds], writes=[hkd])
        for c in range(nc2):
            pc = 64 * c
            P.op("pool", lambda e, c=c, pc=pc: e.tensor_copy(kbgm[pc:pc + cs, c, :], kbg[pc:pc + cs, :]), reads=[hkbg], writes=[kbgm])
            P.op("pool", lambda e, c=c, pc=pc: e.tensor_copy(kdm[pc:pc + cs, c, :], kd[pc:pc + cs, :]), reads=[hkd], writes=[kdm])
        for c, h in CH:
            pc = 64 * c
            P.op("pe", lambda e, pc=pc, h=h: e.matmul(pb[6][pc:pc + cs, h * 64:(h + 1) * 64], Nm[pc:pc + cs, h * 64:h * 64 + cs],
                                                      bv[pc:pc + cs, h * 64:(h + 1) * 64], start=True, stop=True),
                 reads=[hNm, hbv], writes=[pb[6]], sig=(c == nc2 - 1 and h == 3))
        P.op("act", lambda e: e.activation(un[:nt, :], pb[6][:nt, 0:256], AF.Copy), reads=[pb[6]], writes=[hun])
        for c, h in CH:
            pc, r0, pr = 64 * c, 64 * (h % 2), h // 2
            P.op("pe", lambda e, pc=pc, r0=r0, pr=pr, h=h, c=c: e.matmul(
                pb[7][r0:r0 + 64, pr * 128 + 64 * c:pr * 128 + 64 * c + cs], kbgm[:, c, h * 64:(h + 1) * 64],
                Nm[:, h * 64:h * 64 + cs], start=True, stop=True),
                reads=[hNm, kbgm], writes=[pb[7]], sig=(c == nc2 - 1 and h == 3))
        P.op("act", lambda e: e.activation(wT[:, :], pb[7][:, 0:256], AF.Copy), reads=[pb[7]], writes=[hwT])
        if self.o.get("Cstop") == 6:
            break
        for c in range(nc2):
            pc = 64 * c
            rows = slice(pc, pc + cs)
            cc = c0 + 64 * c
            for h in range(4):
                r0, pr = 64 * (h % 2), h // 2
                P.op("pe", lambda e, h=h, r0=r0, pr=pr: e.matmul(
                    pb[0][rows, h * 64:(h + 1) * 64], wT[:, pr * 128 + pc:pr * 128 + pc + cs], SCm[:, h, :],
                    start=True, stop=True), reads=[hwT, SCm], writes=[pb[0]], sig=(h == 3))
            P.op("dve", lambda e: e.tensor_tensor(un[rows, :], un[rows, :], pb[0][rows, 0:256], ALU.subtract),
                 reads=[hun, pb[0]], writes=[hun])
            for h in range(4):
                r0, pr = 64 * (h % 2), h // 2
                P.op("pe", lambda e, h=h, r0=r0, pr=pr: e.matmul(
                    pb[2][rows, h * 64:(h + 1) * 64], XC[0][:, pr, cc:cc + cs], SCm[:, h, :],
                    start=True, stop=True), reads=[Sh[0], SCm], writes=[pb[2]], sig=(h == 3))
            for h in range(4):
                P.op("act", lambda e, h=h: e.activation(o1s[rows, h * 64:(h + 1) * 64], pb[2][rows, h * 64:(h + 1) * 64], AF.Copy,
                                                        scale=eg[rows, h:h + 1]), reads=[pb[2], heg], writes=[ho1s])
            for h in range(4):
                P.op("pe", lambda e, h=h: e.matmul(pb[3][rows, h * 64:(h + 1) * 64], eDT[rows, h * 64:h * 64 + cs],
                                                   un[rows, h * 64:(h + 1) * 64], start=True, stop=True),
                     reads=[heDT, hun], writes=[pb[3]], sig=(h == 3))
            P.op("dve", lambda e: e.tensor_tensor(oTM[rows, :], o1s[rows, :], pb[3][rows, 0:256], ALU.add),
                 reads=[ho1s, pb[3]], writes=[hoTM])
            for h in range(4):
                r0, pr = 64 * (h % 2), h // 2
                P.op("pe", lambda e, h=h, r0=r0, pr=pr: e.matmul(
                    pb[5][r0:r0 + 64, pr * 64:(pr + 1) * 64], kdm[:, c, h * 64:(h + 1) * 64], un[:, h * 64:(h + 1) * 64],
                    start=True, stop=True), reads=[kdm, hun], writes=[pb[5]], sig=(h == 3))
            for pr in range(2):
                P.op("dve", lambda e, pr=pr, c=c: e.scalar_tensor_tensor(
                    SCw[:, pr, :], SCw[:, pr, :], egl[:, 2 * pr + c:2 * pr + c + 1], pb[5][:, pr * 64:(pr + 1) * 64], ALU.mult, ALU.add),
                    reads=[SCw, hegl, pb[5]], writes=[SCw])
            if c < nc2 - 1:
                mask_state()
        if self.o.get("Cstop") == 7:
            break
        for pr in range(2):
            P.op("pe", lambda e, pr=pr: e.transpose(pb[6][:, pr * 128:pr * 128 + nt], oTM[:nt, pr * 128:(pr + 1) * 128], ident[:nt, :nt]),
                 reads=[hoTM, self.cst], writes=[pb[6]])
            P.op("act", lambda e, pr=pr: e.activation(OC[:, pr, c0:c0 + nt], pb[6][:, pr * 128:pr * 128 + nt], AF.Copy),
                 reads=[pb[6]], writes=[Sh[4]])
        last_of_seq = (blk.kind == "S") or (blk.last and ti == len(blk.tiles) - 1)
        if last_of_seq:
            dst = self.od["pc"].ap()[l] if blk.kind == "P" else self.od["sco"].ap()[l, seq]
            P.dma("act", dst, SCw[:, :, :], reads=[SCw], key="sc_out", out=True)
        elif ti == len(blk.tiles) - 1:
            P.op("dve", lambda e: e.tensor_copy(SCp[:, :, :], SCw[:, :, :]), reads=[SCw], writes=[SCp])
    if blk.idx == 0 and l == 0 and blk.kind == "P":
        self.dump("C_o", OC, [128, 2, N], [Sh[4]])
    self.head_norm_gate(OC, Sh[4], SGz, Sh[3], self.V("cng%d" % l), 4, N, gcol=True)


Builder.mixer_C = _mixer_C


_OPTS = dict(mixers="ABCD")


def kernel(**inputs):
    n_cores = 8
    nc = bass.Bass("TRN2", target_bir_lowering=False)
    Builder(nc, dict(_OPTS)).run()
    sh = host_shared(inputs)
    in_maps = []
    for c in range(n_cores):
        m = host_core(inputs, c)
        m.update(sh)
        in_maps.append(m)
    res = run_bass_kernel_spmd(nc, in_maps, core_ids=list(range(n_cores)))
    return host_gather(res.results)
```

```python
import numpy as np
from contextlib import ExitStack
import concourse.bass as bass
import concourse.mybir as mybir
from concourse.bass_utils import run_bass_kernel_spmd

F32 = mybir.dt.float32
BF16 = mybir.dt.bfloat16
I32 = mybir.dt.int32
AF = mybir.ActivationFunctionType
ALU = mybir.AluOpType
AX = mybir.AxisListType


def _hkey(h):
    if isinstance(h, (tuple, str, int)):
        return h
    n = getattr(h, "name", None)
    if n is not None:
        return ("t", n)
    return ("id", id(h))


class Prog:
    ENG = ("pe", "act", "dve", "pool", "sp")

    def __init__(self, nc):
        self.nc = nc
        self.es = ExitStack()
        self.eng = {"pe": nc.tensor, "act": nc.scalar, "dve": nc.vector,
                    "pool": nc.gpsimd, "sp": nc.sync}
        self.sem = {e: self.es.enter_context(nc.semaphore("s_" + e)) for e in ("pe", "act", "dve", "pool")}
        self.cnt = {e: 0 for e in self.sem}
        self.clock = {e: {} for e in self.ENG}
        self.last_w = {}
        self.readers = {}
        self.tok_clock = {}
        self.dsem = {}
        self.dcnt = {}
        self.out_keys = set()
        self.nwait = 0
        self.pe_pending = False
        self.psum_keys = set()
        self.nops = 0

    def sbuf(self, name, shape, dtype):
        return self.es.enter_context(self.nc.sbuf_tensor("sb_" + name, list(shape), dtype))

    def psum(self, name, shape, dtype):
        t = self.es.enter_context(self.nc.psum_tensor("ps_" + name, list(shape), dtype))
        self.psum_keys.add(_hkey(t))
        return t

    def _covered(self, clk, tok):
        return clk.get(tok[0], 0) >= tok[1]

    def _merge(self, clk, other):
        for k, v in other.items():
            if clk.get(k, 0) < v:
                clk[k] = v

    def _deps(self, reads, writes, ename=None):
        deps = []
        for h in list(reads) + list(writes):
            t = self.last_w.get(_hkey(h))
            if t is not None:
                deps.append(t)
        for h in reads:
            k = _hkey(h)
            if k in self.psum_keys:
                deps.extend(t for t in self.readers.get(k, ()) if t[0] != ename)
        for h in writes:
            deps.extend(self.readers.get(_hkey(h), ()))
        return deps

    def _wait(self, ename, deps):
        e = self.eng[ename]
        clk = self.clock[ename]
        pend = []
        best = {}
        for t in deps:
            if ename == "pe" and t[0] == "pe":
                continue
            if self._covered(clk, t):
                continue
            if best.get(t[0], 0) < t[1]:
                best[t[0]] = t[1]
        for k, v in best.items():
            if clk.get(k, 0) >= v:
                continue
            s = self.sem[k] if k in self.sem else self.dsem[k]
            if k == "pe" and v > self.cnt["pe"]:
                raise RuntimeError("wait on un-signalled PE op (mark the producer sig=True)")
            pend.append((s, v))
            self.nwait += 1
            self._merge(clk, self.tok_clock[(k, v)])
        for s, v in pend[:-1]:
            e.wait_ge(s, v)
            if len(pend) > 2:
                e.nop()
        return pend[-1] if pend else None

    def _commit(self, tok, ename, reads, writes):
        c = dict(self.clock[ename])
        c[tok[0]] = max(c.get(tok[0], 0), tok[1])
        self.tok_clock[tok] = c
        for h in writes:
            k = _hkey(h)
            self.last_w[k] = tok
            self.readers[k] = []
        for h in reads:
            self.readers.setdefault(_hkey(h), []).append(tok)

    def op(self, ename, fn, reads=(), writes=(), sig=True):
        lastw = self._wait(ename, self._deps(reads, writes, ename))
        ins = fn(self.eng[ename])
        if lastw is not None:
            ins._wait_ge(lastw[0], lastw[1])
        self.nops += 1
        if ename != "pe":
            sig = True
        if sig:
            self.cnt[ename] += 1
            ins.then_inc(self.sem[ename], 1)
            tok = (ename, self.cnt[ename])
            if ename == "pe":
                self.pe_pending = False
        else:
            tok = (ename, self.cnt[ename] + 1)
            self.pe_pending = True
        self._commit(tok, ename, reads, writes)
        return ins

    def dma(self, qname, out_ap, in_ap, reads=(), writes=(), key=None, out=False, **kw):
        if key is None:
            hs = list(writes) if writes else list(reads)
            key = ("dma",) + tuple(_hkey(h) for h in hs[:1])
        key = ("d", key)
        if key not in self.dsem:
            self.dsem[key] = self.es.enter_context(self.nc.semaphore("d%d" % len(self.dsem)))
            self.dcnt[key] = 0
        lastw = self._wait(qname, self._deps(reads, writes, qname))
        ins = self.eng[qname].dma_start(out=out_ap, in_=in_ap, **kw)
        if lastw is not None:
            ins._wait_ge(lastw[0], lastw[1])
        self.dcnt[key] += 16
        ins.then_inc(self.dsem[key], 16)
        tok = (key, self.dcnt[key])
        self._commit(tok, qname, reads, writes)
        if out:
            self.out_keys.add(key)
        return ins

    def finish(self):
        sp = self.eng["sp"]
        for key in self.out_keys:
            sp.wait_ge(self.dsem[key], self.dcnt[key])
        for e in ("pe", "act", "dve", "pool"):
            if self.cnt[e]:
                sp.wait_ge(self.sem[e], self.cnt[e])
        self.es.close()


D = 1024
DIN = 3336
DFF = 2816
NFF = 22
NTOK = 2048
NBLK = 512
NS = 4
LS = 8
EPS = 1e-6
COLS = dict(aq=0, af=256, ai=512, ag=768, bq=1024, bk=1280, bv=1536, cq=1792, ck=2048,
            cv=2304, cz=2560, cb=2816, ca=2820, dx=2824, dg=3080)
TABL = 2304
ZW = 2560

VEC = {}
_o = 0
def _v(name, n):
    global _o
    VEC[name] = (_o, n)
    _o += n
for _l in range(2):
    for _n, _c in (("n1g", 8), ("n2g", 8), ("png", 8), ("ang", 2), ("bng", 2), ("cng", 1),
                   ("ccw", 24), ("dcw", 8), ("dcb", 2), ("dba", 2), ("dbx", 2), ("dlam", 2),
                   ("dng", 2), ("calog", 4), ("cdtb", 4)):
        _v("%s%d" % (_n, _l), _c)
_v("fng", 8)
_v("alb0", 2)
_v("alb1", 2)
NV = _o

CST = {}
_o = 0
def _c(name, n):
    global _o
    CST[name] = (_o, n)
    _o += n
_c("ident", 128)
_c("onesbd", 128)
_c("mask16T", 128)
_c("blk16", 8)
_c("reset16", 512)
_c("resetS", 32)
_c("uloc", 64)
_c("lsloc", 64)
_c("trilS", 64)
_c("triuI", 64)
_c("id64", 64)
_c("u64", 128)
_c("l64s", 128)
_c("sel64", 128)
_c("selc", 2)
_c("sel8", 8)
NC_ = _o


def make_consts():
    c = np.zeros((128, NC_), np.float32)
    p = np.arange(128)[:, None]
    def put(name, arr):
        o, n = CST[name]
        c[:, o:o + n] = arr
    t = np.arange(128)[None, :]
    put("ident", (p == t))
    put("onesbd", (p // 64 == t // 64))
    put("mask16T", (p // 16 == t // 16) & (p <= t))
    put("blk16", (p // 16 == np.arange(8)[None, :]))
    put("reset16", np.broadcast_to((np.arange(512)[None, :] % 16 != 0), (128, 512)))
    put("resetS", np.broadcast_to((np.arange(32)[None, :] % 8 != 0), (128, 32)))
    pl = p % 64
    s = np.arange(64)[None, :]
    put("uloc", pl <= s)
    put("lsloc", pl > s)
    put("trilS", pl > s)
    put("triuI", pl <= s)
    put("id64", pl == s)
    put("u64", (p // 64 == t // 64) & (p <= t))
    put("l64s", (p // 64 == t // 64) & (p > t))
    put("sel64", p == (t // 64) * 64 + 63)
    put("selc", p == np.arange(2)[None, :] * 64 + 63)
    put("sel8", np.broadcast_to(p == 7, (128, 8)))
    return c


def make_disttab():
    import math
    M = np.zeros((32, TABL), np.float32)
    for u in range(TABL):
        d = u - 127
        if d < 0 or d > 2048:
            continue
        mult = 0
        if d <= 128:
            mult += 1
        if d <= 512 and d % 4 == 0:
            mult += 1
        if d <= 2048 and d % 16 == 0:
            mult += 1
        if mult == 0:
            continue
        if d < 16:
            b = d
        else:
            v = np.float32(np.log(np.float32(max(d, 1)) / np.float32(16.0))) / np.float32(math.log(2048 / 16)) * np.float32(16)
            b = min(16 + int(np.float32(v)), 31)
        M[b, u] = mult
    return M


def fm(v):
    v = np.asarray(v, np.float32)
    return np.ascontiguousarray(v.reshape(-1, 128).T)


def host_shared(inp):
    sh = {}
    f32 = lambda a: np.ascontiguousarray(np.asarray(a, np.float32))
    w_in = f32(inp["w_in"])
    sh["w_in_t"] = f32(w_in.reshape(2, 8, 128, DIN).transpose(0, 2, 1, 3))
    sh["w_out_t"] = f32(f32(inp["w_out"]).reshape(2, 8, 128, D).transpose(0, 2, 1, 3))
    wfi = f32(inp["w_ffn_in"]).reshape(2, 8, 128, 2, NFF, 128)
    sh["w_ffi_t"] = f32(wfi.transpose(0, 4, 2, 1, 3, 5).reshape(2, NFF, 128, 8, 256))
    sh["w_ffo_t"] = f32(f32(inp["w_ffn_out"]).reshape(2, NFF, 128, D).transpose(0, 2, 1, 3))
    sh["w_ple_t"] = f32(f32(inp["w_ple"]).reshape(2, 2, 128, D).transpose(0, 2, 1, 3))
    sh["w_gate_t"] = f32(f32(inp["w_ple_gate"]).reshape(2, 8, 128, D).transpose(0, 2, 1, 3))
    vecs = np.zeros((128, NV), np.float32)
    def put(name, arr):
        o, n = VEC[name]
        vecs[:, o:o + n] = arr
    for l in range(2):
        put("n1g%d" % l, fm(inp["norm1_g"][l]))
        put("n2g%d" % l, fm(inp["norm2_g"][l]))
        put("png%d" % l, fm(inp["ple_norm_g"][l]))
        put("ang%d" % l, fm(inp["a_norm_g"][l]))
        put("bng%d" % l, fm(inp["b_norm_g"][l]))
        put("cng%d" % l, np.tile(np.asarray(inp["c_norm_g"][l], np.float32), 2)[:, None])
        ccw = np.asarray(inp["c_conv_w"][l], np.float32)
        put("ccw%d" % l, ccw.reshape(4, 6, 128).transpose(2, 1, 0).reshape(128, 24))
        dcw = np.asarray(inp["d_conv_w"][l], np.float32)
        put("dcw%d" % l, dcw.reshape(4, 2, 128).transpose(2, 1, 0).reshape(128, 8))
        put("dcb%d" % l, fm(inp["d_conv_b"][l]))
        put("dba%d" % l, fm(inp["d_ba"][l]))
        put("dbx%d" % l, fm(inp["d_bx"][l]))
        put("dlam%d" % l, fm(inp["d_lambda"][l]))
        put("dng%d" % l, fm(inp["d_norm_g"][l]))
        put("calog%d" % l, np.broadcast_to(np.asarray(inp["c_a_log"][l], np.float32)[None, :], (128, 4)))
        put("cdtb%d" % l, np.broadcast_to(np.asarray(inp["c_dt_bias"][l], np.float32)[None, :], (128, 4)))
    put("fng", fm(inp["final_norm_g"]))
    put("alb0", fm(inp["a_lb"][0]))
    put("alb1", fm(inp["a_lb"][1]))
    sh["vecs"] = vecs
    dgw = np.zeros((128, 2, 2, 2, 128), np.float32)
    for l in range(2):
        for wi, nm in enumerate(("d_wa", "d_wx")):
            w = np.asarray(inp[nm][l], np.float32)
            for h in range(4):
                r = (h % 2) * 64
                dgw[r:r + 64, l, wi, h // 2, r:r + 64] = w[h]
    sh["dgw"] = dgw.reshape(128, 2 * 2 * 2 * 128)
    sh["relb"] = f32(inp["rel_bias"])
    sh["cst"] = make_consts()
    sh["cM"] = make_disttab()
    return sh


def host_core(inp, c):
    f32 = lambda a: np.ascontiguousarray(np.asarray(a, np.float32))
    m = {}
    x = f32(inp["x_prompt"][c])
    m["xT"] = f32(x.reshape(NTOK, 8, 128).transpose(2, 1, 0))
    p = f32(inp["p_prompt"][:, c])
    m["pT"] = f32(p.reshape(2, NTOK, 2, 128).transpose(0, 3, 2, 1))
    sl = slice(NS * c, NS * c + NS)
    xs = f32(inp["x_sample"][sl]).reshape(NS * LS, 8, 128)
    m["xsT"] = f32(xs.transpose(2, 1, 0))
    ps = f32(inp["p_sample"][:, sl]).reshape(2, NS * LS, 2, 128)
    m["psT"] = f32(ps.transpose(0, 3, 2, 1))
    ck = f32(inp["cache_b_k"][:, sl])
    ck = ck.reshape(2, NS, 2048, 2, 2, 64)
    m["kcT"] = f32(ck.transpose(0, 1, 4, 5, 3, 2).reshape(2, NS, 128, 2, 2048))
    cv = f32(inp["cache_b_v"][:, sl]).reshape(2, NS, 16, 128, 256)
    m["vc"] = f32(cv.transpose(0, 1, 3, 2, 4))
    def st(a):
        a = f32(a[:, sl]).reshape(2, NS, 2, 2, 64, 64)
        return f32(a.transpose(0, 1, 3, 4, 2, 5).reshape(2, NS, 128, 2, 64))
    m["sa"] = st(inp["state_a"])
    m["sc"] = st(inp["state_c"])
    scc = f32(inp["state_c_conv"][:, sl]).reshape(2, NS, 3, 6, 128)
    m["scc"] = f32(scc.transpose(0, 4, 3, 1, 2))
    sdc = f32(inp["state_d_conv"][:, sl]).reshape(2, NS, 3, 2, 128)
    m["sdc"] = f32(sdc.transpose(0, 4, 3, 1, 2))
    sdh = f32(inp["state_d_h"][:, sl]).reshape(2, NS, 2, 128)
    m["sdh"] = f32(sdh.transpose(0, 3, 2, 1))
    return m


OUT_SPECS = {
    "yT": [128, 8, NTOK], "ysT": [128, 8, NS * LS],
    "pbk": [2, NTOK, 256], "pbv": [2, NTOK, 256],
    "pa": [2, 128, 2, 64], "pc": [2, 128, 2, 64],
    "pcc": [2, 128, 6, 3], "pdh": [2, 128, 2], "pdc": [2, 128, 2, 3],
    "sbk": [2, NS * LS, 256], "sbv": [2, NS * LS, 256],
    "sao": [2, NS, 128, 2, 64], "sco": [2, NS, 128, 2, 64],
    "scco": [2, 128, 6, NS, 3], "sdho": [2, 128, 2, NS], "sdco": [2, 128, 2, NS, 3],
}


def host_gather(res):
    nco = len(res)
    def unst(a):
        a = a.reshape(2, 2, 64, 2, 64)
        return a.transpose(0, 3, 1, 2, 4).reshape(2, 4, 64, 64)
    y = np.stack([r["yT"].transpose(2, 1, 0).reshape(NTOK, D) for r in res])
    ys = np.concatenate([r["ysT"].transpose(2, 1, 0).reshape(NS, LS, D) for r in res])
    pbk = np.stack([r["pbk"].reshape(2, NTOK, 4, 64) for r in res], 1)
    pbv = np.stack([r["pbv"].reshape(2, NTOK, 4, 64) for r in res], 1)
    pa = np.stack([unst(r["pa"]) for r in res], 1)
    pc = np.stack([unst(r["pc"]) for r in res], 1)
    pcc = np.stack([r["pcc"].transpose(0, 3, 2, 1).reshape(2, 3, 768) for r in res], 1)
    pdh = np.stack([r["pdh"].transpose(0, 2, 1).reshape(2, 256) for r in res], 1)
    pdc = np.stack([r["pdc"].transpose(0, 3, 2, 1).reshape(2, 3, 256) for r in res], 1)
    sbk = np.concatenate([r["sbk"].reshape(2, NS, LS, 4, 64) for r in res], 1)
    sbv = np.concatenate([r["sbv"].reshape(2, NS, LS, 4, 64) for r in res], 1)
    sao = np.concatenate([np.stack([unst(r["sao"][:, s]) for s in range(NS)], 1) for r in res], 1)
    sco = np.concatenate([np.stack([unst(r["sco"][:, s]) for s in range(NS)], 1) for r in res], 1)
    scco = np.concatenate([r["scco"].transpose(0, 3, 4, 2, 1).reshape(2, NS, 3, 768) for r in res], 1)
    sdho = np.concatenate([r["sdho"].transpose(0, 3, 2, 1).reshape(2, NS, 256) for r in res], 1)
    sdco = np.concatenate([r["sdco"].transpose(0, 3, 4, 2, 1).reshape(2, NS, 3, 256) for r in res], 1)
    outs = (y, ys, pbk, pbv, pa, pc, pcc, pdh, pdc, sbk, sbv, sao, sco, scco, sdho, sdco)
    return tuple(np.ascontiguousarray(o, dtype=np.float32) for o in outs)


class Blk:
    def __init__(self, kind, idx):
        self.kind = kind
        self.idx = idx
        if kind == "P":
            self.N, self.nseq, self.L = NBLK, 1, NBLK
            self.tok0 = idx * NBLK
            self.tiles = [(i * 128, 128, 0, idx * 4 + i) for i in range(4)]
        else:
            self.N, self.nseq, self.L = NS * LS, NS, LS
            self.tok0 = 0
            self.tiles = [(s * LS, LS, s, 0) for s in range(NS)]
        self.first = (kind == "S") or idx == 0
        self.last = (kind == "S") or idx == NTOK // NBLK - 1


class Builder:
    def __init__(self, nc, opts=None):
        self.nc = nc
        self.o = dict(mixers="ABCD", dense=True, blocks=None, layers=2, dump=())
        if opts:
            self.o.update(opts)
        self.P = Prog(nc)
        self.dumps = {}
        self.decl()
        self.alloc()

    def decl(self):
        nc = self.nc
        I = lambda n, s: nc.dram_tensor(n, list(s), F32, kind="ExternalInput")
        self.d = {}
        for n, s in (("xT", [128, 8, NTOK]), ("pT", [2, 128, 2, NTOK]), ("xsT", [128, 8, NS * LS]),
                     ("psT", [2, 128, 2, NS * LS]), ("kcT", [2, NS, 128, 2, 2048]),
                     ("vc", [2, NS, 128, 16, 256]), ("sa", [2, NS, 128, 2, 64]), ("sc", [2, NS, 128, 2, 64]),
                     ("scc", [2, 128, 6, NS, 3]), ("sdc", [2, 128, 2, NS, 3]), ("sdh", [2, 128, 2, NS]),
                     ("w_in_t", [2, 128, 8, DIN]), ("w_out_t", [2, 128, 8, D]),
                     ("w_ffi_t", [2, NFF, 128, 8, 256]), ("w_ffo_t", [2, 128, NFF, D]),
                     ("w_ple_t", [2, 128, 2, D]), ("w_gate_t", [2, 128, 8, D]),
                     ("vecs", [128, NV]), ("dgw", [128, 1024]), ("relb", [32, 4]),
                     ("cst", [128, NC_]), ("cM", [32, TABL])):
            self.d[n] = I(n, s)
        self.od = {n: nc.dram_tensor(n, list(s), F32, kind="ExternalOutput") for n, s in OUT_SPECS.items()}
        self.zt = nc.dram_tensor("ztab", [4, 128, ZW], BF16, kind="Internal")
        self.wcache = nc.dram_tensor("wcache", [128, 225408], BF16, kind="Internal")
        self.wc_map = {}
        self.wc_off = 0

    def dump(self, name, ap, shape, reads):
        if name not in self.o["dump"] or name in self.dumps:
            return
        t = self.nc.dram_tensor("dbg_" + name, list(shape), ap.dtype, kind="ExternalOutput")
        self.dumps[name] = t
        self.P.dma("act", t.ap(), ap, reads=reads, key="dbg_" + name, out=True)

    def alloc(self):
        P = self.P
        self.cst = P.sbuf("cst", [128, NC_], F32)
        self.vecs = P.sbuf("vecs", [128, NV], F32)
        self.dgw = P.sbuf("dgw", [128, 512], F32)
        self.ones_bf = P.sbuf("ones_bf", [128, 128], BF16)
        self.drv = P.sbuf("drv", [128, 32], F32)
        self.E = P.sbuf("E", [128, 4, TABL], BF16)
        self.qT = P.sbuf("qT", [128, 4, NBLK], BF16)
        self.PT = [P.sbuf("PT%d" % i, [128, 512], BF16) for i in range(2)]
        self.Vnew = P.sbuf("Vnew", [128, NS, 256], BF16)
        self.hT = P.sbuf("hT", [128, 8, NBLK], F32)
        self.hnT = P.sbuf("hnT", [128, 8, NBLK], BF16)
        self.mixT = P.sbuf("mixT", [128, 8, NBLK], BF16)
        self.actT = P.sbuf("actT", [128, NFF // 2, NBLK], BF16)
        self.KT = [P.sbuf("KT%d" % l, [128, 2, NTOK + 128], BF16) for l in range(2)]
        self.Vh = [P.sbuf("Vh%d" % l, [128, 17, 256], BF16) for l in range(2)]
        self.wstg = [P.sbuf("wstg%d" % i, [128, 2048], F32) for i in range(2)]
        self.wbf = [P.sbuf("wbf%d" % i, [128, 2048], BF16) for i in range(3)]
        self.wple = P.sbuf("wple", [128, 2048], BF16)
        self.rt = P.sbuf("rt", [128, NBLK], F32)
        self.sqb = [P.sbuf("sqb%d" % i, [128, NBLK], BF16) for i in range(2)]
        self.S = [P.sbuf("S%d" % i, [128, 1024], F32) for i in range(10)]
        self.pb = [P.psum("pb%d" % i, [128, 512], F32) for i in range(8)]
        self.wi = 0
        self.wj = 0
        self.dtail = [P.sbuf("dtail%d" % l, [128, 2, 3], F32) for l in range(2)]
        self.dhst = [P.sbuf("dhst%d" % l, [128, 2, NS], F32) for l in range(2)]
        self.dext = P.sbuf("dext", [128, 2, NBLK + 3], F32)
        self.SAw = P.sbuf("SAw", [128, 2, 9, 64], F32)
        self.SAp = [P.sbuf("SAp%d" % l, [128, 2, 64], F32) for l in range(2)]
        self.SCp = [P.sbuf("SCp%d" % l, [128, 2, 64], F32) for l in range(2)]
        self.ctail = [P.sbuf("ctail%d" % l, [128, 6, 3], F32) for l in range(2)]
        self.negA = P.sbuf("negA", [128, 8], F32)
        self.SCw = P.sbuf("SCw", [128, 2, 64], F32)
        self.SCm = P.sbuf("SCm", [128, 4, 64], F32)
        self.kmt = P.sbuf("kmt", [128, 4, 128], F32)
        self.kbgm = P.sbuf("kbgm", [128, 2, 256], F32)
        self.kdm = P.sbuf("kdm", [128, 2, 256], F32)

    def V(self, name):
        o, n = VEC[name]
        return self.vecs[:, o:o + n]

    def C(self, name):
        o, n = CST[name]
        return self.cst[:, o:o + n]

    def wload(self, src_ap, shape, key):
        P = self.P
        n = int(np.prod(shape[1:]))
        bufs = self.wbf if getattr(self, "in_ple", False) else self.wbf + [self.wple]
        wb = bufs[self.wj % len(bufs)]
        self.wj += 1
        bv = wb[:, :n]
        if len(shape) == 3:
            bv = bv.rearrange("p (a b) -> p a b", a=shape[1])
        if key not in self.wc_map:
            off = self.wc_off
            self.wc_map[key] = off
            self.wc_off += n
            stg = self.wstg[self.wi % len(self.wstg)]
            self.wi += 1
            sv = stg[:, :n]
            if len(shape) == 3:
                sv = sv.rearrange("p (a b) -> p a b", a=shape[1])
            P.dma("sp", sv, src_ap, writes=[stg])
            P.op("act", lambda e: e.activation(wb[:, :n], stg[:, :n], AF.Copy), reads=[stg], writes=[wb])
            P.dma("act", self.wcache.ap()[:, off:off + n], wb[:, :n], reads=[wb], writes=[("wc", key)],
                  key=("wcw",) + tuple(_hkey(wb)))
        else:
            off = self.wc_map[key]
            P.dma("sp", wb[:, :n], self.wcache.ap()[:, off:off + n], reads=[("wc", key)], writes=[wb])
        return wb, bv

    def w_in_unit(self, l, c0, n):
        return self.wload(self.d["w_in_t"].ap()[l][:, :, c0:c0 + n], [128, 8, n], ("w_in", l, c0))

    def proj_fm(self, wb, wv, j0, m, out_ps, N, handle):
        P = self.P
        for k in range(8):
            P.op("pe", lambda e, k=k: e.matmul(out_ps[:m, :N], wv[:, k, j0:j0 + m], self.hnT[:, k, :N],
                                              start=(k == 0), stop=(k == 7)),
                 reads=[wb, self.hnT], writes=[handle], sig=(k == 7))

    def rmsnorm_fm(self, src, gv, dst, N, srch, dsth):
        P = self.P
        ps = self.pb[4]
        for k in range(8):
            sq = self.sqb[k % 2]
            P.op("act", lambda e, k=k, sq=sq: e.activation(sq[:, :N], src[:, k, :N], AF.Square),
                 reads=[srch], writes=[sq])
            P.op("pe", lambda e, k=k, sq=sq: e.matmul(ps[:, :N], self.ones_bf[:], sq[:, :N],
                                                      start=(k == 0), stop=(k == 7)),
                 reads=[sq, self.ones_bf], writes=[ps])
        rt = self.rt
        P.op("act", lambda e: e.activation(rt[:, :N], ps[:, :N], AF.Ln, bias=EPS, scale=1.0 / D),
             reads=[ps], writes=[rt])
        P.op("act", lambda e: e.activation(rt[:, :N], rt[:, :N], AF.Exp, scale=-0.5),
             reads=[rt], writes=[rt])
        for k in range(8):
            P.op("dve", lambda e, k=k: e.scalar_tensor_tensor(dst[:, k, :N], src[:, k, :N], gv[:, k:k + 1],
                                                             rt[:, :N], ALU.mult, ALU.mult),
                 reads=[srch, rt, self.vecs], writes=[dsth])

    def setup(self):
        P = self.P
        d = self.d
        P.dma("sp", self.cst[:], d["cst"].ap(), writes=[self.cst])
        P.dma("sp", self.vecs[:], d["vecs"].ap(), writes=[self.vecs])
        P.op("dve", lambda e: e.memset(self.ones_bf[:], 1.0), writes=[self.ones_bf])
        P.op("dve", lambda e: e.memset(self.qT[:, :, :], 0.0), writes=[self.qT])
        for i in range(5, 10):
            P.op("dve", lambda e, i=i: e.memset(self.S[i][:, :], 0.0), writes=[self.S[i]])
        drv = self.drv
        for l in range(2):
            t = self.S[0]
            lam = self.V("dlam%d" % l)
            P.op("act", lambda e: e.activation(t[:, 0:2], lam, AF.Exp, scale=-1.0), reads=[self.vecs], writes=[t])
            P.op("act", lambda e: e.activation(t[:, 0:2], t[:, 0:2], AF.Ln, bias=1.0), reads=[t], writes=[t])
            P.op("dve", lambda e, l=l: e.tensor_scalar(drv[:, 4 * l:4 * l + 2], t[:, 0:2], -8.0, None, ALU.mult),
                 reads=[t], writes=[drv])
            P.op("dve", lambda e, l=l: e.tensor_scalar(drv[:, 4 * l + 2:4 * l + 4], t[:, 0:2], -16.0, None, ALU.mult),
                 reads=[t], writes=[drv])
        P.op("dve", lambda e: e.memset(drv[:, 8:10], 0.0), writes=[drv])
        P.op("dve", lambda e: e.memset(drv[:, 10:12], 1.0), writes=[drv])
        P.op("dve", lambda e: e.memset(drv[:, 12:14], -1.0), writes=[drv])
        t = self.S[0]
        P.op("dve", lambda e: e.tensor_tensor(t[:, 8:10], self.V("alb1"), self.V("alb0"), ALU.subtract),
             reads=[self.vecs], writes=[t])
        P.op("act", lambda e: e.activation(drv[:, 14:16], t[:, 8:10], AF.Sigmoid), reads=[t], writes=[drv])
        P.op("dve", lambda e: e.tensor_scalar(drv[:, 16:18], drv[:, 14:16], -1.0, 1.0, ALU.mult, ALU.add),
             reads=[drv], writes=[drv])
        P.op("dve", lambda e: e.tensor_scalar(drv[:, 18:20], drv[:, 14:16], 1.0, -1.0, ALU.mult, ALU.add),
             reads=[drv], writes=[drv])
        for l in range(2):
            P.op("act", lambda e, l=l: e.activation(self.negA[:, 4 * l:4 * l + 4], self.V("calog%d" % l), AF.Exp), reads=[self.vecs], writes=[self.negA])
        P.op("dve", lambda e: e.tensor_scalar(self.negA[:, :], self.negA[:, :], -1.0, None, ALU.mult), reads=[self.negA], writes=[self.negA])
        if "B" in self.o["mixers"]:
            self.setup_E()

    def lbv(self, l):
        o = 8 + 6 * l
        return self.drv[:, o:o + 2], self.drv[:, o + 2:o + 4], self.drv[:, o + 4:o + 6]

    def setup_E(self):
        P = self.P
        relb = self.S[1]
        lh = self.S[2]
        cm = self.S[0]
        P.dma("sp", relb[:32, 0:4], self.d["relb"].ap(), writes=[relb])
        P.op("act", lambda e: e.activation(relb[:32, 4:8], relb[:32, 0:4], AF.Exp), reads=[relb], writes=[relb])
        for h in range(4):
            P.op("dve", lambda e, h=h: e.tensor_copy(lh[:32, h * 128:(h + 1) * 128],
                                                     relb[:32, 4 + h:5 + h].to_broadcast([32, 128])),
                 reads=[relb], writes=[lh])
        for cb in range(5):
            u0 = cb * 512
            n = min(512, TABL - u0)
            P.dma("sp", cm[:32, :n], self.d["cM"].ap()[:, u0:u0 + n], writes=[cm])
            for h in range(4):
                ps = self.pb[h]
                P.op("pe", lambda e, h=h, ps=ps: e.matmul(ps[:, :n], lh[:32, h * 128:(h + 1) * 128], cm[:32, :n],
                                                          start=True, stop=True),
                     reads=[lh, cm], writes=[ps])
                P.op("act", lambda e, h=h, ps=ps: e.activation(self.E[:, h, u0:u0 + n], ps[:, :n], AF.Copy),
                     reads=[ps], writes=[("E", h)])
        for h in range(4):
            dst = bass.AP(self.zt, h * 128 * ZW, [[ZW + 1, 128], [1, TABL]])
            P.dma("sp", dst, self.E[:, h, :TABL], reads=[("E", h)], writes=[("zt", h)], key="ztw%d" % h)
            src = bass.AP(self.zt, h * 128 * ZW + 127, [[ZW, 128], [1, 17 * 128]])
            P.dma("sp", self.E[:, h, :17 * 128], src, reads=[("zt", h)], writes=[("E", h)], key="ztr%d" % h)

    def run(self):
        P = self.P
        self.setup()
        blocks = [Blk("P", i) for i in range(NTOK // NBLK)] + [Blk("S", 0)]
        if self.o["blocks"] is not None:
            blocks = [blocks[i] for i in self.o["blocks"]]
        for blk in blocks:
            N = blk.N
            if blk.kind == "P":
                src = self.d["xT"].ap()[:, :, blk.tok0:blk.tok0 + N]
            else:
                src = self.d["xsT"].ap()
            P.dma("sp", self.hT[:, :, :N], src, writes=[("hT", f) for f in range(8)])
            for l in range(self.o["layers"]):
                self.layer(blk, l)
            self.final_norm(blk)
        P.finish()

    def hh(self):
        return [("hT", f) for f in range(8)]

    def layer(self, blk, l):
        N = blk.N
        self.rmsnorm_fm(self.hT, self.V("n1g%d" % l), self.hnT, N, self.hh(), [self.hnT])
        mx = self.o["mixers"]
        for k in range(8):
            if "ABCD"[k // 2] not in mx:
                self.P.op("dve", lambda e, k=k: e.memset(self.mixT[:, k, :N], 0.0), writes=[("mixT", k)])
        if "D" in mx:
            self.mixer_D(blk, l)
        if "B" in mx:
            self.mixer_B(blk, l)
        if "A" in mx:
            self.mixer_A(blk, l)
        if "C" in mx:
            self.mixer_C(blk, l)
        if self.o.get("mixin") and blk.idx == 0 and l == 0:
            md = self.nc.dram_tensor("mixin_dbg", [128, 8, N], F32, kind="ExternalInput")
            for k2 in range(4):
                t = self.S[k2]
                self.P.dma("sp", t[:, :2 * N].rearrange("p (c n) -> p c n", c=2), md.ap()[:, 2 * k2:2 * k2 + 2, :], writes=[t])
                self.P.op("dve", lambda e, k2=k2, t=t: e.tensor_copy(self.mixT[:, 2 * k2:2 * k2 + 2, :N], t[:, :2 * N].rearrange("p (c n) -> p c n", c=2)),
                          reads=[t], writes=[("mixT", 2 * k2), ("mixT", 2 * k2 + 1)])
        if blk.idx == 0 and l == 0:
            self.dump("mixT_" + blk.kind, self.mixT[:, :, :N], [128, 8, N], [("mixT", k) for k in range(8)])
        if self.o["dense"]:
            self.dense(blk, l)

    def rmsnorm_fm(self, src, gv, dst, N, srch, dsth, dst_fn=None):
        P = self.P
        ps = self.pb[4]
        sqa = self.actT
        ah_ = [("actT", k) for k in range(8)]
        P.op("act", lambda e: e.activation(sqa[:, 0:4, :N], src[:, 0:4, :N], AF.Square),
             reads=srch[0:4], writes=ah_[0:4])
        P.op("dve", lambda e: e.tensor_tensor(sqa[:, 4:8, :N], src[:, 4:8, :N], src[:, 4:8, :N], ALU.mult),
             reads=srch[4:8], writes=ah_[4:8])
        for k in range(8):
            P.op("pe", lambda e, k=k: e.matmul(ps[:, :N], self.ones_bf[:], sqa[:, k, :N],
                                               start=(k == 0), stop=(k == 7)),
                 reads=[ah_[k], self.ones_bf], writes=[ps])
        rt = self.rt
        P.op("act", lambda e: e.activation(rt[:, :N], ps[:, :N], AF.Ln, bias=EPS, scale=1.0 / D),
             reads=[ps], writes=[rt])
        P.op("act", lambda e: e.activation(rt[:, :N], rt[:, :N], AF.Exp, scale=-0.5),
             reads=[rt], writes=[rt])
        for k in range(8):
            if dst_fn is None:
                o, oh = dst[:, k, :N], dsth
            else:
                o, oh = dst_fn(k)
            P.op("dve", lambda e, k=k, o=o: e.scalar_tensor_tensor(o, src[:, k, :N], gv[:, k:k + 1],
                                                                  rt[:, :N], ALU.mult, ALU.mult),
                 reads=[srch[k], rt, self.vecs], writes=oh)

    def final_norm(self, blk):
        P = self.P
        N = blk.N
        def dst_fn(k):
            t = self.S[k // 2]
            return t[:, (k % 2) * N:(k % 2 + 1) * N], [t]
        self.rmsnorm_fm(self.hT, self.V("fng"), None, N, self.hh(), None, dst_fn=dst_fn)
        for k2 in range(4):
            t = self.S[k2]
            if blk.kind == "P":
                dst = self.od["yT"].ap()[:, 2 * k2:2 * k2 + 2, blk.tok0:blk.tok0 + N]
            else:
                dst = self.od["ysT"].ap()[:, 2 * k2:2 * k2 + 2, :]
            P.dma("act", dst, t[:, :2 * N].rearrange("p (c n) -> p c n", c=2), reads=[t], key="y%d" % k2, out=True)

    def dense(self, blk, l):
        P = self.P
        N = blk.N
        d = self.d
        hT, hnT = self.hT, self.hnT
        mixh = [("mixT", k) for k in range(8)]
        for u in range(4):
            wb, wv = self.wload(d["w_out_t"].ap()[l][:, :, 256 * u:256 * u + 256], [128, 8, 256], ("w_out", l, u))
            for fc in range(2):
                f = 2 * u + fc
                ps = self.pb[f % 2]
                for k in range(8):
                    P.op("pe", lambda e, k=k, ps=ps, wv=wv, fc=fc: e.matmul(
                        ps[:, :N], wv[:, k, fc * 128:(fc + 1) * 128], self.mixT[:, k, :N],
                        start=(k == 0), stop=(k == 7)), reads=[wb, mixh[k]], writes=[ps], sig=(k == 7))
                P.op("dve", lambda e, f=f, ps=ps: e.tensor_tensor(hT[:, f, :N], hT[:, f, :N], ps[:, :N], ALU.add),
                     reads=[ps, ("hT", f)], writes=[("hT", f)])
        if l == 0:
            self.dump("h1_" + blk.kind + str(blk.idx), hT[:, :, :N], [128, 8, N], self.hh())
        self.rmsnorm_fm(hT, self.V("n2g%d" % l), hnT, N, self.hh(), [hnT])
        for hf in range(2):
            for cc in range(NFF // 2):
                c = hf * (NFF // 2) + cc
                wb, wv = self.wload(d["w_ffi_t"].ap()[l, c], [128, 8, 256], ("w_ffi", l, c))
                pg, pu = self.pb[2 * (c % 2)], self.pb[2 * (c % 2) + 1]
                for half, ps in ((0, pg), (1, pu)):
                    for k in range(8):
                        P.op("pe", lambda e, k=k, ps=ps, wv=wv, half=half: e.matmul(
                            ps[:, :N], wv[:, k, half * 128:(half + 1) * 128], hnT[:, k, :N],
                            start=(k == 0), stop=(k == 7)), reads=[wb, hnT], writes=[ps], sig=(k == 7))
                tmp = self.S[8 + c % 2]
                P.op("act", lambda e, pg=pg, tmp=tmp: e.activation(tmp[:, :N], pg[:, :N], AF.Silu),
                     reads=[pg], writes=[tmp])
                P.op("dve", lambda e, cc=cc, pu=pu, tmp=tmp: e.tensor_tensor(self.actT[:, cc, :N], tmp[:, :N], pu[:, :N], ALU.mult),
                     reads=[tmp, pu], writes=[("actT", cc)])
            for f in range(8):
                ps = self.pb[5 + f % 2]
                wb, wv = self.wload(d["w_ffo_t"].ap()[l][:, 11 * hf:11 * hf + 11, 128 * f:128 * f + 128], [128, 11, 128], ("w_ffo", l, hf, f))
                for cc in range(11):
                    P.op("pe", lambda e, cc=cc, ps=ps, wv=wv: e.matmul(
                        ps[:, :N], wv[:, cc, :], self.actT[:, cc, :N], start=(cc == 0), stop=(cc == 10)),
                        reads=[wb, ("actT", cc)], writes=[ps], sig=(cc == 10))
                P.op("dve", lambda e, f=f, ps=ps: e.tensor_tensor(hT[:, f, :N], hT[:, f, :N], ps[:, :N], ALU.add),
                     reads=[ps, ("hT", f)], writes=[("hT", f)])
        if l == 0:
            self.dump("h2_" + blk.kind + str(blk.idx), hT[:, :, :N], [128, 8, N], self.hh())
        self.rmsnorm_fm(hT, self.V("png%d" % l), hnT, N, self.hh(), [hnT])
        pst = self.S[7]
        if blk.kind == "P":
            src = d["pT"].ap()[l][:, :, blk.tok0:blk.tok0 + N]
        else:
            src = d["psT"].ap()[l]
        P.dma("sp", pst[:, :2 * N].rearrange("p (c n) -> p c n", c=2), src, writes=[pst])
        for k in range(2):
            P.op("dve", lambda e, k=k: e.tensor_copy(self.PT[k][:, :N], pst[:, k * N:(k + 1) * N]), reads=[pst], writes=[self.PT[k]])
        self.in_ple = True
        pkey = ("w_ple", l)
        if pkey not in self.wc_map:
            off = self.wc_off
            self.wc_map[pkey] = off
            self.wc_off += 2048
            stg = self.wstg[self.wi % len(self.wstg)]
            self.wi += 1
            P.dma("sp", stg[:, :2048].rearrange("p (a b) -> p a b", a=2), d["w_ple_t"].ap()[l], writes=[stg])
            P.op("act", lambda e: e.activation(self.wple[:, :], stg[:, :], AF.Copy), reads=[stg], writes=[self.wple])
            P.dma("act", self.wcache.ap()[:, off:off + 2048], self.wple[:, :], reads=[self.wple], writes=[("wc", pkey)], key="wcwp")
        else:
            off = self.wc_map[pkey]
            P.dma("sp", self.wple[:, :], self.wcache.ap()[:, off:off + 2048], reads=[("wc", pkey)], writes=[self.wple])
        wpv = self.wple[:, :].rearrange("p (a b) -> p a b", a=2)
        for u in range(4):
            wb, wv = self.wload(d["w_gate_t"].ap()[l][:, :, 256 * u:256 * u + 256], [128, 8, 256], ("w_gate", l, u))
            for fc in range(2):
                f = 2 * u + fc
                pg, pp = self.pb[2 * (f % 2)], self.pb[2 * (f % 2) + 1]
                for k in range(8):
                    P.op("pe", lambda e, k=k, pg=pg, wv=wv, fc=fc: e.matmul(
                        pg[:, :N], wv[:, k, fc * 128:(fc + 1) * 128], hnT[:, k, :N],
                        start=(k == 0), stop=(k == 7)), reads=[wb, hnT], writes=[pg], sig=(k == 7))
                for k in range(2):
                    P.op("pe", lambda e, k=k, pp=pp, f=f: e.matmul(
                        pp[:, :N], wpv[:, k, f * 128:(f + 1) * 128], self.PT[k][:, :N],
                        start=(k == 0), stop=(k == 1)), reads=[self.wple, self.PT[k]], writes=[pp], sig=(k == 1))
                tmp = self.S[8 + f % 2]
                P.op("act", lambda e, pg=pg, tmp=tmp: e.activation(tmp[:, :N], pg[:, :N], AF.Sigmoid),
                     reads=[pg], writes=[tmp])
                P.op("dve", lambda e, pp=pp, tmp=tmp: e.tensor_tensor(tmp[:, :N], tmp[:, :N], pp[:, :N], ALU.mult),
                     reads=[tmp, pp], writes=[tmp])
                P.op("dve", lambda e, f=f, tmp=tmp: e.tensor_tensor(hT[:, f, :N], hT[:, f, :N], tmp[:, :N], ALU.add),
                     reads=[tmp, ("hT", f)], writes=[("hT", f)])
        self.in_ple = False
        if l == 0:
            self.dump("h3_" + blk.kind + str(blk.idx), hT[:, :, :N], [128, 8, N], self.hh())

    def Sv(self, i, N):
        return self.S[i][:, :2 * N].rearrange("p (c n) -> p c n", c=2)

    def mixer_D(self, blk, l):
        P = self.P
        N, nseq, L = blk.N, blk.nseq, blk.L
        d = self.d
        W = 3 + L
        ext = self.dext
        extv = ext[:, :, :nseq * W].rearrange("p c (s j) -> p c s j", s=nseq) if nseq > 1 else None
        def ev(ch, j0, j1):
            if nseq == 1:
                return ext[:, ch, j0:j1]
            return extv[:, ch, :, j0:j1]
        def v3(ap2):
            if nseq == 1:
                return ap2
            return ap2.rearrange("p (s j) -> p s j", s=nseq)
        gg, dm, r, ig, a, w5, hd, y = (self.Sv(i, N) for i in range(8))
        Sh = self.S
        dh = self.dhst[l]
        if blk.kind == "P":
            if blk.first:
                P.op("dve", lambda e: e.memset(ext[:, :, 0:3], 0.0), writes=[ext])
                P.op("dve", lambda e: e.memset(dh[:, :, :], 0.0), writes=[dh])
            else:
                P.op("dve", lambda e: e.tensor_copy(ext[:, :, 0:3], self.dtail[l][:, :, :]),
                     reads=[self.dtail[l]], writes=[ext])
        else:
            for ch in range(2):
                P.dma("sp", extv[:, ch, :, 0:3], d["sdc"].ap()[l][:, ch], writes=[ext], key="sdc_in")
            P.dma("sp", dh[:, :, :], d["sdh"].ap()[l], writes=[dh])
        wb, wv = self.w_in_unit(l, COLS["dx"], 256)
        for ch in range(2):
            ps = self.pb[ch]
            self.proj_fm(wb, wv, ch * 128, 128, ps, N, ps)
            P.op("act", lambda e, ch=ch, ps=ps: e.activation(ev(ch, 3, 3 + L), v3(ps[:, :N]), AF.Copy),
                 reads=[ps], writes=[ext])
        wb, wv = self.w_in_unit(l, COLS["dg"], 256)
        for ch in range(2):
            ps = self.pb[2 + ch]
            self.proj_fm(wb, wv, ch * 128, 128, ps, N, ps)
            P.op("act", lambda e, ch=ch, ps=ps: e.activation(gg[:, ch, :], ps[:, :N], AF.Gelu_apprx_tanh),
                 reads=[ps], writes=[Sh[0]])
        cw = self.V("dcw%d" % l)
        cbias = self.V("dcb%d" % l)
        for ch in range(2):
            P.op("dve", lambda e, ch=ch: e.tensor_scalar(v3(dm[:, ch, :]), ev(ch, 0, L), cw[:, ch * 4:ch * 4 + 1],
                                                         cbias[:, ch:ch + 1], ALU.mult, ALU.add),
                 reads=[ext, self.vecs], writes=[Sh[1]])
            for j in range(1, 4):
                P.op("dve", lambda e, ch=ch, j=j: e.scalar_tensor_tensor(
                    v3(dm[:, ch, :]), ev(ch, j, j + L), cw[:, ch * 4 + j:ch * 4 + j + 1], v3(dm[:, ch, :]),
                    ALU.mult, ALU.add), reads=[ext, self.vecs, Sh[1]], writes=[Sh[1]])
        if blk.kind == "P":
            if blk.last:
                P.dma("act", self.od["pdc"].ap()[l], ext[:, :, L:L + 3], reads=[ext], key="so1_%d_%s" % (l, str(locals().get("gc", "")) + str(locals().get("ch", ""))), out=True)
            else:
                P.op("dve", lambda e: e.tensor_copy(self.dtail[l][:, :, :], ext[:, :, L:L + 3]),
                     reads=[ext], writes=[self.dtail[l]])
        else:
            for ch in range(2):
                P.dma("act", self.od["sdco"].ap()[l][:, ch], extv[:, ch, :, L:L + 3], reads=[ext], key="so2_%d_%s" % (l, str(locals().get("gc", "")) + str(locals().get("ch", ""))), out=True)
        P.dma("sp", self.dgw[:, :], d["dgw"].ap()[:, 512 * l:512 * l + 512], writes=[self.dgw])
        gw = self.dgw[:, :].rearrange("p (w c j) -> p w c j", w=2, c=2)
        sp8 = self.drv[:, 4 * l:4 * l + 2]
        sp16 = self.drv[:, 4 * l + 2:4 * l + 4]
        for ch in range(2):
            for wi, (dst, dsth, bname) in enumerate(((r, Sh[2], "dba"), (ig, Sh[3], "dbx"))):
                ps = self.pb[5 + wi]
                P.op("pe", lambda e, ch=ch, wi=wi, ps=ps: e.matmul(ps[:, :N], gw[:, wi, ch, :], dm[:, ch, :],
                                                                 start=True, stop=True),
                     reads=[self.dgw, Sh[1]], writes=[ps])
                bv = self.V("%s%d" % (bname, l))
                P.op("act", lambda e, ch=ch, ps=ps, dst=dst, bv=bv: e.activation(dst[:, ch, :], ps[:, :N], AF.Sigmoid,
                                                                               bias=bv[:, ch:ch + 1]),
                     reads=[ps, self.vecs], writes=[dsth])
            P.op("act", lambda e, ch=ch: e.activation(a[:, ch, :], r[:, ch, :], AF.Exp, scale=sp8[:, ch:ch + 1]),
                 reads=[Sh[2], self.drv], writes=[Sh[4]])
            P.op("act", lambda e, ch=ch: e.activation(w5[:, ch, :], r[:, ch, :], AF.Exp, scale=sp16[:, ch:ch + 1]),
                 reads=[Sh[2], self.drv], writes=[Sh[5]])
            P.op("dve", lambda e, ch=ch: e.tensor_scalar(w5[:, ch, :], w5[:, ch, :], -1.0, 1.0, ALU.mult, ALU.add),
                 reads=[Sh[5]], writes=[Sh[5]])
            P.op("act", lambda e, ch=ch: e.activation(w5[:, ch, :], w5[:, ch, :], AF.Sqrt), reads=[Sh[5]], writes=[Sh[5]])
            P.op("dve", lambda e, ch=ch: e.tensor_tensor(w5[:, ch, :], w5[:, ch, :], ig[:, ch, :], ALU.mult),
                 reads=[Sh[5], Sh[3]], writes=[Sh[5]])
            P.op("dve", lambda e, ch=ch: e.tensor_tensor(w5[:, ch, :], w5[:, ch, :], dm[:, ch, :], ALU.mult),
                 reads=[Sh[5], Sh[1]], writes=[Sh[5]])
            for s in range(nseq):
                P.op("dve", lambda e, ch=ch, s=s: e.tensor_tensor_scan(
                    hd[:, ch, s * L:(s + 1) * L], a[:, ch, s * L:(s + 1) * L], w5[:, ch, s * L:(s + 1) * L],
                    dh[:, ch, s:s + 1], ALU.mult, ALU.add), reads=[Sh[4], Sh[5], dh], writes=[Sh[6]])
            if nseq == 1:
                src = hd[:, ch, L - 1:L]
            else:
                src = hd[:, ch, :].rearrange("p (s j) -> p s j", s=nseq)[:, :, L - 1]
            P.op("dve", lambda e, ch=ch, src=src: e.tensor_copy(dh[:, ch, 0:nseq], src), reads=[Sh[6]], writes=[dh])
            P.op("dve", lambda e, ch=ch: e.tensor_tensor(y[:, ch, :], hd[:, ch, :], gg[:, ch, :], ALU.mult),
                 reads=[Sh[6], Sh[0]], writes=[Sh[7]])
        if blk.last:
            if blk.kind == "P":
                P.dma("act", self.od["pdh"].ap()[l], dh[:, :, 0], reads=[dh], key="so3_%d_%s" % (l, str(locals().get("gc", "")) + str(locals().get("ch", ""))), out=True, allow_slow_non_contiguous=True)
            else:
                P.dma("act", self.od["sdho"].ap()[l], dh[:, :, :], reads=[dh], key="so4_%d_%s" % (l, str(locals().get("gc", "")) + str(locals().get("ch", ""))), out=True)
        self.group_rmsnorm(y, Sh[7], self.V("dng%d" % l), 6, N)

    def group_rmsnorm(self, y, yh, gv, k0, N):
        P = self.P
        ps = self.pb[4]
        for ch in range(2):
            sq = self.sqb[ch]
            P.op("act", lambda e, ch=ch, sq=sq: e.activation(sq[:, :N], y[:, ch, :], AF.Square), reads=[yh], writes=[sq])
            P.op("pe", lambda e, ch=ch, sq=sq: e.matmul(ps[:, :N], self.ones_bf[:], sq[:, :N], start=(ch == 0), stop=(ch == 1)),
                 reads=[sq, self.ones_bf], writes=[ps])
        rt = self.rt
        P.op("act", lambda e: e.activation(rt[:, :N], ps[:, :N], AF.Ln, bias=EPS, scale=1.0 / 256), reads=[ps], writes=[rt])
        P.op("act", lambda e: e.activation(rt[:, :N], rt[:, :N], AF.Exp, scale=-0.5), reads=[rt], writes=[rt])
        for ch in range(2):
            P.op("dve", lambda e, ch=ch: e.scalar_tensor_tensor(self.mixT[:, k0 + ch, :N], y[:, ch, :], gv[:, ch:ch + 1],
                                                               rt[:, :N], ALU.mult, ALU.mult),
                 reads=[yh, rt, self.vecs], writes=[("mixT", k0 + ch)])

    def mixer_B(self, blk, l):
        P = self.P
        N = blk.N
        d = self.d
        KT, Vh, qT = self.KT[l], self.Vh[l], self.qT
        Eh = [("E", h) for h in range(4)]
        kcol0 = blk.tok0 if blk.kind == "P" else NTOK
        wb, wv = self.w_in_unit(l, COLS["bq"], 256)
        for pr in range(2):
            ps = self.pb[pr]
            self.proj_fm(wb, wv, pr * 128, 128, ps, N, ps)
            for hh in range(2):
                P.op("act", lambda e, pr=pr, ps=ps, hh=hh: e.activation(
                    qT[hh * 64:(hh + 1) * 64, 2 * pr + hh, :N], ps[hh * 64:(hh + 1) * 64, :N], AF.Copy, scale=0.125),
                    reads=[ps], writes=[qT])
        wbk, wvk = self.w_in_unit(l, COLS["bk"], 256)
        for pr in range(2):
            ps = self.pb[2 + pr]
            self.proj_fm(wbk, wvk, pr * 128, 128, ps, N, ps)
            P.op("act", lambda e, pr=pr, ps=ps: e.activation(KT[:, pr, kcol0:kcol0 + N], ps[:, :N], AF.Copy),
                 reads=[ps], writes=[("KT", l, "new")])
        if self.o.get("Bstop") == 1:
            return
        wbv, wvv = self.w_in_unit(l, COLS["bv"], 256)
        for ti, (c0, nt, seq, pos) in enumerate(blk.tiles):
            if self.o.get("Bvar") == 23:
                break
            ps = self.pb[ti % 2]
            for half, (wbx, wvx) in enumerate(((wbk, wvk), (wbv, wvv))):
                for k in range(8):
                    P.op("pe", lambda e, k=k, ps=ps, wvx=wvx, half=half, c0=c0, nt=nt: e.matmul(
                        ps[:nt, half * 256:(half + 1) * 256], self.hnT[:, k, c0:c0 + nt], wvx[:, k, :],
                        start=(k == 0), stop=(k == 7)), reads=[wbx, self.hnT], writes=[ps], sig=(k == 7 and half == 1))
            if self.o.get("Bvar") == 24:
                continue
            kvs = self.S[8 + ti % 2]
            if self.o.get("Bvar") != 26:
                P.op("act", lambda e, ps=ps, kvs=kvs, nt=nt: e.activation(kvs[:nt, :512], ps[:nt, :512], AF.Copy),
                     reads=[ps], writes=[kvs])
            if blk.kind == "P":
                r0 = blk.tok0 + c0
                ok, ov = self.od["pbk"].ap()[l][r0:r0 + nt, :], self.od["pbv"].ap()[l][r0:r0 + nt, :]
                vdst, vh = Vh[:nt, pos, :], ("Vh", l, pos)
            else:
                ok, ov = self.od["sbk"].ap()[l][c0:c0 + nt, :], self.od["sbv"].ap()[l][c0:c0 + nt, :]
                vdst, vh = self.Vnew[:nt, seq, :], ("Vnew", seq)
            if self.o.get("Bvar") not in (21, 25, 26):
                P.dma("act", ok, kvs[:nt, 0:256], reads=[kvs], key="kv_out%d" % (ti % 2), out=True)
                P.dma("act", ov, kvs[:nt, 256:512], reads=[kvs], key="kv_out%d" % (ti % 2), out=True)
            if self.o.get("Bvar") not in (22, 25):
                P.op("dve", lambda e, ps=ps, vdst=vdst, nt=nt: e.tensor_copy(vdst, ps[:nt, 256:512]), reads=[ps], writes=[vh])
        if self.o.get("Bstop") == 2:
            return
        ob = self.Sv(0, N)
        obh = self.S[0]
        if blk.kind == "P":
            for ti, (c0, nt, seq, pos) in enumerate(blk.tiles):
                self.attn_prompt_tile(l, c0, pos, ob, obh)
        else:
            for ti, (c0, nt, seq, pos) in enumerate(blk.tiles):
                self.attn_sample_seq(l, c0, seq, ob, obh)
        self.group_rmsnorm(ob, obh, self.V("bng%d" % l), 2, N)

    def attn_finish(self, accps, lps, ob, obh, c0, nt):
        P = self.P
        rl = self.S[2]
        n4 = 4 * nt
        P.op("act", lambda e: e.activation(rl[:, :n4], lps[:, :n4], AF.Ln), reads=[lps], writes=[rl])
        P.op("act", lambda e: e.activation(rl[:, :n4], rl[:, :n4], AF.Exp, scale=-1.0), reads=[rl], writes=[rl])
        for h in range(4):
            r0, pr = (h % 2) * 64, h // 2
            P.op("dve", lambda e, h=h, r0=r0, pr=pr: e.tensor_tensor(
                ob[r0:r0 + 64, pr, c0:c0 + nt], accps[r0:r0 + 64, pr * nt:(pr + 1) * nt],
                rl[r0:r0 + 64, h * nt:(h + 1) * nt], ALU.mult), reads=[accps, rl], writes=[obh])

    def attn_prompt_tile(self, l, c0, pos, ob, obh):
        P = self.P
        KT, Vh, qT = self.KT[l], self.Vh[l], self.qT
        accps, lps = self.pb[7], self.pb[3]
        ex = self.S[1]
        P.op("dve", lambda e: e.memset(accps[:, :256], 0.0), writes=[accps])
        def qk(j):
            dl = pos - j
            ST = self.pb[5 + j % 2]
            PT = self.PT[j % 2]
            exj = self.Vnew[:, 2 * (j % 2):2 * (j % 2) + 2, :].rearrange("p a b -> p (a b)")
            exh = ("Vnew", 2 * (j % 2))
            for h in range(4):
                pr = h // 2
                P.op("pe", lambda e, h=h, pr=pr, ST=ST, j=j: e.matmul(
                    ST[:, h * 128:(h + 1) * 128], KT[:, pr, j * 128:(j + 1) * 128],
                    qT[:, h, c0:c0 + 128], start=True, stop=True),
                    reads=[("KT", l, "new"), qT], writes=[ST], sig=(h == 3))
            P.op("act", lambda e, ST=ST: e.activation(exj, ST[:, :512], AF.Exp), reads=[ST],
                 writes=[exh, ("Vnew", 2 * (j % 2) + 1)])
            P.op("dve", lambda e, PT=PT, dl=dl: e.tensor_tensor(
                PT[:, :].rearrange("p (h t) -> p h t", h=4), exj.rearrange("p (h t) -> p h t", h=4),
                self.E[:, :, dl * 128:(dl + 1) * 128], ALU.mult), reads=[exh] + [("E", h) for h in range(4)], writes=[PT])

        def pv(j):
            PT = self.PT[j % 2]
            for h in range(4):
                r0, pr = (h % 2) * 64, h // 2
                P.op("pe", lambda e, h=h, r0=r0, pr=pr, PT=PT, j=j: e.matmul(
                    accps[r0:r0 + 64, pr * 128:(pr + 1) * 128], Vh[:, j, h * 64:(h + 1) * 64],
                    PT[:, h * 128:(h + 1) * 128], start=False, stop=(j == pos), skip_group_check=True),
                    reads=[("Vh", l, j), PT], writes=[accps], sig=False)
            P.op("pe", lambda e, PT=PT, j=j: e.matmul(lps[:, :512], self.ones_bf[:], PT[:, :512],
                                                      start=(j == 0), stop=(j == pos)),
                 reads=[self.ones_bf, PT], writes=[lps])

        qk(0)
        for j in range(pos + 1):
            if j + 1 <= pos:
                qk(j + 1)
            pv(j)
        if self.o.get("Bstop") == 3:
            return
        self.attn_finish(accps, lps, ob, obh, c0, 128)

    def attn_sample_seq(self, l, c0, seq, ob, obh):
        P = self.P
        d = self.d
        KT, Vh, qT = self.KT[l], self.Vh[l], self.qT
        nt = LS
        kth = ("KT", l, "new")
        vhh = [("Vh", l, j) for j in range(16)]
        for q4 in range(4):
            t = self.S[4 + q4]
            P.dma("sp", t[:, :1024].rearrange("p (c n) -> p c n", c=2),
                  d["kcT"].ap()[l, seq][:, :, q4 * 512:(q4 + 1) * 512], writes=[t])
            P.op("dve", lambda e, t=t, q4=q4: e.tensor_copy(KT[:, :, q4 * 512:(q4 + 1) * 512],
                                                           t[:, :1024].rearrange("p (c n) -> p c n", c=2)),
                 reads=[t], writes=[kth])
        for q4 in range(4):
            t = self.S[4 + q4]
            P.dma("sp", t[:, :1024].rearrange("p (j c) -> p j c", j=4),
                  d["vc"].ap()[l, seq][:, 4 * q4:4 * q4 + 4, :], writes=[t])
            P.op("act", lambda e, t=t, q4=q4: e.activation(Vh[:, 4 * q4:4 * q4 + 4, :],
                                                          t[:, :1024].rearrange("p (j c) -> p j c", j=4), AF.Copy),
                 reads=[t], writes=vhh[4 * q4:4 * q4 + 4])
        accps, lps = self.pb[7], self.pb[3]
        ST, ST2 = self.pb[5], self.pb[6]
        PT, PT2 = self.PT[0], self.PT[1]
        ex = self.S[1]
        Ehs = [("E", h) for h in range(4)]
        P.op("dve", lambda e: e.memset(accps[:, :256], 0.0), writes=[accps])
        for dl in range(1, 17):
            j = 16 - dl
            for h in range(4):
                pr = h // 2
                o = (dl - 1) * 32 + h * 8
                P.op("pe", lambda e, h=h, pr=pr, o=o, j=j: e.matmul(
                    ST[:, o:o + 8], KT[:, pr, j * 128:(j + 1) * 128], qT[:, h, c0:c0 + 8], start=True, stop=True),
                    reads=[kth, qT], writes=[ST], sig=(dl == 16 and h == 3))
        P.op("act", lambda e: e.activation(ex[:, :512], ST[:, :512], AF.Exp), reads=[ST], writes=[ex])
        v4 = lambda ap: ap.rearrange("p (d h t) -> p d h t", d=16, h=4)
        Ev = self.E[:, :, 128:128 * 17].rearrange("p h (d t) -> p d h t", t=128)[:, :, :, 0:8]
        P.op("dve", lambda e: e.tensor_tensor(v4(PT[:, :512]), v4(ex[:, :512]), Ev, ALU.mult),
             reads=[ex] + Ehs, writes=[PT])
        for dl in range(1, 17):
            j = 16 - dl
            for h in range(4):
                r0, pr = (h % 2) * 64, h // 2
                o = (dl - 1) * 32 + h * 8
                P.op("pe", lambda e, h=h, r0=r0, pr=pr, o=o, j=j: e.matmul(
                    accps[r0:r0 + 64, pr * 8:(pr + 1) * 8], Vh[:, j, h * 64:(h + 1) * 64], PT[:, o:o + 8],
                    start=False, stop=False, skip_group_check=True), reads=[vhh[j], PT], writes=[accps], sig=False)
            P.op("pe", lambda e, dl=dl: e.matmul(lps[:, 0:32], self.ones_bf[:], PT[:, (dl - 1) * 32:dl * 32],
                                                 start=(dl == 1), stop=False),
                 reads=[self.ones_bf, PT], writes=[lps], sig=False)
        kc = NTOK + c0
        for h in range(4):
            pr = h // 2
            P.op("pe", lambda e, h=h, pr=pr: e.matmul(ST2[:nt, h * 8:(h + 1) * 8], KT[:, pr, kc:kc + nt],
                                                      qT[:, h, c0:c0 + nt], start=True, stop=True),
                 reads=[kth, qT], writes=[ST2], sig=(h == 3))
        P.op("act", lambda e: e.activation(ex[:nt, 512:544], ST2[:nt, 0:32], AF.Exp), reads=[ST2], writes=[ex])
        P.op("dve", lambda e: e.tensor_tensor(PT2[:nt, 0:32].rearrange("p (h t) -> p h t", h=4),
                                              ex[:nt, 512:544].rearrange("p (h t) -> p h t", h=4),
                                              self.E[:nt, :, 0:nt], ALU.mult), reads=[ex] + Ehs, writes=[PT2])
        for h in range(4):
            r0, pr = (h % 2) * 64, h // 2
            P.op("pe", lambda e, h=h, r0=r0, pr=pr: e.matmul(
                accps[r0:r0 + 64, pr * 8:(pr + 1) * 8], self.Vnew[:nt, seq, h * 64:(h + 1) * 64], PT2[:nt, h * 8:(h + 1) * 8],
                start=False, stop=True, skip_group_check=True), reads=[("Vnew", seq), PT2], writes=[accps], sig=(h == 3))
        P.op("pe", lambda e: e.matmul(lps[:, 0:32], self.ones_bf[:nt, :], PT2[:nt, 0:32], start=False, stop=True),
             reads=[self.ones_bf, PT2], writes=[lps])
        self.attn_finish(accps, lps, ob, obh, c0, nt)


def _mixer_A(self, blk, l):
    P = self.P
    N, nseq, L = blk.N, blk.nseq, blk.L
    d = self.d
    csz = 16 if blk.kind == "P" else LS
    ncht = N // csz
    Sh = self.S
    Q, KA, LF, G, EG, W5, OA, SG = (self.Sv(i, N) for i in range(8))
    lb, omlb, nomlb = self.lbv(l)
    aT = self.actT
    ah = lambda c: ("actT", c)
    kt = aT[:, 0:2, :N]
    SA = self.SAw
    SAp = self.SAp[l]
    qT = self.qT
    wb, wv = self.w_in_unit(l, COLS["aq"], 256)
    for pr in range(2):
        ps = self.pb[pr]
        self.proj_fm(wb, wv, pr * 128, 128, ps, N, ps)
        P.op("act", lambda e, pr=pr, ps=ps: e.activation(Q[:, pr, :], ps[:, :N], AF.Copy), reads=[ps], writes=[Sh[0]])
    wb, wv = self.w_in_unit(l, COLS["af"], 256)
    for pr in range(2):
        ps = self.pb[2 + pr]
        self.proj_fm(wb, wv, pr * 128, 128, ps, N, ps)
        P.op("act", lambda e, pr=pr, ps=ps: e.activation(KA[:, pr, :], ps[:, :N], AF.Sigmoid), reads=[ps], writes=[Sh[1]])
        P.op("dve", lambda e, pr=pr: e.tensor_scalar(LF[:, pr, :], KA[:, pr, :], omlb[:, pr:pr + 1], lb[:, pr:pr + 1],
                                                     ALU.mult, ALU.add), reads=[Sh[1], self.drv], writes=[Sh[2]])
        P.op("act", lambda e, pr=pr: e.activation(LF[:, pr, :], LF[:, pr, :], AF.Ln), reads=[Sh[2]], writes=[Sh[2]])
        P.op("dve", lambda e, pr=pr: e.tensor_scalar(KA[:, pr, :], KA[:, pr, :], nomlb[:, pr:pr + 1], omlb[:, pr:pr + 1],
                                                     ALU.mult, ALU.add), reads=[Sh[1], self.drv], writes=[Sh[1]])
        rs = self.C("reset16")[:, :N] if blk.kind == "P" else self.C("resetS")[:, :N]
        P.op("dve", lambda e, pr=pr, rs=rs: e.tensor_tensor_scan(G[:, pr, :], rs, LF[:, pr, :], 0.0, ALU.mult, ALU.add),
             reads=[Sh[2], self.cst], writes=[Sh[3]])
        P.op("act", lambda e, pr=pr: e.activation(EG[:, pr, :], G[:, pr, :], AF.Exp), reads=[Sh[3]], writes=[Sh[4]])
        P.op("act", lambda e, pr=pr: e.activation(W5[:, pr, :], G[:, pr, :], AF.Exp, scale=-1.0), reads=[Sh[3]], writes=[Sh[5]])
        P.op("dve", lambda e, pr=pr: e.tensor_tensor(Q[:, pr, :], Q[:, pr, :], EG[:, pr, :], ALU.mult),
             reads=[Sh[0], Sh[4]], writes=[Sh[0]])
        for hh in range(2):
            P.op("act", lambda e, pr=pr, hh=hh: e.activation(qT[hh * 64:(hh + 1) * 64, 2 * pr + hh, :N],
                                                            Q[hh * 64:(hh + 1) * 64, pr, :], AF.Copy),
                 reads=[Sh[0]], writes=[qT])
        P.op("dve", lambda e, pr=pr: e.tensor_tensor(kt[:, pr, :], KA[:, pr, :], W5[:, pr, :], ALU.mult),
             reads=[Sh[1], Sh[5]], writes=[ah(pr)])
        Gv = G[:, pr, :].rearrange("p (n j) -> p n j", j=csz)
        Wv = W5[:, pr, :].rearrange("p (n j) -> p n j", j=csz)
        P.op("dve", lambda e, Gv=Gv, Wv=Wv: e.tensor_tensor(Wv, Gv[:, :, csz - 1:csz].to_broadcast([128, ncht, csz]), Gv,
                                                            ALU.subtract), reads=[Sh[3], ah(pr)], writes=[Sh[5]])
        P.op("act", lambda e, pr=pr: e.activation(W5[:, pr, :], W5[:, pr, :], AF.Exp), reads=[Sh[5]], writes=[Sh[5]])
        P.op("dve", lambda e, pr=pr: e.tensor_tensor(W5[:, pr, :], W5[:, pr, :], KA[:, pr, :], ALU.mult),
             reads=[Sh[5], Sh[1]], writes=[Sh[5]])
    wb, wv = self.w_in_unit(l, COLS["ag"], 256)
    for pr in range(2):
        ps = self.pb[pr]
        self.proj_fm(wb, wv, pr * 128, 128, ps, N, ps)
        P.op("act", lambda e, pr=pr, ps=ps: e.activation(SG[:, pr, :], ps[:, :N], AF.Silu), reads=[ps], writes=[Sh[7]])
    wbi, wvi = self.w_in_unit(l, COLS["ai"], 256)
    ident = self.C("ident")
    for ti, (c0, nt, seq, pos) in enumerate(blk.tiles):
        nch = max(nt // csz, 1)
        slot = 6 + ti % 2
        kdT = aT[:, slot, 0:256]
        Vb = aT[:, slot, 256:512]
        Vblk = aT[:, 2:6, :]
        if blk.kind == "P":
            if pos == 0:
                P.op("dve", lambda e: e.memset(SA[:, :, 0, :], 0.0), writes=[SA])
            elif ti == 0:
                P.op("dve", lambda e: e.tensor_copy(SA[:, :, 0, :], SAp[:, :, :]), reads=[SAp], writes=[SA])
        else:
            P.dma("sp", SA[:, :, 0, :], d["sa"].ap()[l, seq], writes=[SA])
        pv = self.pb[2]
        for k in range(8):
            P.op("pe", lambda e, k=k, c0=c0, nt=nt: e.matmul(pv[:nt, 0:256], self.hnT[:, k, c0:c0 + nt], wvi[:, k, :],
                                                            start=(k == 0), stop=(k == 7)),
                 reads=[wbi, self.hnT], writes=[pv], sig=(k == 7))
        P.op("act", lambda e, nt=nt, Vb=Vb: e.activation(Vb[:nt, :], pv[:nt, 0:256], AF.Copy), reads=[pv], writes=[ah(slot)])
        if nch > 1:
            for h in range(4):
                P.op("dve", lambda e, h=h, nt=nt: e.tensor_tensor(
                    Vblk[:nt, h, :].rearrange("p (n v) -> p n v", n=8),
                    pv[:nt, h * 64:(h + 1) * 64].rearrange("p (o v) -> p o v", o=1).to_broadcast([nt, 8, 64]),
                    self.C("blk16")[:nt, :].rearrange("p (n o) -> p n o", o=1).to_broadcast([nt, 8, 64]), ALU.mult),
                    reads=[pv, self.cst], writes=[ah(2 + h)])
        pk = self.pb[3]
        for pr in range(2):
            P.op("pe", lambda e, pr=pr, c0=c0, nt=nt: e.transpose(pk[:nt, pr * 128:(pr + 1) * 128], W5[:, pr, c0:c0 + nt], ident),
                 reads=[Sh[5], self.cst], writes=[pk])
        P.op("act", lambda e, nt=nt, kdT=kdT: e.activation(kdT[:nt, :], pk[:nt, 0:256], AF.Copy), reads=[pk], writes=[ah(slot)])
        ST = self.pb[5]
        for h in range(4):
            pr = h // 2
            P.op("pe", lambda e, h=h, pr=pr, c0=c0, nt=nt: e.matmul(ST[:nt, h * 128:h * 128 + nt], kt[:, pr, c0:c0 + nt],
                                                                   qT[:, h, c0:c0 + nt], start=True, stop=True),
                 reads=[ah(pr), qT], writes=[ST], sig=(h == 3))
        PTa = self.PT[ti % 2]
        P.op("dve", lambda e, nt=nt, PTa=PTa: e.tensor_tensor(
            PTa[:nt, :].rearrange("p (h t) -> p h t", h=4)[:, :, :nt],
            ST[:nt, :].rearrange("p (h t) -> p h t", h=4)[:, :, :nt],
            self.C("mask16T")[:nt, :nt].rearrange("p (o t) -> p o t", o=1).to_broadcast([nt, 4, nt]), ALU.mult),
            reads=[ST, self.cst], writes=[PTa])
        Ups = (self.pb[0], self.pb[1])
        for h in range(4):
            r0, pr = (h % 2) * 64, h // 2
            rhs = Vblk[:nt, h, :] if nch > 1 else Vb[:nt, h * 64:(h + 1) * 64]
            rh = ah(2 + h) if nch > 1 else ah(slot)
            P.op("pe", lambda e, h=h, r0=r0, pr=pr, nt=nt, rhs=rhs, nch=nch: e.matmul(
                Ups[pr][r0:r0 + 64, 0:nch * 64], kdT[:nt, h * 64:(h + 1) * 64], rhs, start=True, stop=True),
                reads=[ah(slot), rh], writes=[Ups[pr]], sig=(h % 2 == 1))
        ob = self.pb[6]
        P.op("dve", lambda e: e.memset(ob[:, 0:256], 0.0), writes=[ob])
        for h in range(4):
            r0, pr = (h % 2) * 64, h // 2
            P.op("pe", lambda e, h=h, r0=r0, pr=pr, nt=nt, PTa=PTa, Vb=Vb: e.matmul(
                ob[r0:r0 + 64, pr * 128:pr * 128 + nt], Vb[:nt, h * 64:(h + 1) * 64], PTa[:nt, h * 128:h * 128 + nt],
                start=False, stop=False, skip_group_check=True), reads=[ah(slot), PTa], writes=[ob], sig=False)
        for n in range(nch):
            cend = c0 + n * csz + csz - 1
            for pr in range(2):
                P.op("dve", lambda e, n=n, pr=pr, cend=cend: e.scalar_tensor_tensor(
                    SA[:, pr, n + 1, :], SA[:, pr, n, :], EG[:, pr, cend:cend + 1], Ups[pr][:, n * 64:(n + 1) * 64],
                    ALU.mult, ALU.add), reads=[SA, Sh[4], Ups[pr]], writes=[SA])
        for h in range(4):
            r0, pr = (h % 2) * 64, h // 2
            for n in range(nch):
                cs = c0 + n * csz
                P.op("pe", lambda e, h=h, r0=r0, pr=pr, n=n, cs=cs: e.matmul(
                    ob[r0:r0 + 64, pr * 128 + n * csz:pr * 128 + (n + 1) * csz], SA[r0:r0 + 64, pr, n, :],
                    Q[r0:r0 + 64, pr, cs:cs + csz], start=False, stop=True, skip_group_check=True),
                    reads=[SA, Sh[0]], writes=[ob], sig=(h == 3 and n == nch - 1))
        for pr in range(2):
            P.op("act", lambda e, pr=pr, c0=c0, nt=nt: e.activation(OA[:, pr, c0:c0 + nt], ob[:, pr * 128:pr * 128 + nt], AF.Copy),
                 reads=[ob], writes=[Sh[6]])
        last_of_seq = (blk.kind == "S") or (blk.last and ti == len(blk.tiles) - 1)
        if last_of_seq:
            dst = self.od["pa"].ap()[l] if blk.kind == "P" else self.od["sao"].ap()[l, seq]
            P.dma("act", dst, SA[:, :, nch, :], reads=[SA], key="sa_out", out=True)
        elif ti == len(blk.tiles) - 1:
            P.op("dve", lambda e, nch=nch: e.tensor_copy(SAp[:, :, :], SA[:, :, nch, :]), reads=[SA], writes=[SAp])
        else:
            P.op("dve", lambda e, nch=nch: e.tensor_copy(SA[:, :, 0, :], SA[:, :, nch, :]), reads=[SA], writes=[SA])
    self.head_norm_gate(OA, Sh[6], SG, Sh[7], self.V("ang%d" % l), 0, N)


def _head_norm_gate(self, O, Oh, SG, SGh, gv, k0, N, gcol=None):
    P = self.P
    sq, tmp = self.S[8], self.S[9]
    for pr in range(2):
        ps = self.pb[4]
        P.op("act", lambda e, pr=pr: e.activation(sq[:, :N], O[:, pr, :], AF.Square), reads=[Oh], writes=[sq])
        P.op("pe", lambda e: e.matmul(ps[:, :N], self.C("onesbd"), sq[:, :N], start=True, stop=True),
             reads=[sq, self.cst], writes=[ps])
        rt = self.rt
        P.op("act", lambda e: e.activation(rt[:, :N], ps[:, :N], AF.Ln, bias=EPS, scale=1.0 / 64), reads=[ps], writes=[rt])
        P.op("act", lambda e: e.activation(rt[:, :N], rt[:, :N], AF.Exp, scale=-0.5), reads=[rt], writes=[rt])
        g = gv[:, pr:pr + 1] if gcol is None else gv[:, 0:1]
        P.op("dve", lambda e, pr=pr, g=g: e.scalar_tensor_tensor(tmp[:, :N], O[:, pr, :], g, rt[:, :N], ALU.mult, ALU.mult),
             reads=[Oh, rt, self.vecs], writes=[tmp])
        P.op("dve", lambda e, pr=pr: e.tensor_tensor(self.mixT[:, k0 + pr, :N], tmp[:, :N], SG[:, pr, :], ALU.mult),
             reads=[tmp, SGh], writes=[("mixT", k0 + pr)])


Builder.mixer_A = _mixer_A
Builder.head_norm_gate = _head_norm_gate


def _mixer_C(self, blk, l):
    P = self.P
    N, nseq, L = blk.N, blk.nseq, blk.L
    d = self.d
    cs = 64 if blk.kind == "P" else LS
    nlev = 5 if blk.kind == "P" else 2
    Sh = self.S
    XC = [self.Sv(g, N) for g in range(3)]
    SGz, OC = self.Sv(3, N), self.Sv(4, N)
    W = 3 + L
    ext = self.dext
    extv = ext[:, :, :nseq * W].rearrange("p c (s j) -> p c s j", s=nseq) if nseq > 1 else None
    def ev(ch, j0, j1):
        return ext[:, ch, j0:j1] if nseq == 1 else extv[:, ch, :, j0:j1]
    def v3(ap2):
        return ap2 if nseq == 1 else ap2.rearrange("p (s j) -> p s j", s=nseq)
    cw = self.V("ccw%d" % l)
    ident = self.C("ident")
    for g, cname in enumerate(("cq", "ck", "cv")):
        wb, wv = self.w_in_unit(l, COLS[cname], 256)
        X = XC[g]
        for ch in range(2):
            gc = 2 * g + ch
            if blk.kind == "P":
                if blk.first:
                    P.op("dve", lambda e, ch=ch: e.memset(ext[:, ch, 0:3], 0.0), writes=[ext])
                else:
                    P.op("dve", lambda e, ch=ch, gc=gc: e.tensor_copy(ext[:, ch, 0:3], self.ctail[l][:, gc, :]),
                         reads=[self.ctail[l]], writes=[ext])
            else:
                P.dma("sp", extv[:, ch, :, 0:3], d["scc"].ap()[l][:, gc], writes=[ext], key="scc_in")
            ps = self.pb[ch]
            self.proj_fm(wb, wv, ch * 128, 128, ps, N, ps)
            P.op("act", lambda e, ch=ch, ps=ps: e.activation(ev(ch, 3, 3 + L), v3(ps[:, :N]), AF.Copy),
                 reads=[ps], writes=[ext])
            P.op("dve", lambda e, ch=ch, gc=gc: e.tensor_scalar(v3(X[:, ch, :]), ev(ch, 0, L), cw[:, gc * 4:gc * 4 + 1], None, ALU.mult),
                 reads=[ext, self.vecs], writes=[Sh[g]])
            for j in range(1, 4):
                P.op("dve", lambda e, ch=ch, gc=gc, j=j: e.scalar_tensor_tensor(
                    v3(X[:, ch, :]), ev(ch, j, j + L), cw[:, gc * 4 + j:gc * 4 + j + 1], v3(X[:, ch, :]), ALU.mult, ALU.add),
                    reads=[ext, self.vecs, Sh[g]], writes=[Sh[g]])
            if blk.kind == "P":
                if blk.last:
                    P.dma("act", self.od["pcc"].ap()[l][:, gc, :], ext[:, ch, L:L + 3], reads=[ext], key="so5_%d_%s" % (l, str(locals().get("gc", "")) + str(locals().get("ch", ""))), out=True)
                else:
                    P.op("dve", lambda e, ch=ch, gc=gc: e.tensor_copy(self.ctail[l][:, gc, :], ext[:, ch, L:L + 3]),
                         reads=[ext], writes=[self.ctail[l]])
            else:
                P.dma("act", self.od["scco"].ap()[l][:, gc], extv[:, ch, :, L:L + 3], reads=[ext], key="so6_%d_%s" % (l, str(locals().get("gc", "")) + str(locals().get("ch", ""))), out=True)
            P.op("act", lambda e, ch=ch: e.activation(X[:, ch, :], X[:, ch, :], AF.Silu), reads=[Sh[g]], writes=[Sh[g]])
    for g, sc_ in ((0, 0.125), (1, 1.0)):
        X = XC[g]
        for pr in range(2):
            sq, ps, rt = Sh[8], self.pb[4], self.rt
            P.op("act", lambda e, pr=pr, X=X: e.activation(sq[:, :N], X[:, pr, :], AF.Square), reads=[Sh[g]], writes=[sq])
            P.op("pe", lambda e: e.matmul(ps[:, :N], self.C("onesbd"), sq[:, :N], start=True, stop=True),
                 reads=[sq, self.cst], writes=[ps])
            P.op("act", lambda e: e.activation(rt[:, :N], ps[:, :N], AF.Ln, bias=EPS), reads=[ps], writes=[rt])
            P.op("act", lambda e: e.activation(rt[:, :N], rt[:, :N], AF.Exp, scale=-0.5), reads=[rt], writes=[rt])
            P.op("dve", lambda e, pr=pr, X=X, sc_=sc_: e.scalar_tensor_tensor(X[:, pr, :], X[:, pr, :], sc_, rt[:, :N], ALU.mult, ALU.mult),
                 reads=[Sh[g], rt], writes=[Sh[g]])
    if blk.idx == 0 and l == 0 and blk.kind == "P":
        for g, nm in enumerate(("C_q", "C_k", "C_v")):
            self.dump(nm, XC[g], [128, 2, N], [Sh[g]])
    wb, wv = self.w_in_unit(l, COLS["cz"], 256)
    for pr in range(2):
        ps = self.pb[pr]
        self.proj_fm(wb, wv, pr * 128, 128, ps, N, ps)
        P.op("act", lambda e, pr=pr, ps=ps: e.activation(SGz[:, pr, :], ps[:, :N], AF.Silu), reads=[ps], writes=[Sh[3]])
    wb8, wv8 = self.w_in_unit(l, COLS["cb"], 8)
    SCw, SCp, SCm = self.SCw, self.SCp[l], self.SCm
    kmt, kbgm, kdm = self.kmt, self.kbgm, self.kdm
    for t_ in (SCm, kmt, kbgm, kdm):
        P.op("dve", lambda e, t_=t_: e.memset(t_[:], 0.0), writes=[t_])
    def mask_state():
        for hh in range(2):
            P.op("act", lambda e, hh=hh: e.activation(SCm[hh * 64:(hh + 1) * 64, hh::2, :], SCw[hh * 64:(hh + 1) * 64, :, :], AF.Copy),
                 reads=[SCw], writes=[SCm])
    scb = Sh[9]
    def Sq(i, q):
        return Sh[i][:, q * 256:(q + 1) * 256], ("Sq", i, q)
    (kTM, hkTM), (vTM, hvTM), (bv, hbv), (kbg, hkbg) = (Sq(5, q) for q in range(4))
    (eD, heD), (PA, hPA), (eDT, heDT), (RA, hRA) = (Sq(6, q) for q in range(4))
    (Nm, hNm), (PB, hPB), (RB, hRB), (kd, hkd) = (Sq(7, q) for q in range(4))
    (un, hun), (o1s, ho1s), (oTM, hoTM), (wT, hwT) = (Sq(8, q) for q in range(4))
    def sc_t(name, ti):
        o = dict(bet=0, la=4, g=8, eg=12, nbet=16, begp=20, glb=24, kds=28, egl=32, t1=36)[name] + 40 * ti
        return scb[:, o:o + 4], ("sc", name, ti)
    pb = self.pb
    def tile_scalars(ti):
        return tuple(sc_t(n, ti) for n in ("bet", "la", "g", "eg", "nbet", "begp", "glb", "kds", "egl", "t1"))
    for ti, (c0, nt, seq, pos) in enumerate(blk.tiles):
        nc2 = max(nt // 64, 1)
        g0 = 32 * ti
        sc = lambda n: sc_t(n, ti)
        for k in range(8):
            P.op("pe", lambda e, k=k: e.matmul(pb[1][:nt, g0:g0 + 8], self.hnT[:, k, c0:c0 + nt], wv8[:, k, :], start=(k == 0), stop=(k == 7)),
                 reads=[wb8, self.hnT], writes=[pb[1]], sig=(k == 7))
        (bet, hbet), (la, hla), (gg, hgg), (eg, heg), (nbet, hnbet), (begp, hbegp), (glb, hglb), (kds, hkds), (egl, hegl), (t1, ht1) = (
            sc(n) for n in ("bet", "la", "g", "eg", "nbet", "begp", "glb", "kds", "egl", "t1"))
        P.op("act", lambda e: e.activation(bet[:nt, :], pb[1][:nt, g0:g0 + 4], AF.Sigmoid), reads=[pb[1]], writes=[hbet])
        P.op("dve", lambda e: e.tensor_tensor(t1[:nt, :], pb[1][:nt, g0 + 4:g0 + 8], self.V("cdtb%d" % l)[:nt, :], ALU.add),
             reads=[pb[1], self.vecs], writes=[ht1])
        P.op("act", lambda e: e.activation(t1[:nt, :], t1[:nt, :], AF.Exp), reads=[ht1], writes=[ht1])
        P.op("act", lambda e: e.activation(t1[:nt, :], t1[:nt, :], AF.Ln, bias=1.0), reads=[ht1], writes=[ht1])
        P.op("dve", lambda e: e.tensor_tensor(la[:nt, :], t1[:nt, :], self.negA[:nt, 4 * l:4 * l + 4], ALU.mult),
             reads=[ht1, self.negA], writes=[hla])
        P.op("pe", lambda e: e.matmul(pb[1][:nt, g0 + 8:g0 + 12], self.C("u64")[:nt, :nt], la[:nt, :], start=True, stop=True),
             reads=[hla, self.cst], writes=[pb[1]])
        P.op("act", lambda e: e.activation(gg[:nt, :], pb[1][:nt, g0 + 8:g0 + 12], AF.Copy), reads=[pb[1]], writes=[hgg])
        P.op("act", lambda e: e.activation(eg[:nt, :], gg[:nt, :], AF.Exp), reads=[hgg], writes=[heg])
        sel = self.C("sel64")[:nt, :nt] if blk.kind == "P" else self.C("sel8")[:nt, :nt]
        P.op("pe", lambda e: e.matmul(pb[1][:nt, g0 + 12:g0 + 16], sel, gg[:nt, :], start=True, stop=True),
             reads=[hgg, self.cst], writes=[pb[1]])
        P.op("dve", lambda e: e.tensor_tensor(kds[:nt, :], pb[1][:nt, g0 + 12:g0 + 16], gg[:nt, :], ALU.subtract),
             reads=[pb[1], hgg], writes=[hkds])
        P.op("act", lambda e: e.activation(kds[:nt, :], kds[:nt, :], AF.Exp), reads=[hkds], writes=[hkds])
        P.op("dve", lambda e: e.tensor_scalar(nbet[:nt, :], bet[:nt, :], -1.0, None, ALU.mult), reads=[hbet], writes=[hnbet])
        P.op("dve", lambda e: e.tensor_tensor(begp[:nt, :], bet[:nt, :], eg[:nt, :], ALU.mult), reads=[hbet, heg], writes=[hbegp])
        selc = self.C("selc")[:nt, 0:nc2] if blk.kind == "P" else self.C("sel8")[:nt, 0:1]
        for pr in range(2):
            rep = scb[:, 768 + pr * 128:896 + pr * 128]
            P.op("dve", lambda e, pr=pr, rep=rep: e.tensor_copy(
                rep[:nt, :].rearrange("p (a b) -> p a b", a=2),
                eg[:nt, 2 * pr:2 * pr + 2].rearrange("p (a o) -> p a o", o=1).to_broadcast([nt, 2, 64])),
                reads=[heg], writes=[("sc", "rep", pr)])
            P.op("pe", lambda e, pr=pr, rep=rep: e.matmul(pb[1][:, g0 + 16 + 2 * pr:g0 + 16 + 2 * pr + nc2], rep[:nt, :], selc, start=True, stop=True),
                 reads=[("sc", "rep", pr), self.cst], writes=[pb[1]])
        P.op("act", lambda e: e.activation(egl[:, :], pb[1][:, g0 + 16:g0 + 20], AF.Copy), reads=[pb[1]], writes=[hegl])
        if self.o.get("Cstop") == 2:
            break
    for ti, (c0, nt, seq, pos) in enumerate(blk.tiles):
        nc2 = max(nt // 64, 1)
        (bet, hbet), (la, hla), (gg, hgg), (eg, heg), (nbet, hnbet), (begp, hbegp), (glb, hglb), (kds, hkds), (egl, hegl), (t1, ht1) = tile_scalars(ti)
        def hv(ap):
            return ap[:nt, 0:256].rearrange("p (h s) -> p h s", h=4)[:, :, :cs]
        def bc(ap2):
            return ap2.rearrange("p (o s) -> p o s", o=1).to_broadcast([nt, 4, cs])
        CH = [(c, h) for c in range(nc2) for h in range(4)]
        if self.o.get("Cdiag"):
            CH = [(c, h) for (c, h) in CH if h % 2 == c]
        if blk.kind == "P":
            if pos == 0:
                P.op("dve", lambda e: e.memset(SCw[:, :, :], 0.0), writes=[SCw])
            elif ti == 0:
                P.op("dve", lambda e: e.tensor_copy(SCw[:, :, :], SCp[:, :, :]), reads=[SCp], writes=[SCw])
        else:
            P.dma("sp", SCw[:, :, :], d["sc"].ap()[l, seq], writes=[SCw])
        if self.o.get("Cstop") == 1:
            break
        mask_state()
        for hh in range(2):
            P.op("act", lambda e, hh=hh: e.activation(kmt[hh * 64:(hh + 1) * 64, hh::2, :nt], XC[1][hh * 64:(hh + 1) * 64, :, c0:c0 + nt], AF.Copy),
                 reads=[Sh[1]], writes=[kmt])
        for g in (1, 2):
            for pr in range(2):
                o = (g - 1) * 256 + pr * 128
                P.op("pe", lambda e, g=g, pr=pr, o=o: e.transpose(pb[0][:nt, o:o + 128], XC[g][:, pr, c0:c0 + nt], ident),
                     reads=[Sh[g], self.cst], writes=[pb[0]])
        P.op("act", lambda e: e.activation(Sh[5][:nt, 0:512], pb[0][:nt, 0:512], AF.Copy), reads=[pb[0]], writes=[hkTM, hvTM])
        if ti == 0 and blk.idx == 0 and l == 0 and blk.kind == "P":
            self.dump("C_sc", scb[:, 0:64], [128, 64], [hbet, hla, hgg, heg, hnbet, hbegp, hkds, hegl])
        for h in range(4):
            bufU = scb[:, 256 + (h % 2) * 128:384 + (h % 2) * 128]
            bufL = scb[:, 512 + (h % 2) * 128:640 + (h % 2) * 128]
            P.op("dve", lambda e, h=h, bufU=bufU: e.tensor_scalar(bufU[:nt, :nt], self.C("u64")[:nt, :nt], la[:nt, h:h + 1], None, ALU.mult),
                 reads=[hla, self.cst], writes=[("sc", "U", h % 2)])
            P.op("pe", lambda e, h=h, bufU=bufU: e.matmul(pb[2][:nt, h * 64:h * 64 + cs], bufU[:nt, :nt], self.C("lsloc")[:nt, :cs],
                                                          start=True, stop=True), reads=[("sc", "U", h % 2), self.cst], writes=[pb[2]])
            P.op("dve", lambda e, h=h, bufL=bufL: e.tensor_scalar(bufL[:nt, :nt], self.C("l64s")[:nt, :nt], la[:nt, h:h + 1], None, ALU.mult),
                 reads=[hla, self.cst], writes=[("sc", "L", h % 2)])
            P.op("pe", lambda e, h=h, bufL=bufL: e.matmul(pb[3][:nt, h * 64:h * 64 + cs], bufL[:nt, :nt], self.C("uloc")[:nt, :cs],
                                                          start=True, stop=True), reads=[("sc", "L", h % 2), self.cst], writes=[pb[3]])
        P.op("act", lambda e: e.activation(hv(eD), hv(pb[2]), AF.Exp), reads=[pb[2]], writes=[heD])
        P.op("act", lambda e: e.activation(hv(eDT), hv(pb[3]), AF.Exp), reads=[pb[3]], writes=[heDT])
        for c, h in CH:
            pc, r0, pr = 64 * c, 64 * (h % 2), h // 2
            cc = c0 + 64 * c
            P.op("pe", lambda e, pc=pc, r0=r0, pr=pr, cc=cc, h=h: e.matmul(
                pb[5][pc:pc + cs, h * 64:h * 64 + cs], kmt[:, h, pc:pc + cs], XC[1][:, pr, cc:cc + cs],
                start=True, stop=True), reads=[Sh[1], kmt], writes=[pb[5]], sig=(c == nc2 - 1 and h == 3))
        for c, h in CH:
            pc, r0, pr = 64 * c, 64 * (h % 2), h // 2
            cc = c0 + 64 * c
            P.op("pe", lambda e, pc=pc, r0=r0, pr=pr, cc=cc, h=h: e.matmul(
                pb[6][pc:pc + cs, h * 64:h * 64 + cs], kmt[:, h, pc:pc + cs], XC[0][:, pr, cc:cc + cs],
                start=True, stop=True), reads=[kmt, Sh[0]], writes=[pb[6]], sig=(c == nc2 - 1 and h == 3))
        P.op("dve", lambda e: e.tensor_tensor(hv(eD), hv(pb[5]), hv(eD), ALU.mult), reads=[pb[5], heD], writes=[heD])
        for h in range(4):
            P.op("dve", lambda e, h=h: e.scalar_tensor_tensor(PA[:nt, h * 64:h * 64 + cs], eD[:nt, h * 64:h * 64 + cs], nbet[:nt, h:h + 1],
                                                             self.C("trilS")[:nt, :cs], ALU.mult, ALU.mult),
                 reads=[heD, hnbet, self.cst], writes=[hPA])
        P.op("dve", lambda e: e.tensor_tensor(hv(eDT), hv(pb[6]), hv(eDT), ALU.mult), reads=[pb[6], heDT], writes=[heDT])
        P.op("dve", lambda e: e.tensor_tensor(hv(eDT), hv(eDT), bc(self.C("triuI")[:nt, :cs]), ALU.mult),
             reads=[heDT, self.cst], writes=[heDT])
        if self.o.get("Cstop") == 3:
            break
        for c, h in CH:
            pc = 64 * c
            P.op("pe", lambda e, pc=pc, h=h: e.matmul(pb[7][pc:pc + cs, h * 64:h * 64 + cs], PA[pc:pc + cs, h * 64:h * 64 + cs],
                                                     ident[pc:pc + cs, pc:pc + cs], start=True, stop=True),
                 reads=[hPA, self.cst], writes=[pb[7]], sig=(c == nc2 - 1 and h == 3))
        P.op("act", lambda e: e.activation(hv(RA), hv(pb[7]), AF.Copy), reads=[pb[7]], writes=[hRA])
        P.op("dve", lambda e: e.tensor_tensor(hv(Nm), hv(RA), bc(self.C("id64")[:nt, :cs]), ALU.add), reads=[hRA, self.cst], writes=[hNm])
        if self.o.get("Cstop") == 4:
            break
        (Pc, hPc), (Rc, hRc), (Pn, hPn), (Rn, hRn) = (PA, hPA), (RA, hRA), (PB, hPB), (RB, hRB)
        for lev in range(1, nlev + 1):
            lastl = lev == nlev
            for c, h in CH:
                pc = 64 * c
                sl = (slice(pc, pc + cs), slice(h * 64, h * 64 + cs))
                P.op("pe", lambda e, sl=sl, Rc=Rc, Pc=Pc: e.matmul(pb[2][sl], Rc[sl], Pc[sl], start=True, stop=True),
                     reads=[hRc, hPc], writes=[pb[2]], sig=(c == nc2 - 1 and h == 3))
            if not lastl:
                for c, h in CH:
                    pc = 64 * c
                    sl = (slice(pc, pc + cs), slice(h * 64, h * 64 + cs))
                    P.op("pe", lambda e, sl=sl, Rc=Rc, Pc=Pc: e.matmul(pb[3][sl], Pc[sl], Rc[sl], start=True, stop=True),
                         reads=[hRc, hPc], writes=[pb[3]], sig=(c == nc2 - 1 and h == 3))
            P.op("act", lambda e, Pn=Pn: e.activation(hv(Pn), hv(pb[2]), AF.Copy), reads=[pb[2]], writes=[hPn])
            if not lastl:
                P.op("act", lambda e, Rn=Rn: e.activation(hv(Rn), hv(pb[3]), AF.Copy), reads=[pb[3]], writes=[hRn])
            for c, h in CH:
                pc = 64 * c
                sl = (slice(pc, pc + cs), slice(h * 64, h * 64 + cs))
                P.op("pe", lambda e, sl=sl, Pn=Pn: e.matmul(pb[5][sl], Pn[sl], Nm[sl], start=True, stop=True),
                     reads=[hPn, hNm], writes=[pb[5]], sig=(c == nc2 - 1 and h == 3))
            P.op("dve", lambda e: e.tensor_tensor(hv(Nm), hv(Nm), hv(pb[5]), ALU.add), reads=[hNm, pb[5]], writes=[hNm])
            (Pc, hPc), (Rc, hRc), (Pn, hPn), (Rn, hRn) = (Pn, hPn), (Rn, hRn), (Pc, hPc), (Rc, hRc)
        if self.o.get("Cstop") == 5:
            break
        if ti == 0 and blk.idx == 0 and l == 0 and blk.kind == "P":
            self.dump("C_N", Nm, [128, 256], [hNm])
            self.dump("C_qk", eDT, [128, 256], [heDT])
        for h in range(4):
            hs = slice(h * 64, (h + 1) * 64)
            P.op("dve", lambda e, h=h, hs=hs: e.tensor_scalar(bv[:nt, hs], vTM[:nt, hs], bet[:nt, h:h + 1], None, ALU.mult),
                 reads=[hvTM, hbet], writes=[hbv])
            P.op("dve", lambda e, h=h, hs=hs: e.tensor_scalar(kbg[:nt, hs], kTM[:nt, hs], begp[:nt, h:h + 1], None, ALU.mult),
                 reads=[hkTM, hbegp], writes=[hkbg])
            P.op("dve", lambda e, h=h, hs=hs: e.tensor_scalar(kd[:nt, hs], kTM[:nt, hs], kds[:nt, h:h + 1], None, ALU.mult),
                 reads=[hkTM, hkds], writes=[hkd])
        for c in range(nc2):
            pc = 64 * c
            P.op("act", lambda e, c=c, pc=pc: e.activation(kbgm[pc:pc + cs, c, :], kbg[pc:pc + cs, :], AF.Copy), reads=[hkbg], writes=[kbgm])
            P.op("act", lambda e, c=c, pc=pc: e.activation(kdm[pc:pc + cs, c, :], kd[pc:pc + cs, :], AF.Copy), reads=[hkd], writes=[kdm])
        for c, h in CH:
            pc = 64 * c
            P.op("pe", lambda e, pc=pc, h=h: e.matmul(pb[6][pc:pc + cs, h * 64:(h + 1) * 64], Nm[pc:pc + cs, h * 64:h * 64 + cs],
                                                      bv[pc:pc + cs, h * 64:(h + 1) * 64], start=True, stop=True),
                 reads=[hNm, hbv], writes=[pb[6]], sig=(c == nc2 - 1 and h == 3))
        P.op("act", lambda e: e.activation(un[:nt, :], pb[6][:nt, 0:256], AF.Copy), reads=[pb[6]], writes=[hun])
        for c, h in CH:
            pc, r0, pr = 64 * c, 64 * (h % 2), h // 2
            P.op("pe", lambda e, pc=pc, r0=r0, pr=pr, h=h, c=c: e.matmul(
                pb[7][r0:r0 + 64, pr * 128 + 64 * c:pr * 128 + 64 * c + cs], kbgm[:, c, h * 64:(h + 1) * 64],
                Nm[:, h * 64:h * 64 + cs], start=True, stop=True),
                reads=[hNm, kbgm], writes=[pb[7]], sig=(c == nc2 - 1 and h == 3))
        P.op("act", lambda e: e.activation(wT[:, :], pb[7][:, 0:256], AF.Copy), reads=[pb[7]], writes=[hwT])
        if self.o.get("Cstop") == 6:
            break
        for c in range(nc2):
            pc = 64 * c
            rows = slice(pc, pc + cs)
            cc = c0 + 64 * c
            for h in range(4):
                r0, pr = 64 * (h % 2), h // 2
                P.op("pe", lambda e, h=h, r0=r0, pr=pr: e.matmul(
                    pb[0][rows, h * 64:(h + 1) * 64], wT[:, pr * 128 + pc:pr * 128 + pc + cs], SCm[:, h, :],
                    start=True, stop=True), reads=[hwT, SCm], writes=[pb[0]], sig=(h == 3))
            P.op("dve", lambda e: e.tensor_tensor(un[rows, :], un[rows, :], pb[0][rows, 0:256], ALU.subtract),
                 reads=[hun, pb[0]], writes=[hun])
            for h in range(4):
                r0, pr = 64 * (h % 2), h // 2
                P.op("pe", lambda e, h=h, r0=r0, pr=pr: e.matmul(
                    pb[2][rows, h * 64:(h + 1) * 64], XC[0][:, pr, cc:cc + cs], SCm[:, h, :],
                    start=True, stop=True), reads=[Sh[0], SCm], writes=[pb[2]], sig=(h == 3))
            for h in range(4):
                P.op("act", lambda e, h=h: e.activation(o1s[rows, h * 64:(h + 1) * 64], pb[2][rows, h * 64:(h + 1) * 64], AF.Copy,
                                                        scale=eg[rows, h:h + 1]), reads=[pb[2], heg], writes=[ho1s])
            for h in range(4):
                P.op("pe", lambda e, h=h: e.matmul(pb[3][rows, h * 64:(h + 1) * 64], eDT[rows, h * 64:h * 64 + cs],
                                                   un[rows, h * 64:(h + 1) * 64], start=True, stop=True),
                     reads=[heDT, hun], writes=[pb[3]], sig=(h == 3))
            P.op("dve", lambda e: e.tensor_tensor(oTM[rows, :], o1s[rows, :], pb[3][rows, 0:256], ALU.add),
                 reads=[ho1s, pb[3]], writes=[hoTM])
            for h in range(4):
                r0, pr = 64 * (h % 2), h // 2
                P.op("pe", lambda e, h=h, r0=r0, pr=pr: e.matmul(
                    pb[5][r0:r0 + 64, pr * 64:(pr + 1) * 64], kdm[:, c, h * 64:(h + 1) * 64], un[:, h * 64:(h + 1) * 64],
                    start=True, stop=True), reads=[kdm, hun], writes=[pb[5]], sig=(h == 3))
            for pr in range(2):
                P.op("dve", lambda e, pr=pr, c=c: e.scalar_tensor_tensor(
                    SCw[:, pr, :], SCw[:, pr, :], egl[:, 2 * pr + c:2 * pr + c + 1], pb[5][:, pr * 64:(pr + 1) * 64], ALU.mult, ALU.add),
                    reads=[SCw, hegl, pb[5]], writes=[SCw])
            if c < nc2 - 1:
                mask_state()
        if self.o.get("Cstop") == 7:
            break
        for pr in range(2):
            P.op("pe", lambda e, pr=pr: e.transpose(pb[6][:, pr * 128:pr * 128 + nt], oTM[:nt, pr * 128:(pr + 1) * 128], ident[:nt, :nt]),
                 reads=[hoTM, self.cst], writes=[pb[6]])
            P.op("act", lambda e, pr=pr: e.activation(OC[:, pr, c0:c0 + nt], pb[6][:, pr * 128:pr * 128 + nt], AF.Copy),
                 reads=[pb[6]], writes=[Sh[4]])
        last_of_seq = (blk.kind == "S") or (blk.last and ti == len(blk.tiles) - 1)
        if last_of_seq:
            dst = self.od["pc"].ap()[l] if blk.kind == "P" else self.od["sco"].ap()[l, seq]
            P.dma("act", dst, SCw[:, :, :], reads=[SCw], key="sc_out", out=True)
        elif ti == len(blk.tiles) - 1:
            P.op("dve", lambda e: e.tensor_copy(SCp[:, :, :], SCw[:, :, :]), reads=[SCw], writes=[SCp])
    if blk.idx == 0 and l == 0 and blk.kind == "P":
        self.dump("C_o", OC, [128, 2, N], [Sh[4]])
    self.head_norm_gate(OC, Sh[4], SGz, Sh[3], self.V("cng%d" % l), 4, N, gcol=True)


Builder.mixer_C = _mixer_C


_OPTS = dict(mixers="ABCD")


def kernel(**inputs):
    n_cores = 8
    nc = bass.Bass("TRN2", target_bir_lowering=False)
    Builder(nc, dict(_OPTS)).run()
    sh = host_shared(inputs)
    in_maps = []
    for c in range(n_cores):
        m = host_core(inputs, c)
        m.update(sh)
        in_maps.append(m)
    res = run_bass_kernel_spmd(nc, in_maps, core_ids=list(range(n_cores)))
    return host_gather(res.results)
```

```python
import numpy as np
from contextlib import ExitStack
import concourse.bass as bass
import concourse.mybir as mybir
from concourse.bass_utils import run_bass_kernel_spmd

F32 = mybir.dt.float32
BF16 = mybir.dt.bfloat16
I32 = mybir.dt.int32
AF = mybir.ActivationFunctionType
ALU = mybir.AluOpType
AX = mybir.AxisListType


def _hkey(h):
    if isinstance(h, (tuple, str, int)):
        return h
    n = getattr(h, "name", None)
    if n is not None:
        return ("t", n)
    return ("id", id(h))


class Prog:
    ENG = ("pe", "act", "dve", "pool", "sp")

    def __init__(self, nc):
        self.nc = nc
        self.es = ExitStack()
        self.eng = {"pe": nc.tensor, "act": nc.scalar, "dve": nc.vector,
                    "pool": nc.gpsimd, "sp": nc.sync}
        self.sem = {e: self.es.enter_context(nc.semaphore("s_" + e)) for e in ("pe", "act", "dve", "pool")}
        self.cnt = {e: 0 for e in self.sem}
        self.clock = {e: {} for e in self.ENG}
        self.last_w = {}
        self.readers = {}
        self.tok_clock = {}
        self.dsem = {}
        self.dcnt = {}
        self.out_keys = set()
        self.nwait = 0
        self.pe_pending = False
        self.psum_keys = set()
        self.nops = 0

    def sbuf(self, name, shape, dtype):
        return self.es.enter_context(self.nc.sbuf_tensor("sb_" + name, list(shape), dtype))

    def psum(self, name, shape, dtype):
        t = self.es.enter_context(self.nc.psum_tensor("ps_" + name, list(shape), dtype))
        self.psum_keys.add(_hkey(t))
        return t

    def _covered(self, clk, tok):
        return clk.get(tok[0], 0) >= tok[1]

    def _merge(self, clk, other):
        for k, v in other.items():
            if clk.get(k, 0) < v:
                clk[k] = v

    def _deps(self, reads, writes, ename=None):
        deps = []
        for h in list(reads) + list(writes):
            t = self.last_w.get(_hkey(h))
            if t is not None:
                deps.append(t)
        for h in reads:
            k = _hkey(h)
            if k in self.psum_keys:
                deps.extend(t for t in self.readers.get(k, ()) if t[0] != ename)
        for h in writes:
            deps.extend(self.readers.get(_hkey(h), ()))
        return deps

    def _wait(self, ename, deps):
        e = self.eng[ename]
        clk = self.clock[ename]
        pend = []
        best = {}
        for t in deps:
            if ename == "pe" and t[0] == "pe":
                continue
            if self._covered(clk, t):
                continue
            if best.get(t[0], 0) < t[1]:
                best[t[0]] = t[1]
        for k, v in best.items():
            if clk.get(k, 0) >= v:
                continue
            s = self.sem[k] if k in self.sem else self.dsem[k]
            if k == "pe" and v > self.cnt["pe"]:
                raise RuntimeError("wait on un-signalled PE op (mark the producer sig=True)")
            pend.append((s, v))
            self.nwait += 1
            self._merge(clk, self.tok_clock[(k, v)])
        for s, v in pend[:-1]:
            e.wait_ge(s, v)
            if len(pend) > 2:
                e.nop()
        return pend[-1] if pend else None

    def _commit(self, tok, ename, reads, writes):
        c = dict(self.clock[ename])
        c[tok[0]] = max(c.get(tok[0], 0), tok[1])
        self.tok_clock[tok] = c
        for h in writes:
            k = _hkey(h)
            self.last_w[k] = tok
            self.readers[k] = []
        for h in reads:
            self.readers.setdefault(_hkey(h), []).append(tok)

    def op(self, ename, fn, reads=(), writes=(), sig=True):
        lastw = self._wait(ename, self._deps(reads, writes, ename))
        ins = fn(self.eng[ename])
        if lastw is not None:
            ins._wait_ge(lastw[0], lastw[1])
        self.nops += 1
        if ename != "pe":
            sig = True
        if sig:
            self.cnt[ename] += 1
            ins.then_inc(self.sem[ename], 1)
            tok = (ename, self.cnt[ename])
            if ename == "pe":
                self.pe_pending = False
        else:
            tok = (ename, self.cnt[ename] + 1)
            self.pe_pending = True
        self._commit(tok, ename, reads, writes)
        return ins

    def dma(self, qname, out_ap, in_ap, reads=(), writes=(), key=None, out=False, **kw):
        if key is None:
            hs = list(writes) if writes else list(reads)
            key = ("dma",) + tuple(_hkey(h) for h in hs[:1])
        key = ("d", key)
        if key not in self.dsem:
            self.dsem[key] = self.es.enter_context(self.nc.semaphore("d%d" % len(self.dsem)))
            self.dcnt[key] = 0
        lastw = self._wait(qname, self._deps(reads, writes, qname))
        ins = self.eng[qname].dma_start(out=out_ap, in_=in_ap, **kw)
        if lastw is not None:
            ins._wait_ge(lastw[0], lastw[1])
        self.dcnt[key] += 16
        ins.then_inc(self.dsem[key], 16)
        tok = (key, self.dcnt[key])
        self._commit(tok, qname, reads, writes)
        if out:
            self.out_keys.add(key)
        return ins

    def finish(self):
        sp = self.eng["sp"]
        for key in self.out_keys:
            sp.wait_ge(self.dsem[key], self.dcnt[key])
        for e in ("pe", "act", "dve", "pool"):
            if self.cnt[e]:
                sp.wait_ge(self.sem[e], self.cnt[e])
        self.es.close()


D = 1024
DIN = 3336
DFF = 2816
NFF = 22
NTOK = 2048
NBLK = 512
NS = 4
LS = 8
EPS = 1e-6
COLS = dict(aq=0, af=256, ai=512, ag=768, bq=1024, bk=1280, bv=1536, cq=1792, ck=2048,
            cv=2304, cz=2560, cb=2816, ca=2820, dx=2824, dg=3080)
TABL = 2304
ZW = 2560

VEC = {}
_o = 0
def _v(name, n):
    global _o
    VEC[name] = (_o, n)
    _o += n
for _l in range(2):
    for _n, _c in (("n1g", 8), ("n2g", 8), ("png", 8), ("ang", 2), ("bng", 2), ("cng", 1),
                   ("ccw", 24), ("dcw", 8), ("dcb", 2), ("dba", 2), ("dbx", 2), ("dlam", 2),
                   ("dng", 2), ("calog", 4), ("cdtb", 4)):
        _v("%s%d" % (_n, _l), _c)
_v("fng", 8)
_v("alb0", 2)
_v("alb1", 2)
NV = _o

CST = {}
_o = 0
def _c(name, n):
    global _o
    CST[name] = (_o, n)
    _o += n
_c("ident", 128)
_c("onesbd", 128)
_c("mask16T", 128)
_c("blk16", 8)
_c("reset16", 512)
_c("resetS", 32)
_c("uloc", 64)
_c("lsloc", 64)
_c("trilS", 64)
_c("triuI", 64)
_c("id64", 64)
_c("u64", 128)
_c("l64s", 128)
_c("sel64", 128)
_c("selc", 2)
_c("sel8", 8)
NC_ = _o


def make_consts():
    c = np.zeros((128, NC_), np.float32)
    p = np.arange(128)[:, None]
    def put(name, arr):
        o, n = CST[name]
        c[:, o:o + n] = arr
    t = np.arange(128)[None, :]
    put("ident", (p == t))
    put("onesbd", (p // 64 == t // 64))
    put("mask16T", (p // 16 == t // 16) & (p <= t))
    put("blk16", (p // 16 == np.arange(8)[None, :]))
    put("reset16", np.broadcast_to((np.arange(512)[None, :] % 16 != 0), (128, 512)))
    put("resetS", np.broadcast_to((np.arange(32)[None, :] % 8 != 0), (128, 32)))
    pl = p % 64
    s = np.arange(64)[None, :]
    put("uloc", pl <= s)
    put("lsloc", pl > s)
    put("trilS", pl > s)
    put("triuI", pl <= s)
    put("id64", pl == s)
    put("u64", (p // 64 == t // 64) & (p <= t))
    put("l64s", (p // 64 == t // 64) & (p > t))
    put("sel64", p == (t // 64) * 64 + 63)
    put("selc", p == np.arange(2)[None, :] * 64 + 63)
    put("sel8", np.broadcast_to(p == 7, (128, 8)))
    return c


def make_disttab():
    import math
    M = np.zeros((32, TABL), np.float32)
    for u in range(TABL):
        d = u - 127
        if d < 0 or d > 2048:
            continue
        mult = 0
        if d <= 128:
            mult += 1
        if d <= 512 and d % 4 == 0:
            mult += 1
        if d <= 2048 and d % 16 == 0:
            mult += 1
        if mult == 0:
            continue
        if d < 16:
            b = d
        else:
            v = np.float32(np.log(np.float32(max(d, 1)) / np.float32(16.0))) / np.float32(math.log(2048 / 16)) * np.float32(16)
            b = min(16 + int(np.float32(v)), 31)
        M[b, u] = mult
    return M


def fm(v):
    v = np.asarray(v, np.float32)
    return np.ascontiguousarray(v.reshape(-1, 128).T)


def host_shared(inp):
    sh = {}
    f32 = lambda a: np.ascontiguousarray(np.asarray(a, np.float32))
    w_in = f32(inp["w_in"])
    sh["w_in_t"] = f32(w_in.reshape(2, 8, 128, DIN).transpose(0, 2, 1, 3))
    sh["w_out_t"] = f32(f32(inp["w_out"]).reshape(2, 8, 128, D).transpose(0, 2, 1, 3))
    wfi = f32(inp["w_ffn_in"]).reshape(2, 8, 128, 2, NFF, 128)
    sh["w_ffi_t"] = f32(wfi.transpose(0, 4, 2, 1, 3, 5).reshape(2, NFF, 128, 8, 256))
    sh["w_ffo_t"] = f32(f32(inp["w_ffn_out"]).reshape(2, NFF, 128, D).transpose(0, 2, 1, 3))
    sh["w_ple_t"] = f32(f32(inp["w_ple"]).reshape(2, 2, 128, D).transpose(0, 2, 1, 3))
    sh["w_gate_t"] = f32(f32(inp["w_ple_gate"]).reshape(2, 8, 128, D).transpose(0, 2, 1, 3))
    vecs = np.zeros((128, NV), np.float32)
    def put(name, arr):
        o, n = VEC[name]
        vecs[:, o:o + n] = arr
    for l in range(2):
        put("n1g%d" % l, fm(inp["norm1_g"][l]))
        put("n2g%d" % l, fm(inp["norm2_g"][l]))
        put("png%d" % l, fm(inp["ple_norm_g"][l]))
        put("ang%d" % l, fm(inp["a_norm_g"][l]))
        put("bng%d" % l, fm(inp["b_norm_g"][l]))
        put("cng%d" % l, np.tile(np.asarray(inp["c_norm_g"][l], np.float32), 2)[:, None])
        ccw = np.asarray(inp["c_conv_w"][l], np.float32)
        put("ccw%d" % l, ccw.reshape(4, 6, 128).transpose(2, 1, 0).reshape(128, 24))
        dcw = np.asarray(inp["d_conv_w"][l], np.float32)
        put("dcw%d" % l, dcw.reshape(4, 2, 128).transpose(2, 1, 0).reshape(128, 8))
        put("dcb%d" % l, fm(inp["d_conv_b"][l]))
        put("dba%d" % l, fm(inp["d_ba"][l]))
        put("dbx%d" % l, fm(inp["d_bx"][l]))
        put("dlam%d" % l, fm(inp["d_lambda"][l]))
        put("dng%d" % l, fm(inp["d_norm_g"][l]))
        put("calog%d" % l, np.broadcast_to(np.asarray(inp["c_a_log"][l], np.float32)[None, :], (128, 4)))
        put("cdtb%d" % l, np.broadcast_to(np.asarray(inp["c_dt_bias"][l], np.float32)[None, :], (128, 4)))
    put("fng", fm(inp["final_norm_g"]))
    put("alb0", fm(inp["a_lb"][0]))
    put("alb1", fm(inp["a_lb"][1]))
    sh["vecs"] = vecs
    dgw = np.zeros((128, 2, 2, 2, 128), np.float32)
    for l in range(2):
        for wi, nm in enumerate(("d_wa", "d_wx")):
            w = np.asarray(inp[nm][l], np.float32)
            for h in range(4):
                r = (h % 2) * 64
                dgw[r:r + 64, l, wi, h // 2, r:r + 64] = w[h]
    sh["dgw"] = dgw.reshape(128, 2 * 2 * 2 * 128)
    sh["relb"] = f32(inp["rel_bias"])
    sh["cst"] = make_consts()
    sh["cM"] = make_disttab()
    return sh


def host_core(inp, c):
    f32 = lambda a: np.ascontiguousarray(np.asarray(a, np.float32))
    m = {}
    x = f32(inp["x_prompt"][c])
    m["xT"] = f32(x.reshape(NTOK, 8, 128).transpose(2, 1, 0))
    p = f32(inp["p_prompt"][:, c])
    m["pT"] = f32(p.reshape(2, NTOK, 2, 128).transpose(0, 3, 2, 1))
    sl = slice(NS * c, NS * c + NS)
    xs = f32(inp["x_sample"][sl]).reshape(NS * LS, 8, 128)
    m["xsT"] = f32(xs.transpose(2, 1, 0))
    ps = f32(inp["p_sample"][:, sl]).reshape(2, NS * LS, 2, 128)
    m["psT"] = f32(ps.transpose(0, 3, 2, 1))
    ck = f32(inp["cache_b_k"][:, sl])
    ck = ck.reshape(2, NS, 2048, 2, 2, 64)
    m["kcT"] = f32(ck.transpose(0, 1, 4, 5, 3, 2).reshape(2, NS, 128, 2, 2048))
    cv = f32(inp["cache_b_v"][:, sl]).reshape(2, NS, 16, 128, 256)
    m["vc"] = f32(cv.transpose(0, 1, 3, 2, 4))
    def st(a):
        a = f32(a[:, sl]).reshape(2, NS, 2, 2, 64, 64)
        return f32(a.transpose(0, 1, 3, 4, 2, 5).reshape(2, NS, 128, 2, 64))
    m["sa"] = st(inp["state_a"])
    m["sc"] = st(inp["state_c"])
    scc = f32(inp["state_c_conv"][:, sl]).reshape(2, NS, 3, 6, 128)
    m["scc"] = f32(scc.transpose(0, 4, 3, 1, 2))
    sdc = f32(inp["state_d_conv"][:, sl]).reshape(2, NS, 3, 2, 128)
    m["sdc"] = f32(sdc.transpose(0, 4, 3, 1, 2))
    sdh = f32(inp["state_d_h"][:, sl]).reshape(2, NS, 2, 128)
    m["sdh"] = f32(sdh.transpose(0, 3, 2, 1))
    return m


OUT_SPECS = {
    "yT": [128, 8, NTOK], "ysT": [128, 8, NS * LS],
    "pbk": [2, NTOK, 256], "pbv": [2, NTOK, 256],
    "pa": [2, 128, 2, 64], "pc": [2, 128, 2, 64],
    "pcc": [2, 128, 6, 3], "pdh": [2, 128, 2], "pdc": [2, 128, 2, 3],
    "sbk": [2, NS * LS, 256], "sbv": [2, NS * LS, 256],
    "sao": [2, NS, 128, 2, 64], "sco": [2, NS, 128, 2, 64],
    "scco": [2, 128, 6, NS, 3], "sdho": [2, 128, 2, NS], "sdco": [2, 128, 2, NS, 3],
}


def host_gather(res):
    nco = len(res)
    def unst(a):
        a = a.reshape(2, 2, 64, 2, 64)
        return a.transpose(0, 3, 1, 2, 4).reshape(2, 4, 64, 64)
    y = np.stack([r["yT"].transpose(2, 1, 0).reshape(NTOK, D) for r in res])
    ys = np.concatenate([r["ysT"].transpose(2, 1, 0).reshape(NS, LS, D) for r in res])
    pbk = np.stack([r["pbk"].reshape(2, NTOK, 4, 64) for r in res], 1)
    pbv = np.stack([r["pbv"].reshape(2, NTOK, 4, 64) for r in res], 1)
    pa = np.stack([unst(r["pa"]) for r in res], 1)
    pc = np.stack([unst(r["pc"]) for r in res], 1)
    pcc = np.stack([r["pcc"].transpose(0, 3, 2, 1).reshape(2, 3, 768) for r in res], 1)
    pdh = np.stack([r["pdh"].transpose(0, 2, 1).reshape(2, 256) for r in res], 1)
    pdc = np.stack([r["pdc"].transpose(0, 3, 2, 1).reshape(2, 3, 256) for r in res], 1)
    sbk = np.concatenate([r["sbk"].reshape(2, NS, LS, 4, 64) for r in res], 1)
    sbv = np.concatenate([r["sbv"].reshape(2, NS, LS, 4, 64) for r in res], 1)
    sao = np.concatenate([np.stack([unst(r["sao"][:, s]) for s in range(NS)], 1) for r in res], 1)
    sco = np.concatenate([np.stack([unst(r["sco"][:, s]) for s in range(NS)], 1) for r in res], 1)
    scco = np.concatenate([r["scco"].transpose(0, 3, 4, 2, 1).reshape(2, NS, 3, 768) for r in res], 1)
    sdho = np.concatenate([r["sdho"].transpose(0, 3, 2, 1).reshape(2, NS, 256) for r in res], 1)
    sdco = np.concatenate([r["sdco"].transpose(0, 3, 4, 2, 1).reshape(2, NS, 3, 256) for r in res], 1)
    outs = (y, ys, pbk, pbv, pa, pc, pcc, pdh, pdc, sbk, sbv, sao, sco, scco, sdho, sdco)
    return tuple(np.ascontiguousarray(o, dtype=np.float32) for o in outs)


class Blk:
    def __init__(self, kind, idx):
        self.kind = kind
        self.idx = idx
        if kind == "P":
            self.N, self.nseq, self.L = NBLK, 1, NBLK
            self.tok0 = idx * NBLK
            self.tiles = [(i * 128, 128, 0, idx * 4 + i) for i in range(4)]
        else:
            self.N, self.nseq, self.L = NS * LS, NS, LS
            self.tok0 = 0
            self.tiles = [(s * LS, LS, s, 0) for s in range(NS)]
        self.first = (kind == "S") or idx == 0
        self.last = (kind == "S") or idx == NTOK // NBLK - 1


class Builder:
    def __init__(self, nc, opts=None):
        self.nc = nc
        self.o = dict(mixers="ABCD", dense=True, blocks=None, layers=2, dump=())
        if opts:
            self.o.update(opts)
        self.P = Prog(nc)
        self.dumps = {}
        self.decl()
        self.alloc()

    def decl(self):
        nc = self.nc
        I = lambda n, s: nc.dram_tensor(n, list(s), F32, kind="ExternalInput")
        self.d = {}
        for n, s in (("xT", [128, 8, NTOK]), ("pT", [2, 128, 2, NTOK]), ("xsT", [128, 8, NS * LS]),
                     ("psT", [2, 128, 2, NS * LS]), ("kcT", [2, NS, 128, 2, 2048]),
                     ("vc", [2, NS, 128, 16, 256]), ("sa", [2, NS, 128, 2, 64]), ("sc", [2, NS, 128, 2, 64]),
                     ("scc", [2, 128, 6, NS, 3]), ("sdc", [2, 128, 2, NS, 3]), ("sdh", [2, 128, 2, NS]),
                     ("w_in_t", [2, 128, 8, DIN]), ("w_out_t", [2, 128, 8, D]),
                     ("w_ffi_t", [2, NFF, 128, 8, 256]), ("w_ffo_t", [2, 128, NFF, D]),
                     ("w_ple_t", [2, 128, 2, D]), ("w_gate_t", [2, 128, 8, D]),
                     ("vecs", [128, NV]), ("dgw", [128, 1024]), ("relb", [32, 4]),
                     ("cst", [128, NC_]), ("cM", [32, TABL])):
            self.d[n] = I(n, s)
        self.od = {n: nc.dram_tensor(n, list(s), F32, kind="ExternalOutput") for n, s in OUT_SPECS.items()}
        self.zt = nc.dram_tensor("ztab", [4, 128, ZW], BF16, kind="Internal")
        self.wcache = nc.dram_tensor("wcache", [128, 225408], BF16, kind="Internal")
        self.wc_map = {}
        self.wc_off = 0

    def dump(self, name, ap, shape, reads):
        if name not in self.o["dump"] or name in self.dumps:
            return
        t = self.nc.dram_tensor("dbg_" + name, list(shape), ap.dtype, kind="ExternalOutput")
        self.dumps[name] = t
        self.P.dma("act", t.ap(), ap, reads=reads, key="dbg_" + name, out=True)

    def alloc(self):
        P = self.P
        self.cst = P.sbuf("cst", [128, NC_], F32)
        self.vecs = P.sbuf("vecs", [128, NV], F32)
        self.dgw = P.sbuf("dgw", [128, 512], F32)
        self.ones_bf = P.sbuf("ones_bf", [128, 128], BF16)
        self.drv = P.sbuf("drv", [128, 32], F32)
        self.E = P.sbuf("E", [128, 4, TABL], BF16)
        self.qT = P.sbuf("qT", [128, 4, NBLK], BF16)
        self.PT = [P.sbuf("PT%d" % i, [128, 512], BF16) for i in range(2)]
        self.Vnew = P.sbuf("Vnew", [128, NS, 256], BF16)
        self.hT = P.sbuf("hT", [128, 8, NBLK], F32)
        self.hnT = P.sbuf("hnT", [128, 8, NBLK], BF16)
        self.mixT = P.sbuf("mixT", [128, 8, NBLK], BF16)
        self.actT = P.sbuf("actT", [128, NFF // 2, NBLK], BF16)
        self.KT = [P.sbuf("KT%d" % l, [128, 2, NTOK + 128], BF16) for l in range(2)]
        self.Vh = [P.sbuf("Vh%d" % l, [128, 17, 256], BF16) for l in range(2)]
        self.wstg = [P.sbuf("wstg%d" % i, [128, 2048], F32) for i in range(2)]
        self.wbf = [P.sbuf("wbf%d" % i, [128, 2048], BF16) for i in range(3)]
        self.wple = P.sbuf("wple", [128, 2048], BF16)
        self.rt = P.sbuf("rt", [128, NBLK], F32)
        self.sqb = [P.sbuf("sqb%d" % i, [128, NBLK], BF16) for i in range(2)]
        self.S = [P.sbuf("S%d" % i, [128, 1024], F32) for i in range(10)]
        self.pb = [P.psum("pb%d" % i, [128, 512], F32) for i in range(8)]
        self.wi = 0
        self.wj = 0
        self.dtail = [P.sbuf("dtail%d" % l, [128, 2, 3], F32) for l in range(2)]
        self.dhst = [P.sbuf("dhst%d" % l, [128, 2, NS], F32) for l in range(2)]
        self.dext = P.sbuf("dext", [128, 2, NBLK + 3], F32)
        self.SAw = P.sbuf("SAw", [128, 2, 9, 64], F32)
        self.SAp = [P.sbuf("SAp%d" % l, [128, 2, 64], F32) for l in range(2)]
        self.SCp = [P.sbuf("SCp%d" % l, [128, 2, 64], F32) for l in range(2)]
        self.ctail = [P.sbuf("ctail%d" % l, [128, 6, 3], F32) for l in range(2)]
        self.negA = P.sbuf("negA", [128, 8], F32)
        self.SCw = P.sbuf("SCw", [128, 2, 64], F32)
        self.SCm = P.sbuf("SCm", [128, 4, 64], F32)
        self.kmt = P.sbuf("kmt", [128, 4, 128], F32)
        self.kbgm = P.sbuf("kbgm", [128, 2, 256], F32)
        self.kdm = P.sbuf("kdm", [128, 2, 256], F32)

    def V(self, name):
        o, n = VEC[name]
        return self.vecs[:, o:o + n]

    def C(self, name):
        o, n = CST[name]
        return self.cst[:, o:o + n]

    def wload(self, src_ap, shape, key):
        P = self.P
        n = int(np.prod(shape[1:]))
        bufs = self.wbf if getattr(self, "in_ple", False) else self.wbf + [self.wple]
        wb = bufs[self.wj % len(bufs)]
        self.wj += 1
        bv = wb[:, :n]
        if len(shape) == 3:
            bv = bv.rearrange("p (a b) -> p a b", a=shape[1])
        if key not in self.wc_map:
            off = self.wc_off
            self.wc_map[key] = off
            self.wc_off += n
            stg = self.wstg[self.wi % len(self.wstg)]
            self.wi += 1
            sv = stg[:, :n]
            if len(shape) == 3:
                sv = sv.rearrange("p (a b) -> p a b", a=shape[1])
            P.dma("sp", sv, src_ap, writes=[stg])
            P.op("act", lambda e: e.activation(wb[:, :n], stg[:, :n], AF.Copy), reads=[stg], writes=[wb])
            P.dma("act", self.wcache.ap()[:, off:off + n], wb[:, :n], reads=[wb], writes=[("wc", key)],
                  key=("wcw",) + tuple(_hkey(wb)))
        else:
            off = self.wc_map[key]
            P.dma("sp", wb[:, :n], self.wcache.ap()[:, off:off + n], reads=[("wc", key)], writes=[wb])
        return wb, bv

    def w_in_unit(self, l, c0, n):
        return self.wload(self.d["w_in_t"].ap()[l][:, :, c0:c0 + n], [128, 8, n], ("w_in", l, c0))

    def proj_fm(self, wb, wv, j0, m, out_ps, N, handle):
        P = self.P
        for k in range(8):
            P.op("pe", lambda e, k=k: e.matmul(out_ps[:m, :N], wv[:, k, j0:j0 + m], self.hnT[:, k, :N],
                                              start=(k == 0), stop=(k == 7)),
                 reads=[wb, ("hnT", k)], writes=[handle], sig=(k == 7))

    def rmsnorm_fm(self, src, gv, dst, N, srch, dsth):
        P = self.P
        ps = self.pb[4]
        for k in range(8):
            sq = self.sqb[k % 2]
            P.op("act", lambda e, k=k, sq=sq: e.activation(sq[:, :N], src[:, k, :N], AF.Square),
                 reads=[srch], writes=[sq])
            P.op("pe", lambda e, k=k, sq=sq: e.matmul(ps[:, :N], self.ones_bf[:], sq[:, :N],
                                                      start=(k == 0), stop=(k == 7)),
                 reads=[sq, self.ones_bf], writes=[ps])
        rt = self.rt
        P.op("act", lambda e: e.activation(rt[:, :N], ps[:, :N], AF.Ln, bias=EPS, scale=1.0 / D),
             reads=[ps], writes=[rt])
        P.op("act", lambda e: e.activation(rt[:, :N], rt[:, :N], AF.Exp, scale=-0.5),
             reads=[rt], writes=[rt])
        for k in range(8):
            P.op("dve", lambda e, k=k: e.scalar_tensor_tensor(dst[:, k, :N], src[:, k, :N], gv[:, k:k + 1],
                                                             rt[:, :N], ALU.mult, ALU.mult),
                 reads=[srch, rt, self.vecs], writes=[dsth])

    def setup(self):
        P = self.P
        d = self.d
        P.dma("sp", self.cst[:], d["cst"].ap(), writes=[self.cst])
        P.dma("sp", self.vecs[:], d["vecs"].ap(), writes=[self.vecs])
        P.op("dve", lambda e: e.memset(self.ones_bf[:], 1.0), writes=[self.ones_bf])
        P.op("dve", lambda e: e.memset(self.qT[:, :, :], 0.0), writes=[self.qT])
        for i in range(5, 10):
            P.op("dve", lambda e, i=i: e.memset(self.S[i][:, :], 0.0), writes=[self.S[i]])
        drv = self.drv
        for l in range(2):
            t = self.S[0]
            lam = self.V("dlam%d" % l)
            P.op("act", lambda e: e.activation(t[:, 0:2], lam, AF.Exp, scale=-1.0), reads=[self.vecs], writes=[t])
            P.op("act", lambda e: e.activation(t[:, 0:2], t[:, 0:2], AF.Ln, bias=1.0), reads=[t], writes=[t])
            P.op("dve", lambda e, l=l: e.tensor_scalar(drv[:, 4 * l:4 * l + 2], t[:, 0:2], -8.0, None, ALU.mult),
                 reads=[t], writes=[drv])
            P.op("dve", lambda e, l=l: e.tensor_scalar(drv[:, 4 * l + 2:4 * l + 4], t[:, 0:2], -16.0, None, ALU.mult),
                 reads=[t], writes=[drv])
        P.op("dve", lambda e: e.memset(drv[:, 8:10], 0.0), writes=[drv])
        P.op("dve", lambda e: e.memset(drv[:, 10:12], 1.0), writes=[drv])
        P.op("dve", lambda e: e.memset(drv[:, 12:14], -1.0), writes=[drv])
        t = self.S[0]
        P.op("dve", lambda e: e.tensor_tensor(t[:, 8:10], self.V("alb1"), self.V("alb0"), ALU.subtract),
             reads=[self.vecs], writes=[t])
        P.op("act", lambda e: e.activation(drv[:, 14:16], t[:, 8:10], AF.Sigmoid), reads=[t], writes=[drv])
        P.op("dve", lambda e: e.tensor_scalar(drv[:, 16:18], drv[:, 14:16], -1.0, 1.0, ALU.mult, ALU.add),
             reads=[drv], writes=[drv])
        P.op("dve", lambda e: e.tensor_scalar(drv[:, 18:20], drv[:, 14:16], 1.0, -1.0, ALU.mult, ALU.add),
             reads=[drv], writes=[drv])
        for l in range(2):
            P.op("act", lambda e, l=l: e.activation(self.negA[:, 4 * l:4 * l + 4], self.V("calog%d" % l), AF.Exp), reads=[self.vecs], writes=[self.negA])
        P.op("dve", lambda e: e.tensor_scalar(self.negA[:, :], self.negA[:, :], -1.0, None, ALU.mult), reads=[self.negA], writes=[self.negA])
        if "B" in self.o["mixers"]:
            self.setup_E()

    def lbv(self, l):
        o = 8 + 6 * l
        return self.drv[:, o:o + 2], self.drv[:, o + 2:o + 4], self.drv[:, o + 4:o + 6]

    def setup_E(self):
        P = self.P
        relb = self.S[1]
        lh = self.S[2]
        cm = self.S[0]
        P.dma("sp", relb[:32, 0:4], self.d["relb"].ap(), writes=[relb])
        P.op("act", lambda e: e.activation(relb[:32, 4:8], relb[:32, 0:4], AF.Exp), reads=[relb], writes=[relb])
        for h in range(4):
            P.op("dve", lambda e, h=h: e.tensor_copy(lh[:32, h * 128:(h + 1) * 128],
                                                     relb[:32, 4 + h:5 + h].to_broadcast([32, 128])),
                 reads=[relb], writes=[lh])
        for cb in range(5):
            u0 = cb * 512
            n = min(512, TABL - u0)
            P.dma("sp", cm[:32, :n], self.d["cM"].ap()[:, u0:u0 + n], writes=[cm])
            for h in range(4):
                ps = self.pb[h]
                P.op("pe", lambda e, h=h, ps=ps: e.matmul(ps[:, :n], lh[:32, h * 128:(h + 1) * 128], cm[:32, :n],
                                                          start=True, stop=True),
                     reads=[lh, cm], writes=[ps])
                P.op("act", lambda e, h=h, ps=ps: e.activation(self.E[:, h, u0:u0 + n], ps[:, :n], AF.Copy),
                     reads=[ps], writes=[("E", h)])
        for h in range(4):
            dst = bass.AP(self.zt, h * 128 * ZW, [[ZW + 1, 128], [1, TABL]])
            P.dma("sp", dst, self.E[:, h, :TABL], reads=[("E", h)], writes=[("zt", h)], key="ztw%d" % h)
            src = bass.AP(self.zt, h * 128 * ZW + 127, [[ZW, 128], [1, 17 * 128]])
            P.dma("sp", self.E[:, h, :17 * 128], src, reads=[("zt", h)], writes=[("E", h)], key="ztr%d" % h)

    def run(self):
        P = self.P
        self.setup()
        blocks = [Blk("P", i) for i in range(NTOK // NBLK)] + [Blk("S", 0)]
        if self.o["blocks"] is not None:
            blocks = [blocks[i] for i in self.o["blocks"]]
        for blk in blocks:
            N = blk.N
            if blk.kind == "P":
                src = self.d["xT"].ap()[:, :, blk.tok0:blk.tok0 + N]
            else:
                src = self.d["xsT"].ap()
            P.dma("sp", self.hT[:, :, :N], src, writes=[("hT", f) for f in range(8)])
            for l in range(self.o["layers"]):
                self.layer(blk, l)
            self.final_norm(blk)
        P.finish()

    def hh(self):
        return [("hT", f) for f in range(8)]

    def layer(self, blk, l):
        N = blk.N
        self.rmsnorm_fm(self.hT, self.V("n1g%d" % l), self.hnT, N, self.hh(), [self.hnT])
        mx = self.o["mixers"]
        for k in range(8):
            if "ABCD"[k // 2] not in mx:
                self.P.op("dve", lambda e, k=k: e.memset(self.mixT[:, k, :N], 0.0), writes=[("mixT", k)])
        if "D" in mx:
            self.mixer_D(blk, l)
        if "B" in mx:
            self.mixer_B(blk, l)
        if "A" in mx:
            self.mixer_A(blk, l)
        if "C" in mx:
            self.mixer_C(blk, l)
        if self.o.get("mixin") and blk.idx == 0 and l == 0:
            md = self.nc.dram_tensor("mixin_dbg", [128, 8, N], F32, kind="ExternalInput")
            for k2 in range(4):
                t = self.S[k2]
                self.P.dma("sp", t[:, :2 * N].rearrange("p (c n) -> p c n", c=2), md.ap()[:, 2 * k2:2 * k2 + 2, :], writes=[t])
                self.P.op("dve", lambda e, k2=k2, t=t: e.tensor_copy(self.mixT[:, 2 * k2:2 * k2 + 2, :N], t[:, :2 * N].rearrange("p (c n) -> p c n", c=2)),
                          reads=[t], writes=[("mixT", 2 * k2), ("mixT", 2 * k2 + 1)])
        if blk.idx == 0 and l == 0:
            self.dump("mixT_" + blk.kind, self.mixT[:, :, :N], [128, 8, N], [("mixT", k) for k in range(8)])
        if self.o["dense"]:
            self.dense(blk, l)

    def rmsnorm_fm(self, src, gv, dst, N, srch, dsth, dst_fn=None):
        P = self.P
        ps = self.pb[4]
        sqa = self.actT
        ah_ = [("actT", k) for k in range(8)]
        P.op("act", lambda e: e.activation(sqa[:, 0:4, :N], src[:, 0:4, :N], AF.Square),
             reads=srch[0:4], writes=ah_[0:4])
        P.op("dve", lambda e: e.tensor_tensor(sqa[:, 4:8, :N], src[:, 4:8, :N], src[:, 4:8, :N], ALU.mult),
             reads=srch[4:8], writes=ah_[4:8])
        for k in range(8):
            P.op("pe", lambda e, k=k: e.matmul(ps[:, :N], self.ones_bf[:], sqa[:, k, :N],
                                               start=(k == 0), stop=(k == 7)),
                 reads=[ah_[k], self.ones_bf], writes=[ps])
        rt = self.rt
        P.op("act", lambda e: e.activation(rt[:, :N], ps[:, :N], AF.Ln, bias=EPS, scale=1.0 / D),
             reads=[ps], writes=[rt])
        P.op("act", lambda e: e.activation(rt[:, :N], rt[:, :N], AF.Exp, scale=-0.5),
             reads=[rt], writes=[rt])
        for k in range(8):
            if dst_fn is None:
                o, oh = dst[:, k, :N], [("hnT", k)]
            else:
                o, oh = dst_fn(k)
            P.op("dve", lambda e, k=k, o=o: e.scalar_tensor_tensor(o, src[:, k, :N], gv[:, k:k + 1],
                                                                  rt[:, :N], ALU.mult, ALU.mult),
                 reads=[srch[k], rt, self.vecs], writes=oh)

    def final_norm(self, blk):
        P = self.P
        N = blk.N
        def dst_fn(k):
            t = self.S[k // 2]
            return t[:, (k % 2) * N:(k % 2 + 1) * N], [t]
        self.rmsnorm_fm(self.hT, self.V("fng"), None, N, self.hh(), None, dst_fn=dst_fn)
        for k2 in range(4):
            t = self.S[k2]
            if blk.kind == "P":
                dst = self.od["yT"].ap()[:, 2 * k2:2 * k2 + 2, blk.tok0:blk.tok0 + N]
            else:
                dst = self.od["ysT"].ap()[:, 2 * k2:2 * k2 + 2, :]
            P.dma("act", dst, t[:, :2 * N].rearrange("p (c n) -> p c n", c=2), reads=[t], key="y%d" % k2, out=True)

    def dense(self, blk, l):
        P = self.P
        N = blk.N
        d = self.d
        hT, hnT = self.hT, self.hnT
        mixh = [("mixT", k) for k in range(8)]
        for u in range(4):
            wb, wv = self.wload(d["w_out_t"].ap()[l][:, :, 256 * u:256 * u + 256], [128, 8, 256], ("w_out", l, u))
            for fc in range(2):
                f = 2 * u + fc
                ps = self.pb[f % 2]
                for k in range(8):
                    P.op("pe", lambda e, k=k, ps=ps, wv=wv, fc=fc: e.matmul(
                        ps[:, :N], wv[:, k, fc * 128:(fc + 1) * 128], self.mixT[:, k, :N],
                        start=(k == 0), stop=(k == 7)), reads=[wb, mixh[k]], writes=[ps], sig=(k == 7))
                P.op("dve", lambda e, f=f, ps=ps: e.tensor_tensor(hT[:, f, :N], hT[:, f, :N], ps[:, :N], ALU.add),
                     reads=[ps, ("hT", f)], writes=[("hT", f)])
        if l == 0:
            self.dump("h1_" + blk.kind + str(blk.idx), hT[:, :, :N], [128, 8, N], self.hh())
        self.rmsnorm_fm(hT, self.V("n2g%d" % l), hnT, N, self.hh(), [hnT])
        for hf in range(2):
            for cc in range(NFF // 2):
                c = hf * (NFF // 2) + cc
                wb, wv = self.wload(d["w_ffi_t"].ap()[l, c], [128, 8, 256], ("w_ffi", l, c))
                pg, pu = self.pb[2 * (c % 2)], self.pb[2 * (c % 2) + 1]
                for half, ps in ((0, pg), (1, pu)):
                    for k in range(8):
                        P.op("pe", lambda e, k=k, ps=ps, wv=wv, half=half: e.matmul(
                            ps[:, :N], wv[:, k, half * 128:(half + 1) * 128], hnT[:, k, :N],
                            start=(k == 0), stop=(k == 7)), reads=[wb, ("hnT", k)], writes=[ps], sig=(k == 7))
                tmp = self.S[8 + c % 2]
                P.op("act", lambda e, pg=pg, tmp=tmp: e.activation(tmp[:, :N], pg[:, :N], AF.Silu),
                     reads=[pg], writes=[tmp])
                P.op("dve", lambda e, cc=cc, pu=pu, tmp=tmp: e.tensor_tensor(self.actT[:, cc, :N], tmp[:, :N], pu[:, :N], ALU.mult),
                     reads=[tmp, pu], writes=[("actT", cc)])
            for f in range(8):
                ps = self.pb[5 + f % 2]
                wb, wv = self.wload(d["w_ffo_t"].ap()[l][:, 11 * hf:11 * hf + 11, 128 * f:128 * f + 128], [128, 11, 128], ("w_ffo", l, hf, f))
                for cc in range(11):
                    P.op("pe", lambda e, cc=cc, ps=ps, wv=wv: e.matmul(
                        ps[:, :N], wv[:, cc, :], self.actT[:, cc, :N], start=(cc == 0), stop=(cc == 10)),
                        reads=[wb, ("actT", cc)], writes=[ps], sig=(cc == 10))
                P.op("dve", lambda e, f=f, ps=ps: e.tensor_tensor(hT[:, f, :N], hT[:, f, :N], ps[:, :N], ALU.add),
                     reads=[ps, ("hT", f)], writes=[("hT", f)])
        if l == 0:
            self.dump("h2_" + blk.kind + str(blk.idx), hT[:, :, :N], [128, 8, N], self.hh())
        self.rmsnorm_fm(hT, self.V("png%d" % l), hnT, N, self.hh(), [hnT])
        pst = self.S[7]
        if blk.kind == "P":
            src = d["pT"].ap()[l][:, :, blk.tok0:blk.tok0 + N]
        else:
            src = d["psT"].ap()[l]
        P.dma("sp", pst[:, :2 * N].rearrange("p (c n) -> p c n", c=2), src, writes=[pst])
        for k in range(2):
            P.op("dve", lambda e, k=k: e.tensor_copy(self.PT[k][:, :N], pst[:, k * N:(k + 1) * N]), reads=[pst], writes=[self.PT[k]])
        self.in_ple = True
        pkey = ("w_ple", l)
        if pkey not in self.wc_map:
            off = self.wc_off
            self.wc_map[pkey] = off
            self.wc_off += 2048
            stg = self.wstg[self.wi % len(self.wstg)]
            self.wi += 1
            P.dma("sp", stg[:, :2048].rearrange("p (a b) -> p a b", a=2), d["w_ple_t"].ap()[l], writes=[stg])
            P.op("act", lambda e: e.activation(self.wple[:, :], stg[:, :], AF.Copy), reads=[stg], writes=[self.wple])
            P.dma("act", self.wcache.ap()[:, off:off + 2048], self.wple[:, :], reads=[self.wple], writes=[("wc", pkey)], key="wcwp")
        else:
            off = self.wc_map[pkey]
            P.dma("sp", self.wple[:, :], self.wcache.ap()[:, off:off + 2048], reads=[("wc", pkey)], writes=[self.wple])
        wpv = self.wple[:, :].rearrange("p (a b) -> p a b", a=2)
        for u in range(4):
            wb, wv = self.wload(d["w_gate_t"].ap()[l][:, :, 256 * u:256 * u + 256], [128, 8, 256], ("w_gate", l, u))
            for fc in range(2):
                f = 2 * u + fc
                pg, pp = self.pb[2 * (f % 2)], self.pb[2 * (f % 2) + 1]
                for k in range(8):
                    P.op("pe", lambda e, k=k, pg=pg, wv=wv, fc=fc: e.matmul(
                        pg[:, :N], wv[:, k, fc * 128:(fc + 1) * 128], hnT[:, k, :N],
                        start=(k == 0), stop=(k == 7)), reads=[wb, ("hnT", k)], writes=[pg], sig=(k == 7))
                for k in range(2):
                    P.op("pe", lambda e, k=k, pp=pp, f=f: e.matmul(
                        pp[:, :N], wpv[:, k, f * 128:(f + 1) * 128], self.PT[k][:, :N],
                        start=(k == 0), stop=(k == 1)), reads=[self.wple, self.PT[k]], writes=[pp], sig=(k == 1))
                tmp = self.S[8 + f % 2]
                P.op("act", lambda e, pg=pg, tmp=tmp: e.activation(tmp[:, :N], pg[:, :N], AF.Sigmoid),
                     reads=[pg], writes=[tmp])
                P.op("dve", lambda e, pp=pp, tmp=tmp: e.tensor_tensor(tmp[:, :N], tmp[:, :N], pp[:, :N], ALU.mult),
                     reads=[tmp, pp], writes=[tmp])
                P.op("dve", lambda e, f=f, tmp=tmp: e.tensor_tensor(hT[:, f, :N], hT[:, f, :N], tmp[:, :N], ALU.add),
                     reads=[tmp, ("hT", f)], writes=[("hT", f)])
        self.in_ple = False
        if l == 0:
            self.dump("h3_" + blk.kind + str(blk.idx), hT[:, :, :N], [128, 8, N], self.hh())

    def Sv(self, i, N):
        return self.S[i][:, :2 * N].rearrange("p (c n) -> p c n", c=2)

    def mixer_D(self, blk, l):
        P = self.P
        N, nseq, L = blk.N, blk.nseq, blk.L
        d = self.d
        W = 3 + L
        ext = self.dext
        extv = ext[:, :, :nseq * W].rearrange("p c (s j) -> p c s j", s=nseq) if nseq > 1 else None
        def ev(ch, j0, j1):
            if nseq == 1:
                return ext[:, ch, j0:j1]
            return extv[:, ch, :, j0:j1]
        def v3(ap2):
            if nseq == 1:
                return ap2
            return ap2.rearrange("p (s j) -> p s j", s=nseq)
        gg, dm, r, ig, a, w5, hd, y = (self.Sv(i, N) for i in range(8))
        Sh = self.S
        dh = self.dhst[l]
        if blk.kind == "P":
            if blk.first:
                P.op("dve", lambda e: e.memset(ext[:, :, 0:3], 0.0), writes=[ext])
                P.op("dve", lambda e: e.memset(dh[:, :, :], 0.0), writes=[dh])
            else:
                P.op("dve", lambda e: e.tensor_copy(ext[:, :, 0:3], self.dtail[l][:, :, :]),
                     reads=[self.dtail[l]], writes=[ext])
        else:
            for ch in range(2):
                P.dma("sp", extv[:, ch, :, 0:3], d["sdc"].ap()[l][:, ch], writes=[ext], key="sdc_in")
            P.dma("sp", dh[:, :, :], d["sdh"].ap()[l], writes=[dh])
        wb, wv = self.w_in_unit(l, COLS["dx"], 256)
        for ch in range(2):
            ps = self.pb[ch]
            self.proj_fm(wb, wv, ch * 128, 128, ps, N, ps)
            P.op("act", lambda e, ch=ch, ps=ps: e.activation(ev(ch, 3, 3 + L), v3(ps[:, :N]), AF.Copy),
                 reads=[ps], writes=[ext])
        wb, wv = self.w_in_unit(l, COLS["dg"], 256)
        for ch in range(2):
            ps = self.pb[2 + ch]
            self.proj_fm(wb, wv, ch * 128, 128, ps, N, ps)
            P.op("act", lambda e, ch=ch, ps=ps: e.activation(gg[:, ch, :], ps[:, :N], AF.Gelu_apprx_tanh),
                 reads=[ps], writes=[Sh[0]])
        cw = self.V("dcw%d" % l)
        cbias = self.V("dcb%d" % l)
        for ch in range(2):
            P.op("dve", lambda e, ch=ch: e.tensor_scalar(v3(dm[:, ch, :]), ev(ch, 0, L), cw[:, ch * 4:ch * 4 + 1],
                                                         cbias[:, ch:ch + 1], ALU.mult, ALU.add),
                 reads=[ext, self.vecs], writes=[Sh[1]])
            for j in range(1, 4):
                P.op("dve", lambda e, ch=ch, j=j: e.scalar_tensor_tensor(
                    v3(dm[:, ch, :]), ev(ch, j, j + L), cw[:, ch * 4 + j:ch * 4 + j + 1], v3(dm[:, ch, :]),
                    ALU.mult, ALU.add), reads=[ext, self.vecs, Sh[1]], writes=[Sh[1]])
        if blk.kind == "P":
            if blk.last:
                P.dma("act", self.od["pdc"].ap()[l], ext[:, :, L:L + 3], reads=[ext], key="so1_%d_%s" % (l, str(locals().get("gc", "")) + str(locals().get("ch", ""))), out=True)
            else:
                P.op("dve", lambda e: e.tensor_copy(self.dtail[l][:, :, :], ext[:, :, L:L + 3]),
                     reads=[ext], writes=[self.dtail[l]])
        else:
            for ch in range(2):
                P.dma("act", self.od["sdco"].ap()[l][:, ch], extv[:, ch, :, L:L + 3], reads=[ext], key="so2_%d_%s" % (l, str(locals().get("gc", "")) + str(locals().get("ch", ""))), out=True)
        P.dma("sp", self.dgw[:, :], d["dgw"].ap()[:, 512 * l:512 * l + 512], writes=[self.dgw])
        gw = self.dgw[:, :].rearrange("p (w c j) -> p w c j", w=2, c=2)
        sp8 = self.drv[:, 4 * l:4 * l + 2]
        sp16 = self.drv[:, 4 * l + 2:4 * l + 4]
        for ch in range(2):
            for wi, (dst, dsth, bname) in enumerate(((r, Sh[2], "dba"), (ig, Sh[3], "dbx"))):
                ps = self.pb[5 + wi]
                P.op("pe", lambda e, ch=ch, wi=wi, ps=ps: e.matmul(ps[:, :N], gw[:, wi, ch, :], dm[:, ch, :],
                                                                 start=True, stop=True),
                     reads=[self.dgw, Sh[1]], writes=[ps])
                bv = self.V("%s%d" % (bname, l))
                P.op("act", lambda e, ch=ch, ps=ps, dst=dst, bv=bv: e.activation(dst[:, ch, :], ps[:, :N], AF.Sigmoid,
                                                                               bias=bv[:, ch:ch + 1]),
                     reads=[ps, self.vecs], writes=[dsth])
            P.op("act", lambda e, ch=ch: e.activation(a[:, ch, :], r[:, ch, :], AF.Exp, scale=sp8[:, ch:ch + 1]),
                 reads=[Sh[2], self.drv], writes=[Sh[4]])
            P.op("act", lambda e, ch=ch: e.activation(w5[:, ch, :], r[:, ch, :], AF.Exp, scale=sp16[:, ch:ch + 1]),
                 reads=[Sh[2], self.drv], writes=[Sh[5]])
            P.op("dve", lambda e, ch=ch: e.tensor_scalar(w5[:, ch, :], w5[:, ch, :], -1.0, 1.0, ALU.mult, ALU.add),
                 reads=[Sh[5]], writes=[Sh[5]])
            P.op("act", lambda e, ch=ch: e.activation(w5[:, ch, :], w5[:, ch, :], AF.Sqrt), reads=[Sh[5]], writes=[Sh[5]])
            P.op("dve", lambda e, ch=ch: e.tensor_tensor(w5[:, ch, :], w5[:, ch, :], ig[:, ch, :], ALU.mult),
                 reads=[Sh[5], Sh[3]], writes=[Sh[5]])
            P.op("dve", lambda e, ch=ch: e.tensor_tensor(w5[:, ch, :], w5[:, ch, :], dm[:, ch, :], ALU.mult),
                 reads=[Sh[5], Sh[1]], writes=[Sh[5]])
            for s in range(nseq):
                P.op("dve", lambda e, ch=ch, s=s: e.tensor_tensor_scan(
                    hd[:, ch, s * L:(s + 1) * L], a[:, ch, s * L:(s + 1) * L], w5[:, ch, s * L:(s + 1) * L],
                    dh[:, ch, s:s + 1], ALU.mult, ALU.add), reads=[Sh[4], Sh[5], dh], writes=[Sh[6]])
            if nseq == 1:
                src = hd[:, ch, L - 1:L]
            else:
                src = hd[:, ch, :].rearrange("p (s j) -> p s j", s=nseq)[:, :, L - 1]
            P.op("dve", lambda e, ch=ch, src=src: e.tensor_copy(dh[:, ch, 0:nseq], src), reads=[Sh[6]], writes=[dh])
            P.op("dve", lambda e, ch=ch: e.tensor_tensor(y[:, ch, :], hd[:, ch, :], gg[:, ch, :], ALU.mult),
                 reads=[Sh[6], Sh[0]], writes=[Sh[7]])
        if blk.last:
            if blk.kind == "P":
                P.dma("act", self.od["pdh"].ap()[l], dh[:, :, 0], reads=[dh], key="so3_%d_%s" % (l, str(locals().get("gc", "")) + str(locals().get("ch", ""))), out=True, allow_slow_non_contiguous=True)
            else:
                P.dma("act", self.od["sdho"].ap()[l], dh[:, :, :], reads=[dh], key="so4_%d_%s" % (l, str(locals().get("gc", "")) + str(locals().get("ch", ""))), out=True)
        self.group_rmsnorm(y, Sh[7], self.V("dng%d" % l), 6, N)

    def group_rmsnorm(self, y, yh, gv, k0, N):
        P = self.P
        ps = self.pb[4]
        for ch in range(2):
            sq = self.sqb[ch]
            P.op("act", lambda e, ch=ch, sq=sq: e.activation(sq[:, :N], y[:, ch, :], AF.Square), reads=[yh], writes=[sq])
            P.op("pe", lambda e, ch=ch, sq=sq: e.matmul(ps[:, :N], self.ones_bf[:], sq[:, :N], start=(ch == 0), stop=(ch == 1)),
                 reads=[sq, self.ones_bf], writes=[ps])
        rt = self.rt
        P.op("act", lambda e: e.activation(rt[:, :N], ps[:, :N], AF.Ln, bias=EPS, scale=1.0 / 256), reads=[ps], writes=[rt])
        P.op("act", lambda e: e.activation(rt[:, :N], rt[:, :N], AF.Exp, scale=-0.5), reads=[rt], writes=[rt])
        for ch in range(2):
            P.op("dve", lambda e, ch=ch: e.scalar_tensor_tensor(self.mixT[:, k0 + ch, :N], y[:, ch, :], gv[:, ch:ch + 1],
                                                               rt[:, :N], ALU.mult, ALU.mult),
                 reads=[yh, rt, self.vecs], writes=[("mixT", k0 + ch)])

    def mixer_B(self, blk, l):
        P = self.P
        N = blk.N
        d = self.d
        KT, Vh, qT = self.KT[l], self.Vh[l], self.qT
        Eh = [("E", h) for h in range(4)]
        kcol0 = blk.tok0 if blk.kind == "P" else NTOK
        wb, wv = self.w_in_unit(l, COLS["bq"], 256)
        for pr in range(2):
            ps = self.pb[pr]
            self.proj_fm(wb, wv, pr * 128, 128, ps, N, ps)
            for hh in range(2):
                P.op("act", lambda e, pr=pr, ps=ps, hh=hh: e.activation(
                    qT[hh * 64:(hh + 1) * 64, 2 * pr + hh, :N], ps[hh * 64:(hh + 1) * 64, :N], AF.Copy, scale=0.125),
                    reads=[ps], writes=[qT])
        wbk, wvk = self.w_in_unit(l, COLS["bk"], 256)
        for pr in range(2):
            ps = self.pb[2 + pr]
            self.proj_fm(wbk, wvk, pr * 128, 128, ps, N, ps)
            P.op("act", lambda e, pr=pr, ps=ps: e.activation(KT[:, pr, kcol0:kcol0 + N], ps[:, :N], AF.Copy),
                 reads=[ps], writes=[("KT", l, "new")])
        if self.o.get("Bstop") == 1:
            return
        wbv, wvv = self.w_in_unit(l, COLS["bv"], 256)
        for ti, (c0, nt, seq, pos) in enumerate(blk.tiles):
            if self.o.get("Bvar") == 23:
                break
            ps = self.pb[ti % 2]
            for half, (wbx, wvx) in enumerate(((wbk, wvk), (wbv, wvv))):
                for k in range(8):
                    P.op("pe", lambda e, k=k, ps=ps, wvx=wvx, half=half, c0=c0, nt=nt: e.matmul(
                        ps[:nt, half * 256:(half + 1) * 256], self.hnT[:, k, c0:c0 + nt], wvx[:, k, :],
                        start=(k == 0), stop=(k == 7)), reads=[wbx, ("hnT", k)], writes=[ps], sig=(k == 7 and half == 1))
            if self.o.get("Bvar") == 24:
                continue
            kvs = self.S[8 + ti % 2]
            if self.o.get("Bvar") != 26:
                P.op("act", lambda e, ps=ps, kvs=kvs, nt=nt: e.activation(kvs[:nt, :512], ps[:nt, :512], AF.Copy),
                     reads=[ps], writes=[kvs])
            if blk.kind == "P":
                r0 = blk.tok0 + c0
                ok, ov = self.od["pbk"].ap()[l][r0:r0 + nt, :], self.od["pbv"].ap()[l][r0:r0 + nt, :]
                vdst, vh = Vh[:nt, pos, :], ("Vh", l, pos)
            else:
                ok, ov = self.od["sbk"].ap()[l][c0:c0 + nt, :], self.od["sbv"].ap()[l][c0:c0 + nt, :]
                vdst, vh = self.Vnew[:nt, seq, :], ("Vnew", seq)
            if self.o.get("Bvar") not in (21, 25, 26):
                P.dma("act", ok, kvs[:nt, 0:256], reads=[kvs], key="kv_out%d" % (ti % 2), out=True)
                P.dma("act", ov, kvs[:nt, 256:512], reads=[kvs], key="kv_out%d" % (ti % 2), out=True)
            if self.o.get("Bvar") not in (22, 25):
                P.op("dve", lambda e, ps=ps, vdst=vdst, nt=nt: e.tensor_copy(vdst, ps[:nt, 256:512]), reads=[ps], writes=[vh])
        if self.o.get("Bstop") == 2:
            return
        ob = self.Sv(0, N)
        obh = self.S[0]
        if blk.kind == "P":
            for ti, (c0, nt, seq, pos) in enumerate(blk.tiles):
                self.attn_prompt_tile(l, c0, pos, ob, obh)
        else:
            for ti, (c0, nt, seq, pos) in enumerate(blk.tiles):
                self.attn_sample_seq(l, c0, seq, ob, obh)
        self.group_rmsnorm(ob, obh, self.V("bng%d" % l), 2, N)

    def attn_finish(self, accps, lps, ob, obh, c0, nt):
        P = self.P
        rl = self.S[2]
        n4 = 4 * nt
        P.op("act", lambda e: e.activation(rl[:, :n4], lps[:, :n4], AF.Ln), reads=[lps], writes=[rl])
        P.op("act", lambda e: e.activation(rl[:, :n4], rl[:, :n4], AF.Exp, scale=-1.0), reads=[rl], writes=[rl])
        for h in range(4):
            r0, pr = (h % 2) * 64, h // 2
            P.op("dve", lambda e, h=h, r0=r0, pr=pr: e.tensor_tensor(
                ob[r0:r0 + 64, pr, c0:c0 + nt], accps[r0:r0 + 64, pr * nt:(pr + 1) * nt],
                rl[r0:r0 + 64, h * nt:(h + 1) * nt], ALU.mult), reads=[accps, rl], writes=[obh])

    def attn_prompt_tile(self, l, c0, pos, ob, obh):
        P = self.P
        KT, Vh, qT = self.KT[l], self.Vh[l], self.qT
        accps, lps = self.pb[7], self.pb[3]
        ex = self.S[1]
        P.op("dve", lambda e: e.memset(accps[:, :256], 0.0), writes=[accps])
        def qk(j):
            dl = pos - j
            ST = self.pb[5 + j % 2]
            PT = self.PT[j % 2]
            exj = self.Vnew[:, 2 * (j % 2):2 * (j % 2) + 2, :].rearrange("p a b -> p (a b)")
            exh = ("Vnew", 2 * (j % 2))
            for h in range(4):
                pr = h // 2
                P.op("pe", lambda e, h=h, pr=pr, ST=ST, j=j: e.matmul(
                    ST[:, h * 128:(h + 1) * 128], KT[:, pr, j * 128:(j + 1) * 128],
                    qT[:, h, c0:c0 + 128], start=True, stop=True),
                    reads=[("KT", l, "new"), qT], writes=[ST], sig=(h == 3))
            P.op("act", lambda e, ST=ST: e.activation(exj, ST[:, :512], AF.Exp), reads=[ST],
                 writes=[exh, ("Vnew", 2 * (j % 2) + 1)])
            P.op("dve", lambda e, PT=PT, dl=dl: e.tensor_tensor(
                PT[:, :].rearrange("p (h t) -> p h t", h=4), exj.rearrange("p (h t) -> p h t", h=4),
                self.E[:, :, dl * 128:(dl + 1) * 128], ALU.mult), reads=[exh] + [("E", h) for h in range(4)], writes=[PT])

        def pv(j):
            PT = self.PT[j % 2]
            for h in range(4):
                r0, pr = (h % 2) * 64, h // 2
                P.op("pe", lambda e, h=h, r0=r0, pr=pr, PT=PT, j=j: e.matmul(
                    accps[r0:r0 + 64, pr * 128:(pr + 1) * 128], Vh[:, j, h * 64:(h + 1) * 64],
                    PT[:, h * 128:(h + 1) * 128], start=False, stop=(j == pos), skip_group_check=True),
                    reads=[("Vh", l, j), PT], writes=[accps], sig=False)
            P.op("pe", lambda e, PT=PT, j=j: e.matmul(lps[:, :512], self.ones_bf[:], PT[:, :512],
                                                      start=(j == 0), stop=(j == pos)),
                 reads=[self.ones_bf, PT], writes=[lps])

        qk(0)
        for j in range(pos + 1):
            if j + 1 <= pos:
                qk(j + 1)
            pv(j)
        if self.o.get("Bstop") == 3:
            return
        self.attn_finish(accps, lps, ob, obh, c0, 128)

    def attn_sample_seq(self, l, c0, seq, ob, obh):
        P = self.P
        d = self.d
        KT, Vh, qT = self.KT[l], self.Vh[l], self.qT
        nt = LS
        kth = ("KT", l, "new")
        vhh = [("Vh", l, j) for j in range(16)]
        for q4 in range(4):
            t = self.S[4 + q4]
            P.dma("sp", t[:, :1024].rearrange("p (c n) -> p c n", c=2),
                  d["kcT"].ap()[l, seq][:, :, q4 * 512:(q4 + 1) * 512], writes=[t])
            P.op("dve", lambda e, t=t, q4=q4: e.tensor_copy(KT[:, :, q4 * 512:(q4 + 1) * 512],
                                                           t[:, :1024].rearrange("p (c n) -> p c n", c=2)),
                 reads=[t], writes=[kth])
        for q4 in range(4):
            t = self.S[4 + q4]
            P.dma("sp", t[:, :1024].rearrange("p (j c) -> p j c", j=4),
                  d["vc"].ap()[l, seq][:, 4 * q4:4 * q4 + 4, :], writes=[t])
            P.op("act", lambda e, t=t, q4=q4: e.activation(Vh[:, 4 * q4:4 * q4 + 4, :],
                                                          t[:, :1024].rearrange("p (j c) -> p j c", j=4), AF.Copy),
                 reads=[t], writes=vhh[4 * q4:4 * q4 + 4])
        accps, lps = self.pb[7], self.pb[3]
        ST, ST2 = self.pb[5], self.pb[6]
        PT, PT2 = self.PT[0], self.PT[1]
        ex = self.S[1]
        Ehs = [("E", h) for h in range(4)]
        P.op("dve", lambda e: e.memset(accps[:, :256], 0.0), writes=[accps])
        for dl in range(1, 17):
            j = 16 - dl
            for h in range(4):
                pr = h // 2
                o = (dl - 1) * 32 + h * 8
                P.op("pe", lambda e, h=h, pr=pr, o=o, j=j: e.matmul(
                    ST[:, o:o + 8], KT[:, pr, j * 128:(j + 1) * 128], qT[:, h, c0:c0 + 8], start=True, stop=True),
                    reads=[kth, qT], writes=[ST], sig=(dl == 16 and h == 3))
        P.op("act", lambda e: e.activation(ex[:, :512], ST[:, :512], AF.Exp), reads=[ST], writes=[ex])
        v4 = lambda ap: ap.rearrange("p (d h t) -> p d h t", d=16, h=4)
        Ev = self.E[:, :, 128:128 * 17].rearrange("p h (d t) -> p d h t", t=128)[:, :, :, 0:8]
        P.op("dve", lambda e: e.tensor_tensor(v4(PT[:, :512]), v4(ex[:, :512]), Ev, ALU.mult),
             reads=[ex] + Ehs, writes=[PT])
        for dl in range(1, 17):
            j = 16 - dl
            for h in range(4):
                r0, pr = (h % 2) * 64, h // 2
                o = (dl - 1) * 32 + h * 8
                P.op("pe", lambda e, h=h, r0=r0, pr=pr, o=o, j=j: e.matmul(
                    accps[r0:r0 + 64, pr * 8:(pr + 1) * 8], Vh[:, j, h * 64:(h + 1) * 64], PT[:, o:o + 8],
                    start=False, stop=False, skip_group_check=True), reads=[vhh[j], PT], writes=[accps], sig=False)
            P.op("pe", lambda e, dl=dl: e.matmul(lps[:, 0:32], self.ones_bf[:], PT[:, (dl - 1) * 32:dl * 32],
                                                 start=(dl == 1), stop=False),
                 reads=[self.ones_bf, PT], writes=[lps], sig=False)
        kc = NTOK + c0
        for h in range(4):
            pr = h // 2
            P.op("pe", lambda e, h=h, pr=pr: e.matmul(ST2[:nt, h * 8:(h + 1) * 8], KT[:, pr, kc:kc + nt],
                                                      qT[:, h, c0:c0 + nt], start=True, stop=True),
                 reads=[kth, qT], writes=[ST2], sig=(h == 3))
        P.op("act", lambda e: e.activation(ex[:nt, 512:544], ST2[:nt, 0:32], AF.Exp), reads=[ST2], writes=[ex])
        P.op("dve", lambda e: e.tensor_tensor(PT2[:nt, 0:32].rearrange("p (h t) -> p h t", h=4),
                                              ex[:nt, 512:544].rearrange("p (h t) -> p h t", h=4),
                                              self.E[:nt, :, 0:nt], ALU.mult), reads=[ex] + Ehs, writes=[PT2])
        for h in range(4):
            r0, pr = (h % 2) * 64, h // 2
            P.op("pe", lambda e, h=h, r0=r0, pr=pr: e.matmul(
                accps[r0:r0 + 64, pr * 8:(pr + 1) * 8], self.Vnew[:nt, seq, h * 64:(h + 1) * 64], PT2[:nt, h * 8:(h + 1) * 8],
                start=False, stop=True, skip_group_check=True), reads=[("Vnew", seq), PT2], writes=[accps], sig=(h == 3))
        P.op("pe", lambda e: e.matmul(lps[:, 0:32], self.ones_bf[:nt, :], PT2[:nt, 0:32], start=False, stop=True),
             reads=[self.ones_bf, PT2], writes=[lps])
        self.attn_finish(accps, lps, ob, obh, c0, nt)


def _mixer_A(self, blk, l):
    P = self.P
    N, nseq, L = blk.N, blk.nseq, blk.L
    d = self.d
    csz = 16 if blk.kind == "P" else LS
    ncht = N // csz
    Sh = self.S
    Q, KA, LF, G, EG, W5, OA, SG = (self.Sv(i, N) for i in range(8))
    lb, omlb, nomlb = self.lbv(l)
    aT = self.actT
    ah = lambda c: ("actT", c)
    kt = aT[:, 0:2, :N]
    SA = self.SAw
    SAp = self.SAp[l]
    qT = self.qT
    wb, wv = self.w_in_unit(l, COLS["aq"], 256)
    for pr in range(2):
        ps = self.pb[pr]
        self.proj_fm(wb, wv, pr * 128, 128, ps, N, ps)
        P.op("act", lambda e, pr=pr, ps=ps: e.activation(Q[:, pr, :], ps[:, :N], AF.Copy), reads=[ps], writes=[Sh[0]])
    wb, wv = self.w_in_unit(l, COLS["af"], 256)
    for pr in range(2):
        ps = self.pb[2 + pr]
        self.proj_fm(wb, wv, pr * 128, 128, ps, N, ps)
        P.op("act", lambda e, pr=pr, ps=ps: e.activation(KA[:, pr, :], ps[:, :N], AF.Sigmoid), reads=[ps], writes=[Sh[1]])
        P.op("dve", lambda e, pr=pr: e.tensor_scalar(LF[:, pr, :], KA[:, pr, :], omlb[:, pr:pr + 1], lb[:, pr:pr + 1],
                                                     ALU.mult, ALU.add), reads=[Sh[1], self.drv], writes=[Sh[2]])
        P.op("act", lambda e, pr=pr: e.activation(LF[:, pr, :], LF[:, pr, :], AF.Ln), reads=[Sh[2]], writes=[Sh[2]])
        P.op("dve", lambda e, pr=pr: e.tensor_scalar(KA[:, pr, :], KA[:, pr, :], nomlb[:, pr:pr + 1], omlb[:, pr:pr + 1],
                                                     ALU.mult, ALU.add), reads=[Sh[1], self.drv], writes=[Sh[1]])
        rs = self.C("reset16")[:, :N] if blk.kind == "P" else self.C("resetS")[:, :N]
        P.op("dve", lambda e, pr=pr, rs=rs: e.tensor_tensor_scan(G[:, pr, :], rs, LF[:, pr, :], 0.0, ALU.mult, ALU.add),
             reads=[Sh[2], self.cst], writes=[Sh[3]])
        P.op("act", lambda e, pr=pr: e.activation(EG[:, pr, :], G[:, pr, :], AF.Exp), reads=[Sh[3]], writes=[Sh[4]])
        P.op("act", lambda e, pr=pr: e.activation(W5[:, pr, :], G[:, pr, :], AF.Exp, scale=-1.0), reads=[Sh[3]], writes=[Sh[5]])
        P.op("dve", lambda e, pr=pr: e.tensor_tensor(Q[:, pr, :], Q[:, pr, :], EG[:, pr, :], ALU.mult),
             reads=[Sh[0], Sh[4]], writes=[Sh[0]])
        for hh in range(2):
            P.op("act", lambda e, pr=pr, hh=hh: e.activation(qT[hh * 64:(hh + 1) * 64, 2 * pr + hh, :N],
                                                            Q[hh * 64:(hh + 1) * 64, pr, :], AF.Copy),
                 reads=[Sh[0]], writes=[qT])
        P.op("dve", lambda e, pr=pr: e.tensor_tensor(kt[:, pr, :], KA[:, pr, :], W5[:, pr, :], ALU.mult),
             reads=[Sh[1], Sh[5]], writes=[ah(pr)])
        Gv = G[:, pr, :].rearrange("p (n j) -> p n j", j=csz)
        Wv = W5[:, pr, :].rearrange("p (n j) -> p n j", j=csz)
        P.op("dve", lambda e, Gv=Gv, Wv=Wv: e.tensor_tensor(Wv, Gv[:, :, csz - 1:csz].to_broadcast([128, ncht, csz]), Gv,
                                                            ALU.subtract), reads=[Sh[3], ah(pr)], writes=[Sh[5]])
        P.op("act", lambda e, pr=pr: e.activation(W5[:, pr, :], W5[:, pr, :], AF.Exp), reads=[Sh[5]], writes=[Sh[5]])
        P.op("dve", lambda e, pr=pr: e.tensor_tensor(W5[:, pr, :], W5[:, pr, :], KA[:, pr, :], ALU.mult),
             reads=[Sh[5], Sh[1]], writes=[Sh[5]])
    wb, wv = self.w_in_unit(l, COLS["ag"], 256)
    for pr in range(2):
        ps = self.pb[pr]
        self.proj_fm(wb, wv, pr * 128, 128, ps, N, ps)
        P.op("act", lambda e, pr=pr, ps=ps: e.activation(SG[:, pr, :], ps[:, :N], AF.Silu), reads=[ps], writes=[Sh[7]])
    wbi, wvi = self.w_in_unit(l, COLS["ai"], 256)
    ident = self.C("ident")
    for ti, (c0, nt, seq, pos) in enumerate(blk.tiles):
        nch = max(nt // csz, 1)
        slot = 6 + ti % 2
        kdT = aT[:, slot, 0:256]
        Vb = aT[:, slot, 256:512]
        Vblk = aT[:, 2:6, :]
        if blk.kind == "P":
            if pos == 0:
                P.op("dve", lambda e: e.memset(SA[:, :, 0, :], 0.0), writes=[SA])
            elif ti == 0:
                P.op("dve", lambda e: e.tensor_copy(SA[:, :, 0, :], SAp[:, :, :]), reads=[SAp], writes=[SA])
        else:
            P.dma("sp", SA[:, :, 0, :], d["sa"].ap()[l, seq], writes=[SA])
        pv = self.pb[2]
        for k in range(8):
            P.op("pe", lambda e, k=k, c0=c0, nt=nt: e.matmul(pv[:nt, 0:256], self.hnT[:, k, c0:c0 + nt], wvi[:, k, :],
                                                            start=(k == 0), stop=(k == 7)),
                 reads=[wbi, ("hnT", k)], writes=[pv], sig=(k == 7))
        P.op("act", lambda e, nt=nt, Vb=Vb: e.activation(Vb[:nt, :], pv[:nt, 0:256], AF.Copy), reads=[pv], writes=[ah(slot)])
        if nch > 1:
            for h in range(4):
                P.op("dve", lambda e, h=h, nt=nt: e.tensor_tensor(
                    Vblk[:nt, h, :].rearrange("p (n v) -> p n v", n=8),
                    pv[:nt, h * 64:(h + 1) * 64].rearrange("p (o v) -> p o v", o=1).to_broadcast([nt, 8, 64]),
                    self.C("blk16")[:nt, :].rearrange("p (n o) -> p n o", o=1).to_broadcast([nt, 8, 64]), ALU.mult),
                    reads=[pv, self.cst], writes=[ah(2 + h)])
        pk = self.pb[3]
        for pr in range(2):
            P.op("pe", lambda e, pr=pr, c0=c0, nt=nt: e.transpose(pk[:nt, pr * 128:(pr + 1) * 128], W5[:, pr, c0:c0 + nt], ident),
                 reads=[Sh[5], self.cst], writes=[pk])
        P.op("act", lambda e, nt=nt, kdT=kdT: e.activation(kdT[:nt, :], pk[:nt, 0:256], AF.Copy), reads=[pk], writes=[ah(slot)])
        ST = self.pb[5]
        for h in range(4):
            pr = h // 2
            P.op("pe", lambda e, h=h, pr=pr, c0=c0, nt=nt: e.matmul(ST[:nt, h * 128:h * 128 + nt], kt[:, pr, c0:c0 + nt],
                                                                   qT[:, h, c0:c0 + nt], start=True, stop=True),
                 reads=[ah(pr), qT], writes=[ST], sig=(h == 3))
        PTa = self.PT[ti % 2]
        P.op("dve", lambda e, nt=nt, PTa=PTa: e.tensor_tensor(
            PTa[:nt, :].rearrange("p (h t) -> p h t", h=4)[:, :, :nt],
            ST[:nt, :].rearrange("p (h t) -> p h t", h=4)[:, :, :nt],
            self.C("mask16T")[:nt, :nt].rearrange("p (o t) -> p o t", o=1).to_broadcast([nt, 4, nt]), ALU.mult),
            reads=[ST, self.cst], writes=[PTa])
        Ups = (self.pb[0], self.pb[1])
        for h in range(4):
            r0, pr = (h % 2) * 64, h // 2
            rhs = Vblk[:nt, h, :] if nch > 1 else Vb[:nt, h * 64:(h + 1) * 64]
            rh = ah(2 + h) if nch > 1 else ah(slot)
            P.op("pe", lambda e, h=h, r0=r0, pr=pr, nt=nt, rhs=rhs, nch=nch: e.matmul(
                Ups[pr][r0:r0 + 64, 0:nch * 64], kdT[:nt, h * 64:(h + 1) * 64], rhs, start=True, stop=True),
                reads=[ah(slot), rh], writes=[Ups[pr]], sig=(h % 2 == 1))
        ob = self.pb[6]
        P.op("dve", lambda e: e.memset(ob[:, 0:256], 0.0), writes=[ob])
        for h in range(4):
            r0, pr = (h % 2) * 64, h // 2
            P.op("pe", lambda e, h=h, r0=r0, pr=pr, nt=nt, PTa=PTa, Vb=Vb: e.matmul(
                ob[r0:r0 + 64, pr * 128:pr * 128 + nt], Vb[:nt, h * 64:(h + 1) * 64], PTa[:nt, h * 128:h * 128 + nt],
                start=False, stop=False, skip_group_check=True), reads=[ah(slot), PTa], writes=[ob], sig=False)
        for n in range(nch):
            cend = c0 + n * csz + csz - 1
            for pr in range(2):
                P.op("dve", lambda e, n=n, pr=pr, cend=cend: e.scalar_tensor_tensor(
                    SA[:, pr, n + 1, :], SA[:, pr, n, :], EG[:, pr, cend:cend + 1], Ups[pr][:, n * 64:(n + 1) * 64],
                    ALU.mult, ALU.add), reads=[SA, Sh[4], Ups[pr]], writes=[SA])
        for h in range(4):
            r0, pr = (h % 2) * 64, h // 2
            for n in range(nch):
                cs = c0 + n * csz
                P.op("pe", lambda e, h=h, r0=r0, pr=pr, n=n, cs=cs: e.matmul(
                    ob[r0:r0 + 64, pr * 128 + n * csz:pr * 128 + (n + 1) * csz], SA[r0:r0 + 64, pr, n, :],
                    Q[r0:r0 + 64, pr, cs:cs + csz], start=False, stop=True, skip_group_check=True),
                    reads=[SA, Sh[0]], writes=[ob], sig=(h == 3 and n == nch - 1))
        for pr in range(2):
            P.op("act", lambda e, pr=pr, c0=c0, nt=nt: e.activation(OA[:, pr, c0:c0 + nt], ob[:, pr * 128:pr * 128 + nt], AF.Copy),
                 reads=[ob], writes=[Sh[6]])
        last_of_seq = (blk.kind == "S") or (blk.last and ti == len(blk.tiles) - 1)
        if last_of_seq:
            dst = self.od["pa"].ap()[l] if blk.kind == "P" else self.od["sao"].ap()[l, seq]
            P.dma("act", dst, SA[:, :, nch, :], reads=[SA], key="sa_out", out=True)
        elif ti == len(blk.tiles) - 1:
            P.op("dve", lambda e, nch=nch: e.tensor_copy(SAp[:, :, :], SA[:, :, nch, :]), reads=[SA], writes=[SAp])
        else:
            P.op("dve", lambda e, nch=nch: e.tensor_copy(SA[:, :, 0, :], SA[:, :, nch, :]), reads=[SA], writes=[SA])
    self.head_norm_gate(OA, Sh[6], SG, Sh[7], self.V("ang%d" % l), 0, N)


def _head_norm_gate(self, O, Oh, SG, SGh, gv, k0, N, gcol=None):
    P = self.P
    sq, tmp = self.S[8], self.S[9]
    for pr in range(2):
        ps = self.pb[4]
        P.op("act", lambda e, pr=pr: e.activation(sq[:, :N], O[:, pr, :], AF.Square), reads=[Oh], writes=[sq])
        P.op("pe", lambda e: e.matmul(ps[:, :N], self.C("onesbd"), sq[:, :N], start=True, stop=True),
             reads=[sq, self.cst], writes=[ps])
        rt = self.rt
        P.op("act", lambda e: e.activation(rt[:, :N], ps[:, :N], AF.Ln, bias=EPS, scale=1.0 / 64), reads=[ps], writes=[rt])
        P.op("act", lambda e: e.activation(rt[:, :N], rt[:, :N], AF.Exp, scale=-0.5), reads=[rt], writes=[rt])
        g = gv[:, pr:pr + 1] if gcol is None else gv[:, 0:1]
        P.op("dve", lambda e, pr=pr, g=g: e.scalar_tensor_tensor(tmp[:, :N], O[:, pr, :], g, rt[:, :N], ALU.mult, ALU.mult),
             reads=[Oh, rt, self.vecs], writes=[tmp])
        P.op("dve", lambda e, pr=pr: e.tensor_tensor(self.mixT[:, k0 + pr, :N], tmp[:, :N], SG[:, pr, :], ALU.mult),
             reads=[tmp, SGh], writes=[("mixT", k0 + pr)])


Builder.mixer_A = _mixer_A
Builder.head_norm_gate = _head_norm_gate


def _mixer_C(self, blk, l):
    P = self.P
    N, nseq, L = blk.N, blk.nseq, blk.L
    d = self.d
    cs = 64 if blk.kind == "P" else LS
    nlev = 5 if blk.kind == "P" else 2
    Sh = self.S
    XC = [self.Sv(g, N) for g in range(3)]
    SGz, OC = self.Sv(3, N), self.Sv(4, N)
    W = 3 + L
    ext = self.dext
    extv = ext[:, :, :nseq * W].rearrange("p c (s j) -> p c s j", s=nseq) if nseq > 1 else None
    def ev(ch, j0, j1):
        return ext[:, ch, j0:j1] if nseq == 1 else extv[:, ch, :, j0:j1]
    def v3(ap2):
        return ap2 if nseq == 1 else ap2.rearrange("p (s j) -> p s j", s=nseq)
    cw = self.V("ccw%d" % l)
    ident = self.C("ident")
    for g, cname in enumerate(("cq", "ck", "cv")):
        wb, wv = self.w_in_unit(l, COLS[cname], 256)
        X = XC[g]
        for ch in range(2):
            gc = 2 * g + ch
            if blk.kind == "P":
                if blk.first:
                    P.op("dve", lambda e, ch=ch: e.memset(ext[:, ch, 0:3], 0.0), writes=[ext])
                else:
                    P.op("dve", lambda e, ch=ch, gc=gc: e.tensor_copy(ext[:, ch, 0:3], self.ctail[l][:, gc, :]),
                         reads=[self.ctail[l]], writes=[ext])
            else:
                P.dma("sp", extv[:, ch, :, 0:3], d["scc"].ap()[l][:, gc], writes=[ext], key="scc_in")
            ps = self.pb[ch]
            self.proj_fm(wb, wv, ch * 128, 128, ps, N, ps)
            P.op("act", lambda e, ch=ch, ps=ps: e.activation(ev(ch, 3, 3 + L), v3(ps[:, :N]), AF.Copy),
                 reads=[ps], writes=[ext])
            P.op("dve", lambda e, ch=ch, gc=gc: e.tensor_scalar(v3(X[:, ch, :]), ev(ch, 0, L), cw[:, gc * 4:gc * 4 + 1], None, ALU.mult),
                 reads=[ext, self.vecs], writes=[Sh[g]])
            for j in range(1, 4):
                P.op("dve", lambda e, ch=ch, gc=gc, j=j: e.scalar_tensor_tensor(
                    v3(X[:, ch, :]), ev(ch, j, j + L), cw[:, gc * 4 + j:gc * 4 + j + 1], v3(X[:, ch, :]), ALU.mult, ALU.add),
                    reads=[ext, self.vecs, Sh[g]], writes=[Sh[g]])
            if blk.kind == "P":
                if blk.last:
                    P.dma("act", self.od["pcc"].ap()[l][:, gc, :], ext[:, ch, L:L + 3], reads=[ext], key="so5_%d_%s" % (l, str(locals().get("gc", "")) + str(locals().get("ch", ""))), out=True)
                else:
                    P.op("dve", lambda e, ch=ch, gc=gc: e.tensor_copy(self.ctail[l][:, gc, :], ext[:, ch, L:L + 3]),
                         reads=[ext], writes=[self.ctail[l]])
            else:
                P.dma("act", self.od["scco"].ap()[l][:, gc], extv[:, ch, :, L:L + 3], reads=[ext], key="so6_%d_%s" % (l, str(locals().get("gc", "")) + str(locals().get("ch", ""))), out=True)
            P.op("act", lambda e, ch=ch: e.activation(X[:, ch, :], X[:, ch, :], AF.Silu), reads=[Sh[g]], writes=[Sh[g]])
    for g, sc_ in ((0, 0.125), (1, 1.0)):
        X = XC[g]
        for pr in range(2):
            sq, ps, rt = Sh[8], self.pb[4], self.rt
            P.op("act", lambda e, pr=pr, X=X: e.activation(sq[:, :N], X[:, pr, :], AF.Square), reads=[Sh[g]], writes=[sq])
            P.op("pe", lambda e: e.matmul(ps[:, :N], self.C("onesbd"), sq[:, :N], start=True, stop=True),
                 reads=[sq, self.cst], writes=[ps])
            P.op("act", lambda e: e.activation(rt[:, :N], ps[:, :N], AF.Ln, bias=EPS), reads=[ps], writes=[rt])
            P.op("act", lambda e: e.activation(rt[:, :N], rt[:, :N], AF.Exp, scale=-0.5), reads=[rt], writes=[rt])
            P.op("dve", lambda e, pr=pr, X=X, sc_=sc_: e.scalar_tensor_tensor(X[:, pr, :], X[:, pr, :], sc_, rt[:, :N], ALU.mult, ALU.mult),
                 reads=[Sh[g], rt], writes=[Sh[g]])
    if blk.idx == 0 and l == 0 and blk.kind == "P":
        for g, nm in enumerate(("C_q", "C_k", "C_v")):
            self.dump(nm, XC[g], [128, 2, N], [Sh[g]])
    wb, wv = self.w_in_unit(l, COLS["cz"], 256)
    for pr in range(2):
        ps = self.pb[pr]
        self.proj_fm(wb, wv, pr * 128, 128, ps, N, ps)
        P.op("act", lambda e, pr=pr, ps=ps: e.activation(SGz[:, pr, :], ps[:, :N], AF.Silu), reads=[ps], writes=[Sh[3]])
    wb8, wv8 = self.w_in_unit(l, COLS["cb"], 8)
    SCw, SCp, SCm = self.SCw, self.SCp[l], self.SCm
    kmt, kbgm, kdm = self.kmt, self.kbgm, self.kdm
    for t_ in (SCm, kmt, kbgm, kdm):
        P.op("dve", lambda e, t_=t_: e.memset(t_[:], 0.0), writes=[t_])
    def mask_state():
        for hh in range(2):
            P.op("act", lambda e, hh=hh: e.activation(SCm[hh * 64:(hh + 1) * 64, hh::2, :], SCw[hh * 64:(hh + 1) * 64, :, :], AF.Copy),
                 reads=[SCw], writes=[SCm])
    scb = Sh[9]
    def Sq(i, q):
        return Sh[i][:, q * 256:(q + 1) * 256], ("Sq", i, q)
    (kTM, hkTM), (vTM, hvTM), (bv, hbv), (kbg, hkbg) = (Sq(5, q) for q in range(4))
    (eD, heD), (PA, hPA), (eDT, heDT), (RA, hRA) = (Sq(6, q) for q in range(4))
    (Nm, hNm), (PB, hPB), (RB, hRB), (kd, hkd) = (Sq(7, q) for q in range(4))
    (un, hun), (o1s, ho1s), (oTM, hoTM), (wT, hwT) = (Sq(8, q) for q in range(4))
    def sc_t(name, ti):
        o = dict(bet=0, la=4, g=8, eg=12, nbet=16, begp=20, glb=24, kds=28, egl=32, t1=36)[name] + 40 * ti
        return scb[:, o:o + 4], ("sc", name, ti)
    pb = self.pb
    def tile_scalars(ti):
        return tuple(sc_t(n, ti) for n in ("bet", "la", "g", "eg", "nbet", "begp", "glb", "kds", "egl", "t1"))
    for ti, (c0, nt, seq, pos) in enumerate(blk.tiles):
        nc2 = max(nt // 64, 1)
        g0 = 32 * ti
        sc = lambda n: sc_t(n, ti)
        for k in range(8):
            P.op("pe", lambda e, k=k: e.matmul(pb[1][:nt, g0:g0 + 8], self.hnT[:, k, c0:c0 + nt], wv8[:, k, :], start=(k == 0), stop=(k == 7)),
                 reads=[wb8, ("hnT", k)], writes=[pb[1]], sig=(k == 7))
        (bet, hbet), (la, hla), (gg, hgg), (eg, heg), (nbet, hnbet), (begp, hbegp), (glb, hglb), (kds, hkds), (egl, hegl), (t1, ht1) = (
            sc(n) for n in ("bet", "la", "g", "eg", "nbet", "begp", "glb", "kds", "egl", "t1"))
        P.op("act", lambda e: e.activation(bet[:nt, :], pb[1][:nt, g0:g0 + 4], AF.Sigmoid), reads=[pb[1]], writes=[hbet])
        P.op("dve", lambda e: e.tensor_tensor(t1[:nt, :], pb[1][:nt, g0 + 4:g0 + 8], self.V("cdtb%d" % l)[:nt, :], ALU.add),
             reads=[pb[1], self.vecs], writes=[ht1])
        P.op("act", lambda e: e.activation(t1[:nt, :], t1[:nt, :], AF.Exp), reads=[ht1], writes=[ht1])
        P.op("act", lambda e: e.activation(t1[:nt, :], t1[:nt, :], AF.Ln, bias=1.0), reads=[ht1], writes=[ht1])
        P.op("dve", lambda e: e.tensor_tensor(la[:nt, :], t1[:nt, :], self.negA[:nt, 4 * l:4 * l + 4], ALU.mult),
             reads=[ht1, self.negA], writes=[hla])
        P.op("pe", lambda e: e.matmul(pb[1][:nt, g0 + 8:g0 + 12], self.C("u64")[:nt, :nt], la[:nt, :], start=True, stop=True),
             reads=[hla, self.cst], writes=[pb[1]])
        P.op("act", lambda e: e.activation(gg[:nt, :], pb[1][:nt, g0 + 8:g0 + 12], AF.Copy), reads=[pb[1]], writes=[hgg])
        P.op("act", lambda e: e.activation(eg[:nt, :], gg[:nt, :], AF.Exp), reads=[hgg], writes=[heg])
        sel = self.C("sel64")[:nt, :nt] if blk.kind == "P" else self.C("sel8")[:nt, :nt]
        P.op("pe", lambda e: e.matmul(pb[1][:nt, g0 + 12:g0 + 16], sel, gg[:nt, :], start=True, stop=True),
             reads=[hgg, self.cst], writes=[pb[1]])
        P.op("dve", lambda e: e.tensor_tensor(kds[:nt, :], pb[1][:nt, g0 + 12:g0 + 16], gg[:nt, :], ALU.subtract),
             reads=[pb[1], hgg], writes=[hkds])
        P.op("act", lambda e: e.activation(kds[:nt, :], kds[:nt, :], AF.Exp), reads=[hkds], writes=[hkds])
        P.op("dve", lambda e: e.tensor_scalar(nbet[:nt, :], bet[:nt, :], -1.0, None, ALU.mult), reads=[hbet], writes=[hnbet])
        P.op("dve", lambda e: e.tensor_tensor(begp[:nt, :], bet[:nt, :], eg[:nt, :], ALU.mult), reads=[hbet, heg], writes=[hbegp])
        selc = self.C("selc")[:nt, 0:nc2] if blk.kind == "P" else self.C("sel8")[:nt, 0:1]
        for pr in range(2):
            rep = scb[:, 768 + pr * 128:896 + pr * 128]
            P.op("dve", lambda e, pr=pr, rep=rep: e.tensor_copy(
                rep[:nt, :].rearrange("p (a b) -> p a b", a=2),
                eg[:nt, 2 * pr:2 * pr + 2].rearrange("p (a o) -> p a o", o=1).to_broadcast([nt, 2, 64])),
                reads=[heg], writes=[("sc", "rep", pr)])
            P.op("pe", lambda e, pr=pr, rep=rep: e.matmul(pb[1][:, g0 + 16 + 2 * pr:g0 + 16 + 2 * pr + nc2], rep[:nt, :], selc, start=True, stop=True),
                 reads=[("sc", "rep", pr), self.cst], writes=[pb[1]])
        P.op("act", lambda e: e.activation(egl[:, :], pb[1][:, g0 + 16:g0 + 20], AF.Copy), reads=[pb[1]], writes=[hegl])
        if self.o.get("Cstop") == 2:
            break
    for ti, (c0, nt, seq, pos) in enumerate(blk.tiles):
        nc2 = max(nt // 64, 1)
        (bet, hbet), (la, hla), (gg, hgg), (eg, heg), (nbet, hnbet), (begp, hbegp), (glb, hglb), (kds, hkds), (egl, hegl), (t1, ht1) = tile_scalars(ti)
        def hv(ap):
            return ap[:nt, 0:256].rearrange("p (h s) -> p h s", h=4)[:, :, :cs]
        def bc(ap2):
            return ap2.rearrange("p (o s) -> p o s", o=1).to_broadcast([nt, 4, cs])
        CH = [(c, h) for c in range(nc2) for h in range(4)]
        if self.o.get("Cdiag"):
            CH = [(c, h) for (c, h) in CH if h % 2 == c]
        if blk.kind == "P":
            if pos == 0:
                P.op("dve", lambda e: e.memset(SCw[:, :, :], 0.0), writes=[SCw])
            elif ti == 0:
                P.op("dve", lambda e: e.tensor_copy(SCw[:, :, :], SCp[:, :, :]), reads=[SCp], writes=[SCw])
        else:
            P.dma("sp", SCw[:, :, :], d["sc"].ap()[l, seq], writes=[SCw])
        if self.o.get("Cstop") == 1:
            break
        mask_state()
        for hh in range(2):
            P.op("act", lambda e, hh=hh: e.activation(kmt[hh * 64:(hh + 1) * 64, hh::2, :nt], XC[1][hh * 64:(hh + 1) * 64, :, c0:c0 + nt], AF.Copy),
                 reads=[Sh[1]], writes=[kmt])
        for g in (1, 2):
            for pr in range(2):
                o = (g - 1) * 256 + pr * 128
                P.op("pe", lambda e, g=g, pr=pr, o=o: e.transpose(pb[0][:nt, o:o + 128], XC[g][:, pr, c0:c0 + nt], ident),
                     reads=[Sh[g], self.cst], writes=[pb[0]])
        P.op("act", lambda e: e.activation(Sh[5][:nt, 0:512], pb[0][:nt, 0:512], AF.Copy), reads=[pb[0]], writes=[hkTM, hvTM])
        if ti == 0 and blk.idx == 0 and l == 0 and blk.kind == "P":
            self.dump("C_sc", scb[:, 0:64], [128, 64], [hbet, hla, hgg, heg, hnbet, hbegp, hkds, hegl])
        for h in range(4):
            bufU = scb[:, 256 + (h % 2) * 128:384 + (h % 2) * 128]
            bufL = scb[:, 512 + (h % 2) * 128:640 + (h % 2) * 128]
            P.op("dve", lambda e, h=h, bufU=bufU: e.tensor_scalar(bufU[:nt, :nt], self.C("u64")[:nt, :nt], la[:nt, h:h + 1], None, ALU.mult),
                 reads=[hla, self.cst], writes=[("sc", "U", h % 2)])
            P.op("pe", lambda e, h=h, bufU=bufU: e.matmul(pb[2][:nt, h * 64:h * 64 + cs], bufU[:nt, :nt], self.C("lsloc")[:nt, :cs],
                                                          start=True, stop=True), reads=[("sc", "U", h % 2), self.cst], writes=[pb[2]])
            P.op("dve", lambda e, h=h, bufL=bufL: e.tensor_scalar(bufL[:nt, :nt], self.C("l64s")[:nt, :nt], la[:nt, h:h + 1], None, ALU.mult),
                 reads=[hla, self.cst], writes=[("sc", "L", h % 2)])
            P.op("pe", lambda e, h=h, bufL=bufL: e.matmul(pb[3][:nt, h * 64:h * 64 + cs], bufL[:nt, :nt], self.C("uloc")[:nt, :cs],
                                                          start=True, stop=True), reads=[("sc", "L", h % 2), self.cst], writes=[pb[3]])
        P.op("act", lambda e: e.activation(hv(eD), hv(pb[2]), AF.Exp), reads=[pb[2]], writes=[heD])
        P.op("act", lambda e: e.activation(hv(eDT), hv(pb[3]), AF.Exp), reads=[pb[3]], writes=[heDT])
        for c, h in CH:
            pc, r0, pr = 64 * c, 64 * (h % 2), h // 2
            cc = c0 + 64 * c
            P.op("pe", lambda e, pc=pc, r0=r0, pr=pr, cc=cc, h=h: e.matmul(
                pb[5][pc:pc + cs, h * 64:h * 64 + cs], kmt[:, h, pc:pc + cs], XC[1][:, pr, cc:cc + cs],
                start=True, stop=True), reads=[Sh[1], kmt], writes=[pb[5]], sig=(c == nc2 - 1 and h == 3))
        for c, h in CH:
            pc, r0, pr = 64 * c, 64 * (h % 2), h // 2
            cc = c0 + 64 * c
            P.op("pe", lambda e, pc=pc, r0=r0, pr=pr, cc=cc, h=h: e.matmul(
                pb[6][pc:pc + cs, h * 64:h * 64 + cs], kmt[:, h, pc:pc + cs], XC[0][:, pr, cc:cc + cs],
                start=True, stop=True), reads=[kmt, Sh[0]], writes=[pb[6]], sig=(c == nc2 - 1 and h == 3))
        P.op("dve", lambda e: e.tensor_tensor(hv(eD), hv(pb[5]), hv(eD), ALU.mult), reads=[pb[5], heD], writes=[heD])
        for h in range(4):
            P.op("dve", lambda e, h=h: e.scalar_tensor_tensor(PA[:nt, h * 64:h * 64 + cs], eD[:nt, h * 64:h * 64 + cs], nbet[:nt, h:h + 1],
                                                             self.C("trilS")[:nt, :cs], ALU.mult, ALU.mult),
                 reads=[heD, hnbet, self.cst], writes=[hPA])
        P.op("dve", lambda e: e.tensor_tensor(hv(eDT), hv(pb[6]), hv(eDT), ALU.mult), reads=[pb[6], heDT], writes=[heDT])
        P.op("dve", lambda e: e.tensor_tensor(hv(eDT), hv(eDT), bc(self.C("triuI")[:nt, :cs]), ALU.mult),
             reads=[heDT, self.cst], writes=[heDT])
        if self.o.get("Cstop") == 3:
            break
        for c, h in CH:
            pc = 64 * c
            P.op("pe", lambda e, pc=pc, h=h: e.matmul(pb[7][pc:pc + cs, h * 64:h * 64 + cs], PA[pc:pc + cs, h * 64:h * 64 + cs],
                                                     ident[pc:pc + cs, pc:pc + cs], start=True, stop=True),
                 reads=[hPA, self.cst], writes=[pb[7]], sig=(c == nc2 - 1 and h == 3))
        P.op("act", lambda e: e.activation(hv(RA), hv(pb[7]), AF.Copy), reads=[pb[7]], writes=[hRA])
        P.op("dve", lambda e: e.tensor_tensor(hv(Nm), hv(RA), bc(self.C("id64")[:nt, :cs]), ALU.add), reads=[hRA, self.cst], writes=[hNm])
        if self.o.get("Cstop") == 4:
            break
        (Pc, hPc), (Rc, hRc), (Pn, hPn), (Rn, hRn) = (PA, hPA), (RA, hRA), (PB, hPB), (RB, hRB)
        for lev in range(1, nlev + 1):
            lastl = lev == nlev
            for c, h in CH:
                pc = 64 * c
                sl = (slice(pc, pc + cs), slice(h * 64, h * 64 + cs))
                P.op("pe", lambda e, sl=sl, Rc=Rc, Pc=Pc: e.matmul(pb[2][sl], Rc[sl], Pc[sl], start=True, stop=True),
                     reads=[hRc, hPc], writes=[pb[2]], sig=(c == nc2 - 1 and h == 3))
            if not lastl:
                for c, h in CH:
                    pc = 64 * c
                    sl = (slice(pc, pc + cs), slice(h * 64, h * 64 + cs))
                    P.op("pe", lambda e, sl=sl, Rc=Rc, Pc=Pc: e.matmul(pb[3][sl], Pc[sl], Rc[sl], start=True, stop=True),
                         reads=[hRc, hPc], writes=[pb[3]], sig=(c == nc2 - 1 and h == 3))
            P.op("act", lambda e, Pn=Pn: e.activation(hv(Pn), hv(pb[2]), AF.Copy), reads=[pb[2]], writes=[hPn])
            if not lastl:
                P.op("act", lambda e, Rn=Rn: e.activation(hv(Rn), hv(pb[3]), AF.Copy), reads=[pb[3]], writes=[hRn])
            for c, h in CH:
                pc = 64 * c
                sl = (slice(pc, pc + cs), slice(h * 64, h * 64 + cs))
                P.op("pe", lambda e, sl=sl, Pn=Pn: e.matmul(pb[5][sl], Pn[sl], Nm[sl], start=True, stop=True),
                     reads=[hPn, hNm], writes=[pb[5]], sig=(c == nc2 - 1 and h == 3))
            P.op("dve", lambda e: e.tensor_tensor(hv(Nm), hv(Nm), hv(pb[5]), ALU.add), reads=[hNm, pb[5]], writes=[hNm])
            (Pc, hPc), (Rc, hRc), (Pn, hPn), (Rn, hRn) = (Pn, hPn), (Rn, hRn), (Pc, hPc), (Rc, hRc)
        if self.o.get("Cstop") == 5:
            break
        if ti == 0 and blk.idx == 0 and l == 0 and blk.kind == "P":
            self.dump("C_N", Nm, [128, 256], [hNm])
            self.dump("C_qk", eDT, [128, 256], [heDT])
        for h in range(4):
            hs = slice(h * 64, (h + 1) * 64)
            P.op("dve", lambda e, h=h, hs=hs: e.tensor_scalar(bv[:nt, hs], vTM[:nt, hs], bet[:nt, h:h + 1], None, ALU.mult),
                 reads=[hvTM, hbet], writes=[hbv])
            P.op("dve", lambda e, h=h, hs=hs: e.tensor_scalar(kbg[:nt, hs], kTM[:nt, hs], begp[:nt, h:h + 1], None, ALU.mult),
                 reads=[hkTM, hbegp], writes=[hkbg])
            P.op("dve", lambda e, h=h, hs=hs: e.tensor_scalar(kd[:nt, hs], kTM[:nt, hs], kds[:nt, h:h + 1], None, ALU.mult),
                 reads=[hkTM, hkds], writes=[hkd])
        for c in range(nc2):
            pc = 64 * c
            P.op("act", lambda e, c=c, pc=pc: e.activation(kbgm[pc:pc + cs, c, :], kbg[pc:pc + cs, :], AF.Copy), reads=[hkbg], writes=[kbgm])
            P.op("act", lambda e, c=c, pc=pc: e.activation(kdm[pc:pc + cs, c, :], kd[pc:pc + cs, :], AF.Copy), reads=[hkd], writes=[kdm])
        for c, h in CH:
            pc = 64 * c
            P.op("pe", lambda e, pc=pc, h=h: e.matmul(pb[6][pc:pc + cs, h * 64:(h + 1) * 64], Nm[pc:pc + cs, h * 64:h * 64 + cs],
                                                      bv[pc:pc + cs, h * 64:(h + 1) * 64], start=True, stop=True),
                 reads=[hNm, hbv], writes=[pb[6]], sig=(c == nc2 - 1 and h == 3))
        P.op("act", lambda e: e.activation(un[:nt, :], pb[6][:nt, 0:256], AF.Copy), reads=[pb[6]], writes=[hun])
        for c, h in CH:
            pc, r0, pr = 64 * c, 64 * (h % 2), h // 2
            P.op("pe", lambda e, pc=pc, r0=r0, pr=pr, h=h, c=c: e.matmul(
                pb[7][r0:r0 + 64, pr * 128 + 64 * c:pr * 128 + 64 * c + cs], kbgm[:, c, h * 64:(h + 1) * 64],
                Nm[:, h * 64:h * 64 + cs], start=True, stop=True),
                reads=[hNm, kbgm], writes=[pb[7]], sig=(c == nc2 - 1 and h == 3))
        P.op("act", lambda e: e.activation(wT[:, :], pb[7][:, 0:256], AF.Copy), reads=[pb[7]], writes=[hwT])
        if self.o.get("Cstop") == 6:
            break
        for c in range(nc2):
            pc = 64 * c
            rows = slice(pc, pc + cs)
            cc = c0 + 64 * c
            for h in range(4):
                r0, pr = 64 * (h % 2), h // 2
                P.op("pe", lambda e, h=h, r0=r0, pr=pr: e.matmul(
                    pb[0][rows, h * 64:(h + 1) * 64], wT[:, pr * 128 + pc:pr * 128 + pc + cs], SCm[:, h, :],
                    start=True, stop=True), reads=[hwT, SCm], writes=[pb[0]], sig=(h == 3))
            P.op("dve", lambda e: e.tensor_tensor(un[rows, :], un[rows, :], pb[0][rows, 0:256], ALU.subtract),
                 reads=[hun, pb[0]], writes=[hun])
            for h in range(4):
                r0, pr = 64 * (h % 2), h // 2
                P.op("pe", lambda e, h=h, r0=r0, pr=pr: e.matmul(
                    pb[2][rows, h * 64:(h + 1) * 64], XC[0][:, pr, cc:cc + cs], SCm[:, h, :],
                    start=True, stop=True), reads=[Sh[0], SCm], writes=[pb[2]], sig=(h == 3))
            for h in range(4):
                P.op("act", lambda e, h=h: e.activation(o1s[rows, h * 64:(h + 1) * 64], pb[2][rows, h * 64:(h + 1) * 64], AF.Copy,
                                                        scale=eg[rows, h:h + 1]), reads=[pb[2], heg], writes=[ho1s])
            for h in range(4):
                P.op("pe", lambda e, h=h: e.matmul(pb[3][rows, h * 64:(h + 1) * 64], eDT[rows, h * 64:h * 64 + cs],
                                                   un[rows, h * 64:(h + 1) * 64], start=True, stop=True),
                     reads=[heDT, hun], writes=[pb[3]], sig=(h == 3))
            P.op("dve", lambda e: e.tensor_tensor(oTM[rows, :], o1s[rows, :], pb[3][rows, 0:256], ALU.add),
                 reads=[ho1s, pb[3]], writes=[hoTM])
            for h in range(4):
                r0, pr = 64 * (h % 2), h // 2
                P.op("pe", lambda e, h=h, r0=r0, pr=pr: e.matmul(
                    pb[5][r0:r0 + 64, pr * 64:(pr + 1) * 64], kdm[:, c, h * 64:(h + 1) * 64], un[:, h * 64:(h + 1) * 64],
                    start=True, stop=True), reads=[kdm, hun], writes=[pb[5]], sig=(h == 3))
            for pr in range(2):
                P.op("dve", lambda e, pr=pr, c=c: e.scalar_tensor_tensor(
                    SCw[:, pr, :], SCw[:, pr, :], egl[:, 2 * pr + c:2 * pr + c + 1], pb[5][:, pr * 64:(pr + 1) * 64], ALU.mult, ALU.add),
                    reads=[SCw, hegl, pb[5]], writes=[SCw])
            if c < nc2 - 1:
                mask_state()
        if self.o.get("Cstop") == 7:
            break
        for pr in range(2):
            P.op("pe", lambda e, pr=pr: e.transpose(pb[6][:, pr * 128:pr * 128 + nt], oTM[:nt, pr * 128:(pr + 1) * 128], ident[:nt, :nt]),
                 reads=[hoTM, self.cst], writes=[pb[6]])
            P.op("act", lambda e, pr=pr: e.activation(OC[:, pr, c0:c0 + nt], pb[6][:, pr * 128:pr * 128 + nt], AF.Copy),
                 reads=[pb[6]], writes=[Sh[4]])
        last_of_seq = (blk.kind == "S") or (blk.last and ti == len(blk.tiles) - 1)
        if last_of_seq:
            dst = self.od["pc"].ap()[l] if blk.kind == "P" else self.od["sco"].ap()[l, seq]
            P.dma("act", dst, SCw[:, :, :], reads=[SCw], key="sc_out", out=True)
        elif ti == len(blk.tiles) - 1:
            P.op("dve", lambda e: e.tensor_copy(SCp[:, :, :], SCw[:, :, :]), reads=[SCw], writes=[SCp])
    if blk.idx == 0 and l == 0 and blk.kind == "P":
        self.dump("C_o", OC, [128, 2, N], [Sh[4]])
    self.head_norm_gate(OC, Sh[4], SGz, Sh[3], self.V("cng%d" % l), 4, N, gcol=True)


Builder.mixer_C = _mixer_C


_OPTS = dict(mixers="ABCD")


def kernel(**inputs):
    n_cores = 8
    nc = bass.Bass("TRN2", target_bir_lowering=False)
    Builder(nc, dict(_OPTS)).run()
    sh = host_shared(inputs)
    in_maps = []
    for c in range(n_cores):
        m = host_core(inputs, c)
        m.update(sh)
        in_maps.append(m)
    res = run_bass_kernel_spmd(nc, in_maps, core_ids=list(range(n_cores)))
    return host_gather(res.results)
```
